# Optimizing a Trainium2 kernel written in Bass

```python
import math
import jax, jax.numpy as jnp
from jax import lax
import numpy as np

D_MODEL = 1024
BATCH = 8
SEQ = 4096
DEPTH = 2

N_META = 16
MIX_WIDTH = D_MODEL
ATTN_WIDTH = MIX_WIDTH // 2
HGRN_WIDTH = MIX_WIDTH - ATTN_WIDTH
ATTN_HEADS = 8
ATTN_HEAD_DIM = ATTN_WIDTH // ATTN_HEADS
KV_RANK = 128
IDX_HEADS = 4
IDX_DIM = 64
TOPK_MAX = 256
Q_BLOCK = 128
HGRN_EXPAND = 128
HGRN_HEADS = HGRN_WIDTH // HGRN_EXPAND
HGRN_CHUNK = 64
REL_BUCKETS = 32
REL_MAX_DIST = 128
DN_ALPHA = (2 * DEPTH) ** 0.25
DN_BETA = (8 * DEPTH) ** -0.25
EPS = 1e-6

IN_SIZES = (ATTN_WIDTH, KV_RANK, IDX_HEADS * IDX_DIM, IDX_DIM, IDX_HEADS, ATTN_WIDTH,
            HGRN_WIDTH, HGRN_WIDTH, HGRN_WIDTH, HGRN_WIDTH)
N_IN = sum(IN_SIZES)
IN_OFFSETS = tuple(int(v) for v in np.cumsum(IN_SIZES)[:-1])

kernel_name = "hymba_dsa_hgrn2_deepnorm"


def layer_norm(x, g, b):
    xf = x.astype(jnp.float32)
    mu = jnp.mean(xf, axis=-1, keepdims=True)
    var = jnp.mean(jnp.square(xf - mu), axis=-1, keepdims=True)
    return ((xf - mu) * lax.rsqrt(var + EPS) * g + b).astype(x.dtype)


def rms_norm(x, g):
    xf = x.astype(jnp.float32)
    return (xf * lax.rsqrt(jnp.mean(jnp.square(xf), axis=-1, keepdims=True) + EPS) * g).astype(x.dtype)


def t5_bucket(dist):
    n = jnp.maximum(dist, 0)
    max_exact = REL_BUCKETS // 2
    nf = jnp.maximum(n, 1).astype(jnp.float32)
    large = max_exact + (jnp.log(nf / max_exact) / math.log(REL_MAX_DIST / max_exact)
                         * (REL_BUCKETS - max_exact)).astype(jnp.int32)
    large = jnp.minimum(large, REL_BUCKETS - 1)
    return jnp.where(n < max_exact, n, large)


def dsa_attention(q, c, qi, ki, wi, rel_bias, w_uk, w_uv):
    B, E = c.shape[0], c.shape[1]
    S = E - N_META
    k_top = min(TOPK_MAX, S // 4)
    scale = ATTN_HEAD_DIM ** -0.5
    q_lat = jnp.einsum('bthd,hdc->bthc', q, w_uk)
    c_m, c_r = c[:, :N_META], c[:, N_META:]
    ql_m, ql_r = q_lat[:, :N_META], q_lat[:, N_META:]
    qi_r, ki_r = qi[:, N_META:], ki[:, N_META:]
    wi_r = wi[:, N_META:] * (IDX_HEADS ** -0.5)
    meta_pos = jnp.arange(N_META)
    key_pos = jnp.arange(S)
    b_idx = jnp.arange(B)[:, None, None]

    lg = jnp.einsum('bqhc,bmc->bqhm', ql_m, c_m).astype(jnp.float32) * scale
    bm = rel_bias[t5_bucket(meta_pos[:, None] - meta_pos[None, :])]
    lg = lg + jnp.transpose(bm, (0, 2, 1))[None].astype(jnp.float32)
    lg = jnp.where((meta_pos[:, None] >= meta_pos[None, :])[None, :, None, :], lg, -jnp.inf)
    o_m = jnp.einsum('bqhm,bmc->bqhc', jax.nn.softmax(lg, axis=-1).astype(c.dtype), c_m)

    def block(bi):
        start = bi * Q_BLOCK
        q_pos = start + jnp.arange(Q_BLOCK)
        ql = lax.dynamic_slice_in_dim(ql_r, start, Q_BLOCK, axis=1)
        qib = lax.dynamic_slice_in_dim(qi_r, start, Q_BLOCK, axis=1)
        wib = lax.dynamic_slice_in_dim(wi_r, start, Q_BLOCK, axis=1)
        s_h = jax.nn.relu(jnp.einsum('bqhd,bkd->bqhk', qib, ki_r).astype(jnp.float32) * (IDX_DIM ** -0.5))
        s_idx = jnp.einsum('bqhk,bqh->bqk', s_h, wib.astype(jnp.float32))
        causal = key_pos[None, :] <= q_pos[:, None]
        s_idx = jnp.where(causal[None], s_idx, -jnp.inf)
        _, sel = lax.top_k(s_idx, k_top)
        valid = sel <= q_pos[None, :, None]
        c_sel = c_r[b_idx, sel]
        lg_s = jnp.einsum('bqhc,bqkc->bqhk', ql, c_sel).astype(jnp.float32) * scale
        bs = rel_bias[t5_bucket(q_pos[None, :, None] - sel)]
        lg_s = lg_s + jnp.moveaxis(bs, -1, 2).astype(jnp.float32)
        lg_s = jnp.where(valid[:, :, None, :], lg_s, -jnp.inf)
        lg_m = jnp.einsum('bqhc,bmc->bqhm', ql, c_m).astype(jnp.float32) * scale
        bmr = rel_bias[t5_bucket(N_META + q_pos[:, None] - meta_pos[None, :])]
        lg_m = lg_m + jnp.transpose(bmr, (0, 2, 1))[None].astype(jnp.float32)
        p = jax.nn.softmax(jnp.concatenate([lg_m, lg_s], axis=-1), axis=-1).astype(c.dtype)
        return (jnp.einsum('bqhm,bmc->bqhc', p[..., :N_META], c_m)
                + jnp.einsum('bqhk,bqkc->bqhc', p[..., N_META:], c_sel))

    o_r = lax.map(block, jnp.arange(S // Q_BLOCK))
    o_r = jnp.moveaxis(o_r, 0, 1).reshape(B, S, ATTN_HEADS, KV_RANK)
    o_lat = jnp.concatenate([o_m, o_r], axis=1)
    return jnp.einsum('bthc,hcd->bthd', o_lat, w_uv).reshape(B, E, ATTN_WIDTH)


def hgrn2(q, f_raw, i, lb):
    B, E = q.shape[0], q.shape[1]
    dt = q.dtype
    qf = jax.nn.silu(q.astype(jnp.float32)) * (HGRN_EXPAND ** -0.5)
    f = lb + (1.0 - lb) * jax.nn.sigmoid(f_raw.astype(jnp.float32))
    g = jnp.log(f)
    k = 1.0 - f
    v = i.astype(jnp.float32)
    pad = HGRN_CHUNK - N_META

    def to_chunks(a):
        a = jnp.pad(a, ((0, 0), (pad, 0), (0, 0)))
        n = a.shape[1] // HGRN_CHUNK
        return a.reshape(B, n, HGRN_CHUNK, HGRN_HEADS, HGRN_EXPAND).transpose(1, 0, 3, 2, 4)

    tri = jnp.tril(jnp.ones((HGRN_CHUNK, HGRN_CHUNK), dtype=bool))

    def step(state, inp):
        qc, kc, vc, gc = inp
        bcum = jnp.cumsum(gc, axis=2)
        o_inter = jnp.einsum('bhtk,bhkv->bhtv', qc * jnp.exp(bcum), state)
        diff = bcum[:, :, :, None, :] - bcum[:, :, None, :, :]
        decay = jnp.exp(jnp.where(tri[:, :, None], diff, -jnp.inf))
        scores = jnp.sum(qc[:, :, :, None, :] * decay * kc[:, :, None, :, :], axis=-1)
        o_intra = jnp.einsum('bhts,bhsv->bhtv', scores, vc)
        b_last = bcum[:, :, -1:, :]
        new_state = (jnp.exp(b_last[:, :, 0, :])[..., None] * state
                     + jnp.einsum('bhsk,bhsv->bhkv', kc * jnp.exp(b_last - bcum), vc))
        return new_state, o_inter + o_intra

    s0 = jnp.zeros((B, HGRN_HEADS, HGRN_EXPAND, HGRN_EXPAND), jnp.float32)
    _, outs = lax.scan(step, s0, (to_chunks(qf), to_chunks(k), to_chunks(v), to_chunks(g)))
    n = outs.shape[0]
    o = outs.transpose(1, 0, 3, 2, 4).reshape(B, n * HGRN_CHUNK, HGRN_HEADS, HGRN_EXPAND)[:, pad:]
    return o.astype(dt)


def setup_inputs(seed: int = 0) -> dict:
    key = jax.random.key(seed)
    ks = jax.random.split(key, 12)
    nrm = jax.random.normal
    x = nrm(ks[0], (BATCH, SEQ, D_MODEL), jnp.float32)
    meta_tokens = nrm(ks[1], (N_META, D_MODEL), jnp.float32)
    rel_bias = 0.5 * nrm(ks[2], (REL_BUCKETS, ATTN_HEADS), jnp.float32)
    hgrn_lb_raw = nrm(ks[3], (DEPTH, HGRN_WIDTH), jnp.float32)
    w_in = nrm(ks[4], (DEPTH, D_MODEL, N_IN), jnp.float32) * D_MODEL ** -0.5
    kv_norm_g = 1.0 + 0.1 * nrm(ks[5], (DEPTH, KV_RANK), jnp.float32)
    w_uk = nrm(ks[6], (DEPTH, ATTN_HEADS, ATTN_HEAD_DIM, KV_RANK), jnp.float32) * ATTN_HEAD_DIM ** -0.5
    w_uv = nrm(ks[7], (DEPTH, ATTN_HEADS, KV_RANK, ATTN_HEAD_DIM), jnp.float32) * (KV_RANK ** -0.5 * DN_BETA)
    hgrn_norm_g = 1.0 + 0.1 * nrm(ks[8], (DEPTH, HGRN_EXPAND), jnp.float32)
    w_out = nrm(ks[9], (DEPTH, MIX_WIDTH, D_MODEL), jnp.float32) * (MIX_WIDTH ** -0.5 * DN_BETA)
    ln_g = 1.0 + 0.1 * nrm(ks[10], (DEPTH, D_MODEL), jnp.float32)
    ln_b = 0.02 * nrm(ks[11], (DEPTH, D_MODEL), jnp.float32)
    return {"x": x, "meta_tokens": meta_tokens, "rel_bias": rel_bias, "hgrn_lb_raw": hgrn_lb_raw,
            "w_in": w_in, "kv_norm_g": kv_norm_g, "w_uk": w_uk, "w_uv": w_uv,
            "hgrn_norm_g": hgrn_norm_g, "w_out": w_out, "ln_g": ln_g, "ln_b": ln_b}


def reference(x, meta_tokens, rel_bias, hgrn_lb_raw, w_in, kv_norm_g, w_uk, w_uv,
              hgrn_norm_g, w_out, ln_g, ln_b):
    B = x.shape[0]
    h = jnp.concatenate([jnp.broadcast_to(meta_tokens[None].astype(x.dtype), (B, N_META, D_MODEL)), x], axis=1)
    E = h.shape[1]
    lb_p = jax.nn.softmax(hgrn_lb_raw.astype(jnp.float32), axis=0)
    lb_all = jnp.cumsum(lb_p, axis=0) - lb_p[0:1]
    for l in range(DEPTH):
        proj = h @ w_in[l]
        (q_a, c_kv, q_idx, k_idx, w_idx, gate_a,
         q_h, f_h, i_h, gate_h) = jnp.split(proj, IN_OFFSETS, axis=-1)
        a = dsa_attention(q_a.reshape(B, E, ATTN_HEADS, ATTN_HEAD_DIM), rms_norm(c_kv, kv_norm_g[l]),
                          q_idx.reshape(B, E, IDX_HEADS, IDX_DIM), k_idx, w_idx,
                          rel_bias, w_uk[l], w_uv[l])
        a = a * jax.nn.silu(gate_a)
        r = hgrn2(q_h, f_h, i_h, lb_all[l])
        r = rms_norm(r, hgrn_norm_g[l]).reshape(B, E, HGRN_WIDTH) * jax.nn.silu(gate_h)
        y = jnp.concatenate([a, r], axis=-1) @ w_out[l]
        h = layer_norm(DN_ALPHA * h + y, ln_g[l], ln_b[l])
    return h[:, N_META:]
```

```python
from contextlib import ExitStack
import numpy as np
import concourse.bass as bass
import concourse.mybir as mybir
from concourse.bass_utils import run_bass_kernel_spmd

F32 = mybir.dt.float32
BF16 = mybir.dt.bfloat16
U8 = mybir.dt.uint8
AF = mybir.ActivationFunctionType
ALU = mybir.AluOpType
AX = mybir.AxisListType

NDS = 16


class _Sem:
    def __init__(self, h, name):
        self.h = h
        self.name = name


class Buf:
    def __init__(self, name, t=None):
        self.name = name
        self.t = t
        self.last_w = None
        self.readers = {}


class _Eng:
    def __init__(self, name, sem):
        self.name = name
        self.sem = sem
        self.n = 0
        self.seen = {}
        self.q = []
        self.ndma = 0
        self.dsems = []


class Prog:
    ENG = ("pe", "act", "dve", "pool", "sp")

    def __init__(self, nc):
        self.nc = nc
        self.es = ExitStack()
        self.eng = {}
        for n in self.ENG:
            s = _Sem(self.es.enter_context(nc.semaphore("s_" + n)), n)
            self.eng[n] = _Eng(n, s)
        for n in ("sp", "pool", "act"):
            self.eng[n].dsems = [_Sem(self.es.enter_context(nc.semaphore("d_%s_%d" % (n, i))), "d%s%d" % (n, i))
                                 for i in range(NDS)]
        self.out_clocks = []
        self.nwait = 0

    def sb(self, name, shape, dt):
        return Buf(name, self.es.enter_context(self.nc.sbuf_tensor("s_" + name, list(shape), dt)))

    def ps(self, name, shape, dt):
        return Buf(name, self.es.enter_context(self.nc.psum_tensor("p_" + name, list(shape), dt)))

    def view(self, name, t):
        return Buf(name, t)

    def _need(self, E, clock, strict_same):
        sem, val = clock
        if sem is E.sem and not strict_same:
            return
        if E.seen.get(sem, 0) >= val:
            return
        E.seen[sem] = val
        E.q.append(("w", sem, val))
        self.nwait += 1

    def _deps(self, E, reads, writes, is_dma=False):
        strict = is_dma or E.name != "pe"
        for b in reads:
            if b.last_w is not None:
                self._need(E, b.last_w, True)
        for b in writes:
            if b.last_w is not None:
                self._need(E, b.last_w, strict)
            for s, v in b.readers.items():
                self._need(E, (s, v), strict)

    def _mark(self, clock, reads, writes):
        sem, val = clock
        for b in writes:
            b.last_w = clock
            b.readers = {}
        for b in reads:
            if b.readers.get(sem, 0) < val:
                b.readers[sem] = val

    def op(self, eng, fn, reads=(), writes=()):
        E = self.eng[eng]
        self._deps(E, reads, writes)
        E.n += 1
        E.q.append(("o", fn))
        self._mark((E.sem, E.n), reads, writes)

    def dma(self, queue, out, in_, reads=(), writes=(), is_output=False):
        E = self.eng[queue]
        j = E.ndma
        E.ndma += 1
        ds = E.dsems[j % NDS]
        prev = 16 * (j // NDS)
        if prev > 0:
            self._need(E, (ds, prev), True)
        self._deps(E, reads, writes, is_dma=True)
        E.q.append(("d", out, in_, ds))
        clock = (ds, prev + 16)
        self._mark(clock, reads, writes)
        if is_output:
            self.out_clocks.append(clock)
        return clock

    def finish(self, final_eng="sp"):
        E = self.eng[final_eng]
        last = {}
        for s, v in self.out_clocks:
            if last.get(s, (None, 0))[1] < v:
                last[s] = (s, v)
        for s, v in last.values():
            E.q.append(("w", s, v))
        nc = self.nc
        engs = self.eng

        def replay(E, e):
            for it in E.q:
                k = it[0]
                if k == "w":
                    e.wait_ge(it[1].h, it[2])
                elif k == "o":
                    it[1](e).then_inc(E.sem.h, 1)
                else:
                    e.dma_start(out=it[1], in_=it[2]).then_inc(it[3].h, 16)

        with nc.Block() as block:
            @block.tensor
            def _(e):
                replay(engs["pe"], e)

            @block.scalar
            def _(e):
                replay(engs["act"], e)

            @block.vector
            def _(e):
                replay(engs["dve"], e)

            @block.gpsimd
            def _(e):
                replay(engs["pool"], e)

            @block.sync
            def _(e):
                replay(engs["sp"], e)
        self.es.close()

    def alias(self, dst, srcs):
        for s in srcs:
            if s.last_w is not None:
                c = s.last_w
                if dst.readers.get(c[0], 0) < c[1]:
                    dst.readers[c[0]] = c[1]
            for k, v in s.readers.items():
                if dst.readers.get(k, 0) < v:
                    dst.readers[k] = v


D = 1024
NMETA = 16
FQ, FK, FG, TC, TQ, TF, TI, TG = 0, 512, 640, 1152, 1540, 2052, 2564, 3076
NTC = 388
NCOL = 3588
DN_ALPHA = 4.0 ** 0.25
EPS = 1e-6
NEG = -30000.0


def _mid_bc(ap, n):
    a = [list(x) for x in ap.ap]
    return bass.AP(ap.tensor, ap.offset, [a[0], [0, n]] + a[1:])


def host_constants():
    c = {}
    idx = np.arange(128)
    c["identf"] = np.eye(128, dtype=np.float32)
    same = (idx[:, None] // 64) == (idx[None, :] // 64)
    c["m1t"] = (same & (idx[:, None] <= idx[None, :])).astype(np.float32)
    c["m3t"] = (same & (idx[:, None] > idx[None, :])).astype(np.float32)
    sel = np.zeros((128, 2), np.float32)
    sel[:64, 0] = 1.0
    sel[64:, 1] = 1.0
    c["sel"] = sel
    c["causneg"] = np.where(idx[None, :] <= idx[:, None], 0.0, -1e30).astype(np.float32)
    c["cmask"] = (same & (idx[:, None] <= idx[None, :])).astype(np.float32)
    c["j128"] = np.fliplr(np.eye(128, dtype=np.float32)).copy()
    sh = np.zeros((128, 16), np.float32)
    sh[112 + np.arange(16), np.arange(16)] = 1.0
    c["shiftm"] = sh
    d = np.arange(384) - 127
    n = np.maximum(d, 0)
    nf = np.maximum(n, 1).astype(np.float32)
    large = 16 + (np.log(nf / np.float32(16)) / np.float32(np.log(128 / 16)) * np.float32(16)).astype(np.int32)
    large = np.minimum(large, 31)
    bucket = np.where(n < 16, n, large)
    oh = np.zeros((32, 384), np.float32)
    for j in range(384):
        if d[j] >= 0:
            oh[bucket[j], j] = 1.0
    c["ohd"] = oh
    c["negrow"] = np.broadcast_to(np.where(d >= 0, 0.0, NEG).astype(np.float32)[None, :], (8, 384)).copy()
    return c


CONST_SHAPES = {"identf": [128, 128], "m1t": [128, 128], "m3t": [128, 128], "sel": [128, 2],
                "causneg": [128, 128], "cmask": [128, 128], "j128": [128, 128], "shiftm": [128, 16],
                "ohd": [32, 384], "negrow": [8, 384]}


def _resplit(ap, a, b):
    return bass.AP(ap.tensor, ap.offset, [list(ap.ap[0]), [b, a], [1, b]])


def _zs(ap, n):
    return bass.AP(ap.tensor, ap.offset, [list(ap.ap[0]), [0, n]])


def _split(ap, a, b):
    p = list(ap.ap[0])
    st = ap.ap[-1][0]
    return bass.AP(ap.tensor, ap.offset, [p, [b * st, a], [st, b]])


class _StopBuild(Exception):
    pass


def build_program(nc, NT, layers=(0, 1), ktop=256, nit=24, taps=None, final_layer=1, stop_at=None):
    try:
        return _build_program(nc, NT, layers, ktop, nit, taps, final_layer, stop_at)
    except _StopBuild as e:
        P = e.args[0]
        P.finish()
        return P, {}


def _build_program(nc, NT, layers, ktop, nit, taps, final_layer, stop_at):
    NTT = NT + 1
    S = NT * 128
    P = Prog(nc)
    tap_out = {}

    def ck(name):
        if stop_at == name:
            raise _StopBuild(P)

    def dr(name, shape, kind="ExternalInput"):
        return nc.dram_tensor(name, list(shape), F32, kind=kind).ap()

    x_d = dr("x", [S, D])
    meta_d = dr("meta", [NMETA, D])
    out_d = dr("out", [S, D], "ExternalOutput")
    win_d = dr("w_in", [2, D, NCOL])
    wout_d = dr("w_out", [2, D, D])
    wuk_d = dr("w_uk", [2, 128, 1024])
    wuv_d = dr("w_uv", [2, 128, 1024])
    gkv_d = dr("gkv", [2, 128, 128])
    ghn_d = dr("ghn", [2, 128, 512])
    lng_d = dr("lng", [2, 128, D])
    lnb_d = dr("lnb", [2, 128, D])
    lbraw_d = dr("lbraw", [128, 1024])
    rb_d = dr("rb", [32, 8])
    rb31_d = dr("rb31", [8, 1])
    const_d = {k: dr(k, v) for k, v in CONST_SHAPES.items()}
    h1_d = dr("h1s", [NTT * 128, D], "Internal")
    vd_d = dr("vds", [8, 384], "Internal")
    h1B = [Buf("h1_%d" % i) for i in range(NTT)]
    vdB = Buf("vd")

    def tap(name, buf, ap, shape):
        if taps is None or name not in taps:
            return
        d = dr("tap_" + name, shape, "ExternalOutput")
        tap_out[name] = shape
        P.dma("pool", d, ap, reads=[buf], is_output=True)

    sb = P.sb
    win = sb("win", [128, 8, NCOL], BF16)
    winB = [Buf("win%d" % k, win.t) for k in range(8)]
    wout = sb("wout", [128, 8, D], BF16)
    woutB = [Buf("wout%d" % k, wout.t) for k in range(8)]
    wuk = sb("wuk", [128, 8, 128], BF16)
    wuv = sb("wuv", [128, 8, 128], BF16)
    caug = sb("caug", [128, NTT, 130], BF16)
    caugB = [Buf("caug%d" % i, caug.t) for i in range(NTT)]
    cT = sb("cT", [128, NTT * 128], BF16)
    cTB = [Buf("cT%d" % i, cT.t) for i in range(NTT)]
    kiT = sb("kiT", [128, NTT * 128], BF16)
    kiTB = [Buf("kiT%d" % i, kiT.t) for i in range(NTT)]
    B0 = sb("B0", [128, 2, 8, 128], BF16)
    B1 = sb("B1", [128, 2, 8, 128], BF16)
    shiftm = sb("shiftm", [128, 16], BF16)
    gkvB = sb("gkvB", [128, 128], F32)
    ghnB = sb("ghnB", [128, 512], F32)
    lnG = sb("lnG", [128, D], F32)
    lnBt = sb("lnB", [128, D], F32)
    lbB = sb("lbB", [128, 512], F32)
    omlB = sb("omlB", [128, 512], F32)
    identb = sb("identb", [128, 128], BF16)
    hresP = [sb("hres%d" % k, [128, D], F32) for k in range(2)]
    hres = hresP[0]
    hb = sb("hb", [128, D], BF16)
    hT = sb("hT", [128, 8, 128], BF16)
    qT = sb("qT", [128, 4, 128], BF16)
    qlTP = [sb("qlT%d" % k, [128, 8, 128], BF16) for k in range(2)]
    qlT = qlTP[0]
    qis = sb("qis", [128, 256], BF16)
    qiT = sb("qiT", [128, 4, 128], BF16)
    gaTP = [sb("gaT%d" % k, [128, 4, 128], BF16) for k in range(2)]
    gaT = gaTP[0]
    craw = sb("craw", [128, NTC], F32)
    sm = sb("sm", [128, 64], F32)
    smB = {}

    def small(name, c0, n):
        smB[name] = (Buf("sm_" + name, sm.t), c0, n)

    small("wabs", 0, 4); small("wsgn", 4, 4); small("cs", 8, 1); small("crs", 9, 1)
    small("lo", 10, 1); small("step", 11, 1); small("mid", 12, 1); small("cnt", 13, 1); small("fl", 14, 1)
    small("mx", 15, 1); small("rden", 16, 8); small("ssq4", 24, 4); small("rs4", 28, 4)
    small("sg", 36, 1); small("thrA", 37, 1); small("thrc", 38, 1); small("base", 39, 1); small("mv", 32, 2); small("lnr", 34, 1); small("lnm", 35, 1); small("dS", 40, 8)

    def smv(name, R=128, a=None, b=None):
        bf, c0, n = smB[name]
        a = 0 if a is None else a
        b = n if b is None else b
        return sm.t[0:R, c0 + a:c0 + b]

    def smb(name):
        return smB[name][0]

    stats = sb("stats", [128, 12], F32)
    sidx = sb("sidx", [128, 4096], F32)
    stgB = [Buf("stg0", sidx.t), Buf("stg1", sidx.t)]
    itmp = sb("itmp", [128, 4, 128], F32)
    itmpP = [itmp, sb("itmp2", [128, 4, 128], F32)]
    sidxB = [Buf("sidxA"), Buf("sidxB")]
    mball = sb("mball", [128, 4096], U8)
    jkD = sb("jkD", [128, 8], U8)
    jkA = sb("jkA", [128, 8], mybir.dt.int8)
    _iv = itmp.t[0:8, 0:3, :]
    vs = Buf("vs", None)
    vs_ap = bass.AP(_iv.tensor, _iv.offset, [list(_iv.ap[0]), [1, 384]])
    _mb = sb("maskb", [128, 128], BF16)
    maskb = [_mb, _mb]
    maskT = [sb("maskT%d" % i, [128, 128], BF16) for i in range(2)]
    tbM = [Buf("tbM0"), Buf("tbM1")]
    E2 = sb("E2", [128, 2, 8, 128], BF16)
    Eb = [Buf("E%d" % k, E2.t[:, k]) for k in range(2)]
    _e0 = E2.t[:, 0, 0, :]
    olat = sb("olat", [128, 8, 128], BF16)
    olatT = sb("olatT", [128, 8, 128], BF16)
    arTP = [sb("arT%d" % k, [128, 8, 128], BF16) for k in range(2)]
    arT = arTP[0]
    TT = sb("TT", [128, 5, 512], F32)
    Tb = [Buf("T%d" % k, TT.t) for k in range(5)]

    def Tv(k, R=128):
        return TT.t[0:R, k, :]
    Vb = sb("Vb", [128, 512], BF16)
    QD = sb("QD", [128, 512], BF16)
    KD = sb("KD", [128, 512], BF16)
    KL = sb("KL", [128, 512], BF16)
    KLB = sb("KLB", [128, 512], BF16)
    QDTA = sb("QDTA", [128, 4, 128], BF16)
    QDTB = sb("QDTB", [128, 4, 128], BF16)
    KDT = sb("KDT", [128, 4, 128], BF16)
    SC = sb("SC", [128, 4, 128], BF16)
    Rb = QD
    Sf = sb("Sf", [128, 4, 128], F32)
    SbE = sb("SbE", [128, 4, 128], BF16)
    SbM = sb("SbM", [128, 4, 128], BF16)
    rbs = sb("rbs", [32, 8], F32)
    rb31 = sb("rb31", [8, 1], F32)

    _e2f = bass.AP(_e0.tensor, _e0.offset, [list(_e0.ap[0]), [1, 2048]]).bitcast(F32)
    cviews = {"ohd": _e2f[0:32, 0:384], "negrow": _e2f[0:8, 384:768], "j128": _e2f[:, 768:896],
              "shiftm": _e2f[:, 896:912], "identf": itmp.t[:, 3, :]}
    cst = {}
    for k, shp in CONST_SHAPES.items():
        if k in cviews:
            cst[k] = Buf("c_" + k, cviews[k])
        elif k in ("causneg", "cmask"):
            cst[k] = sb("c_" + k, shp, BF16)
            P.dma("pool", cst[k].t[:], const_d[k], writes=[cst[k]])
            continue
        else:
            cst[k] = sb("c_" + k, shp, F32)
        P.dma("sp", cst[k].t[:], const_d[k], writes=[cst[k]])
    identf, m1t, m3t, sel, causneg, cmask, j128 = (cst[k] for k in
                                                   ("identf", "m1t", "m3t", "sel", "causneg", "cmask", "j128"))
    bank = [P.ps("bk%d" % i, [128, 512], F32) if i != 2 else P.ps("bk2", [128, 1024], BF16) for i in range(8)]
    tbB = bank[2]
    tb = bank[2].t[:, :]

    def tbv(c0, n, R=128):
        return tb[0:R, c0:c0 + n]

    op = P.op

    def act(out, in_, func, reads, writes, scale=None, bias=None, accum=None, eng="act"):
        kw = {}
        if scale is not None:
            kw["scale"] = scale
        if bias is not None:
            kw["bias"] = bias
        if accum is not None:
            kw["accum_out"] = accum
        op(eng, lambda e: e.activation(out=out, in_=in_, func=func, **kw), reads, writes)

    def ts(out, in0, s1, op0, reads, writes, s2=None, op1=None, accum=None, eng="dve"):
        kw = {}
        if op1 is not None:
            kw["op1"] = op1
        if accum is not None:
            kw["accum_out"] = accum
        op(eng, lambda e: e.tensor_scalar(out=out, in0=in0, scalar1=s1, scalar2=s2, op0=op0, **kw), reads, writes)

    def tt(out, in0, in1, o, reads, writes, eng="dve"):
        op(eng, lambda e: e.tensor_tensor(out=out, in0=in0, in1=in1, op=o), reads, writes)

    def stt(out, in0, scalar, in1, op0, op1, reads, writes):
        op("dve", lambda e: e.scalar_tensor_tensor(out=out, in0=in0, scalar=scalar, in1=in1, op0=op0, op1=op1),
           reads, writes)

    def cp(out, in_, reads, writes, eng="dve"):
        op(eng, lambda e: e.tensor_copy(out=out, in_=in_), reads, writes)

    def mm(out, lhsT, rhs, start, stop, reads, writes, sg=False):
        op("pe", lambda e: e.matmul(out, lhsT=lhsT, rhs=rhs, start=start, stop=stop, skip_group_check=sg), reads, writes)

    def tr(out, in_, R, reads, writes):
        op("pe", lambda e: e.transpose(out=out, in_=in_, identity=identb.t[0:R, 0:R]), list(reads) + [identb], writes)

    def sigm(dst, src, src_bufs, dbuf):
        act(dst, src, AF.Exp, src_bufs, [dbuf], scale=-1.0)
        act(dst, dst, AF.Ln, [dbuf], [dbuf], bias=1.0)
        act(dst, dst, AF.Exp, [dbuf], [dbuf], scale=-1.0)

    def rsqrt_small(dst, src, R, mul, add):
        dv = smv(dst[0], R, dst[1], dst[2])
        sv = smv(src[0], R, src[1], src[2])
        ts(dv, sv, mul, ALU.mult, [smb(src[0])], [smb(dst[0])], s2=add, op1=ALU.add)
        act(dv, dv, AF.Ln, [smb(dst[0])], [smb(dst[0])])
        act(dv, dv, AF.Exp, [smb(dst[0])], [smb(dst[0])], scale=-0.5)

    cp(identb.t[:], identf.t[:], [identf], [identb])
    op("dve", lambda e: e.memset(caug.t[:, :, 128:130], 1.0), [], caugB)
    op("dve", lambda e: e.memset(QDTA.t[:], 0.0), [], [QDTA])
    op("dve", lambda e: e.memset(QDTB.t[:], 0.0), [], [QDTB])
    op("dve", lambda e: e.memset(qiT.t[:], 0.0), [], [qiT])
    op("dve", lambda e: e.memset(KLB.t[:], 0.0), [], [KLB])
    P.dma("sp", rbs.t[:], rb_d, writes=[rbs])
    P.dma("sp", rb31.t[:], rb31_d, writes=[rb31])
    ohd = cst["ohd"]
    mm(bank[0].t[0:8, 0:384], rbs.t[:], ohd.t[:], True, True, [rbs, ohd], [bank[0]])
    ts(vs_ap, bank[0].t[0:8, 0:384], rb31.t[:, 0:1], ALU.subtract, [bank[0], rb31], [vs])
    tt(vs_ap, vs_ap, cst["negrow"].t[:], ALU.add, [vs, cst["negrow"]], [vs])
    P.dma("sp", vd_d, vs_ap, reads=[vs], writes=[vdB])
    P.alias(itmp, [vs])
    P.dma("sp", _split(hres.t[:, :], 8, 128), bass.AP(vd_d.tensor, 0, [[1, 128], [384, 8], [1, 128]]),
          reads=[vdB], writes=[hres])
    P.dma("sp", _resplit(TT.t[:, 0:2, :], 8, 128),
          bass.AP(vd_d.tensor, 128, [[1, 128], [384, 8], [1, 128]]), reads=[vdB], writes=[Tb[0], Tb[1]])
    for half in range(2):
        mm(bank[half].t[:, :], j128.t[:], hres.t[:, half * 512:(half + 1) * 512], True, True, [j128, hres], [bank[half]])
        act(B0.t[:, 0, half * 4:(half + 1) * 4, :], _split(bank[half].t[:, :], 4, 128), AF.Copy, [bank[half]], [B0])
        tt(B0.t[:, 1, half * 4:(half + 1) * 4, :], _split(bank[half].t[:, :], 4, 128), B0.t[:, 0, half * 4:(half + 1) * 4, :],
           ALU.subtract, [bank[half], B0], [B0])
    for half in range(2):
        mm(bank[half].t[:, :], j128.t[:], TT.t[:, half, :], True, True, [j128, Tb[half]], [bank[half]])
        act(B1.t[:, 0, half * 4:(half + 1) * 4, :], _split(bank[half].t[:, :], 4, 128), AF.Copy, [bank[half]], [B1])
        tt(B1.t[:, 1, half * 4:(half + 1) * 4, :], _split(bank[half].t[:, :], 4, 128), B1.t[:, 0, half * 4:(half + 1) * 4, :],
           ALU.subtract, [bank[half], B1], [B1])
    cp(shiftm.t[:], cst["shiftm"].t[:], [cst["shiftm"]], [shiftm])
    P.dma("sp", TT.t[:, 2:4, :], _split(lbraw_d, 2, 512), writes=[Tb[2], Tb[3]])
    tt(Tv(4), Tv(2), Tv(3), ALU.subtract, [Tb[2], Tb[3]], [Tb[4]])
    act(Tv(4), Tv(4), AF.Exp, [Tb[4]], [Tb[4]])
    ts(Tv(4), Tv(4), 1.0, ALU.add, [Tb[4]], [Tb[4]])
    op("dve", lambda e: e.reciprocal(out=lbB.t[:], in_=Tv(4)), [Tb[4]], [lbB])
    ts(omlB.t[:], lbB.t[:], -1.0, ALU.mult, [lbB], [omlB], s2=1.0, op1=ALU.add)

    for b_ in Eb:
        P.alias(b_, [cst["ohd"], cst["negrow"], cst["j128"], cst["shiftm"]])
    P.alias(itmp, [identf, vs])
    ck("prologue")
    SCALE_Q = 0.125
    SCALE_I = 1.0 / 16.0
    SCALE_H = 128.0 ** -0.5

    castn = [0]

    def cast(out, in_, reads, writes):
        e = ("pool", "dve", "act")[castn[0] % 3]
        castn[0] += 1
        if e == "act":
            act(out, in_, AF.Copy, reads, writes)
        else:
            cp(out, in_, reads, writes, eng=e)

    def pc_ap(h, R, n=129):
        return bank[5 + h // 3].t[0:R, (h % 3) * 129:(h % 3) * 129 + n]

    for l in layers:
        for b_ in stgB:
            P.alias(b_, [sidx])
        HW = NCOL // 2
        n = 0
        for kc in range(8):
            for half in range(2):
                st = stgB[n % 2]
                so = (n % 2) * 2048
                n += 1
                P.dma("sp", sidx.t[:, so:so + HW], win_d[l, kc * 128:(kc + 1) * 128, half * HW:(half + 1) * HW], writes=[st])
                cast(win.t[:, kc, half * HW:(half + 1) * HW], sidx.t[:, so:so + HW], [st], [winB[kc]])
        for j in range(8):
            st = stgB[n % 2]
            so = (n % 2) * 2048
            n += 1
            P.dma("sp", sidx.t[:, so:so + D], wout_d[l, j * 128:(j + 1) * 128, :], writes=[st])
            cast(wout.t[:, j, :], sidx.t[:, so:so + D], [st], [woutB[j]])
        st = stgB[n % 2]; so = (n % 2) * 2048; n += 1
        P.dma("sp", sidx.t[:, so:so + 1024], wuk_d[l], writes=[st])
        cast(wuk.t[:, :, :], _split(sidx.t[:, so:so + 1024], 8, 128), [st], [wuk])
        st = stgB[n % 2]; so = (n % 2) * 2048; n += 1
        P.dma("sp", sidx.t[:, so:so + 1024], wuv_d[l], writes=[st])
        cast(wuv.t[:, :, :], _split(sidx.t[:, so:so + 1024], 8, 128), [st], [wuv])
        P.alias(sidx, stgB)
        P.dma("sp", gkvB.t[:], gkv_d[l], writes=[gkvB])
        P.dma("sp", ghnB.t[:], ghn_d[l], writes=[ghnB])
        P.dma("sp", lnG.t[:], lng_d[l], writes=[lnG])
        P.dma("sp", lnBt.t[:], lnb_d[l], writes=[lnBt])
        op("dve", lambda e: e.memset(Sf.t[:], 0.0), [], [Sf])
        op("dve", lambda e: e.memset(SbE.t[:], 0.0), [], [SbE])
        last_layer = (l == final_layer)
        ck("weights%d" % l)

        def tile_gen(i, l=l, last_layer=last_layer):
            R = NMETA if i == 0 else 128
            RA = min(R, 64)
            qb = i - 1
            need_out = not (last_layer and i == 0)
            hres, qlT, gaT, arT = hresP[i % 2], qlTP[i % 2], gaTP[i % 2], arTP[i % 2]
            if l == 0:
                src, srcB = (meta_d if i == 0 else x_d[qb * 128:(qb + 1) * 128, :]), []
            else:
                src, srcB = h1_d[i * 128:i * 128 + R, :], [h1B[i]]
            P.dma("pool", hb.t[0:R, :], src, reads=srcB, writes=[hb])
            yield "A0"
            for kc in range(8):
                tr(tbv(kc * 128, R), hb.t[0:R, kc * 128:(kc + 1) * 128], R, [hb], [tbB])
            act(hT.t[:, :, 0:R], _split(tb[:, :], 8, 128)[:, :, 0:R], AF.Copy, [tbB], [hT])

            ck("l%dt%d_load" % (l, i))
            def fm_group(bk, col0, nchunk):
                for j in range(nchunk):
                    for kc in range(8):
                        mm(bk.t[:, j * 128:j * 128 + R], win.t[:, kc, col0 + j * 128:col0 + (j + 1) * 128],
                           hT.t[:, kc, 0:R], kc == 0, kc == 7, [winB[kc], hT], [bk])

            def tm_group(bk, col0, ncol):
                for kc in range(8):
                    mm(bk.t[0:R, 0:ncol], hT.t[:, kc, 0:R], win.t[:, kc, col0:col0 + ncol], kc == 0, kc == 7,
                       [winB[kc], hT], [bk])

            yield "A"
            b0, b1 = bank[0], bank[1]
            bq = bank[0]
            fm_group(bq, FQ, 4)
            act(qT.t[:, :, 0:R], _split(bq.t[:, :], 4, 128)[:, :, 0:R], AF.Copy, [bq], [qT], scale=SCALE_Q)
            yield "A"
            bq = bank[1]
            fm_group(bq, FK, 1)
            act(kiT.t[:, i * 128:i * 128 + R], bq.t[:, 0:R], AF.Copy, [bq], [kiTB[i]])
            yield "A"
            bq = bank[3]
            fm_group(bq, FG, 4)
            g4v = _split(bq.t[:, :], 4, 128)[:, :, 0:R]
            t4v = _split(TT.t[:, 4, :], 4, 128)[:, :, 0:R]
            sigm(t4v, g4v, [bq], Tb[4])
            tt(gaT.t[:, :, 0:R], g4v, t4v, ALU.mult, [bq, Tb[4]], [gaT])
            yield "A"
            bq = bank[4]
            tm_group(bq, TC, NTC)
            act(craw.t[0:R, :], bq.t[0:R, 0:NTC], AF.Copy, [bq], [craw])
            yield "A"
            bq = bank[5]
            tm_group(bq, TQ, 512)
            sigm(Tv(0, R), bq.t[0:R, :], [bq], Tb[0])
            stt(Tv(0, R), bq.t[0:R, :], SCALE_H, Tv(0, R), ALU.mult, ALU.mult, [bq, Tb[0]], [Tb[0]])
            yield "A"
            b1 = bank[6]
            tm_group(b1, TF, 512)
            act(Tv(1, R), b1.t[0:R, :], AF.Exp, [b1], [Tb[1]], scale=-1.0)
            act(Tv(2, R), Tv(1, R), AF.Ln, [Tb[1]], [Tb[2]], bias=1.0)
            act(Tv(1, R), Tv(2, R), AF.Exp, [Tb[2]], [Tb[1]], scale=-1.0)
            gs = -1.0
            if l > 0:
                gs = 1.0
                tt(Tv(1, R), Tv(1, R), omlB.t[0:R, :], ALU.mult, [Tb[1], omlB], [Tb[1]])
                tt(Tv(1, R), Tv(1, R), lbB.t[0:R, :], ALU.add, [Tb[1], lbB], [Tb[1]])
                act(Tv(2, R), Tv(1, R), AF.Ln, [Tb[1]], [Tb[2]])
            ts(Tv(1, R), Tv(1, R), -1.0, ALU.mult, [Tb[1]], [Tb[1]], s2=1.0, op1=ALU.add)
            yield "A"
            bq = bank[7]
            tm_group(bq, TI, 512)
            act(Vb.t[0:R, :], bq.t[0:R, :], AF.Copy, [bq], [Vb])
            yield "A"
            b1 = bank[0]
            tm_group(b1, TG, 512)
            sigm(Tv(3, R), b1.t[0:R, :], [b1], Tb[3])
            tt(Tv(3, R), b1.t[0:R, :], Tv(3, R), ALU.mult, [b1, Tb[3]], [Tb[3]])
            tt(Tv(3, R), Tv(3, R), ghnB.t[0:R, :], ALU.mult, [Tb[3], ghnB], [Tb[3]])

            b0, b1 = bank[0], bank[1]
            P.dma("sp", hres.t[0:R, :], src, reads=srcB, writes=[hres])
            yield "A"
            ck("l%dt%d_inproj" % (l, i))
            act(Tv(4, R)[:, 0:128], craw.t[0:R, 0:128], AF.Square, [craw], [Tb[4], smb("cs")], accum=smv("cs", R))
            rsqrt_small(("crs", 0, 1), ("cs", 0, 1), R, 1.0 / 128.0, EPS)
            stt(caug.t[0:R, i, 0:128], craw.t[0:R, 0:128], smv("crs", R), gkvB.t[0:R, :], ALU.mult, ALU.mult,
                [craw, smb("crs"), gkvB], [caugB[i]])
            tr(tbv(0, R), caug.t[0:R, i, 0:128], R, [caugB[i]], [tbB])
            ck("l%dt%d_c0" % (l, i))
            cp(cT.t[:, i * 128:i * 128 + R], tbv(0, R), [tbB], [cTB[i]])
            ck("l%dt%d_c" % (l, i))
            if i >= 1:
                wv = craw.t[0:R, 128:132]
                ts(smv("wsgn", R), wv, 0.0, ALU.is_ge, [craw], [smb("wsgn")], s2=2.0, op1=ALU.mult)
                ts(smv("wsgn", R), smv("wsgn", R), -1.0, ALU.add, [smb("wsgn")], [smb("wsgn")])
                tt(smv("wabs", R), wv, smv("wsgn", R), ALU.mult, [craw, smb("wsgn")], [smb("wabs")])
                ts(smv("wabs", R), smv("wabs", R), SCALE_I, ALU.mult, [smb("wabs")], [smb("wabs")])
                wb = smv("wabs", R)
                wbc = bass.AP(wb.tensor, wb.offset, [list(wb.ap[0]), [1, 4], [0, 64]])
                tt(_split(qis.t[0:R, :], 4, 64), _split(craw.t[0:R, 132:388], 4, 64), wbc, ALU.mult,
                   [craw, smb("wabs")], [qis])
                for j in range(2):
                    tr(tbv(j * 128, R), qis.t[0:R, j * 128:(j + 1) * 128], R, [qis], [tbB])
                for h in range(4):
                    pb = (h % 2) * 64
                    cp(qiT.t[pb:pb + 64, h, 0:R], tb[pb:pb + 64, (h // 2) * 128:(h // 2) * 128 + R], [tbB], [qiT])
            for h in range(8):
                bk = bank[3 + h // 4]
                mm(bk.t[:, (h % 4) * 128:(h % 4) * 128 + R], wuk.t[:, h, :], qT.t[:, h // 2, 0:R],
                   True, True, [wuk, qT], [bk])
            ck("l%dt%d_ql" % (l, i))
            act(qlT.t[:, 0:4, 0:R], _split(bank[3].t[:, :], 4, 128)[:, :, 0:R], AF.Copy, [bank[3]], [qlT])
            cp(qlT.t[:, 4:8, 0:R], _split(bank[4].t[:, :], 4, 128)[:, :, 0:R], [bank[4]], [qlT])

            yield "A"
            ck("l%dt%d_prep" % (l, i))
            mm(b0.t[0:R, :], m1t.t[0:R, 0:R], Tv(2, R), True, True, [m1t, Tb[2]], [b0])
            mm(b1.t[0:R, :], m3t.t[0:R, 0:R], Tv(2, R), True, True, [m3t, Tb[2]], [b1])
            for h in range(4):
                mm(bank[7].t[:, 2 * h:2 * h + 2], TT.t[0:R, 2, h * 128:(h + 1) * 128], sel.t[0:R, :], True, True,
                   [Tb[2], sel], [bank[7]])
            act(smv("dS"), bank[7].t[:, 0:8], AF.Exp, [bank[7]], [smb("dS")], scale=gs)
            act(Tv(4, R), b0.t[0:R, :], AF.Exp, [b0], [Tb[4]], scale=gs)
            tt(QD.t[0:R, :], Tv(0, R), Tv(4, R), ALU.mult, [Tb[0], Tb[4]], [QD])
            act(Tv(4, R), b0.t[0:R, :], AF.Exp, [b0], [Tb[4]], scale=-gs)
            tt(KD.t[0:R, :], Tv(1, R), Tv(4, R), ALU.mult, [Tb[1], Tb[4]], [KD])
            act(Tv(4, R), b1.t[0:R, :], AF.Exp, [b1], [Tb[4]], scale=gs)
            tt(KL.t[0:RA, :], Tv(1, RA), Tv(4, RA), ALU.mult, [Tb[1], Tb[4]], [KL])
            if R == 128:
                tt(KLB.t[64:128, :], TT.t[64:128, 1, :], TT.t[64:128, 4, :], ALU.mult, [Tb[1], Tb[4]], [KLB])
            yield "A"
            for h in range(4):
                tr(tbv(h * 128, R), QD.t[0:R, h * 128:(h + 1) * 128], R, [QD], [tbB])
            for h in range(4):
                tr(tbv(512 + h * 128, R), KD.t[0:R, h * 128:(h + 1) * 128], R, [KD], [tbB])
            t8 = _split(tb[:, :], 8, 128)
            act(QDTA.t[:, :, 0:RA], t8[:, 0:4, 0:RA], AF.Copy, [tbB], [QDTA])
            if R == 128:
                cp(QDTB.t[:, :, 64:128], t8[:, 0:4, 64:128], [tbB], [QDTB])
            act(KDT.t[:, :, 0:R], t8[:, 4:8, 0:R], AF.Copy, [tbB], [KDT])
            yield "A"
            b3, b4 = bank[3], bank[4]
            for h in range(4):
                mm(b3.t[0:R, h * 128:h * 128 + RA], KDT.t[:, h, 0:R], QDTA.t[:, h, 0:RA], True, True, [KDT, QDTA], [b3])
                if R == 128:
                    mm(b3.t[0:R, h * 128 + 64:h * 128 + 128], KDT.t[:, h, 0:R], QDTB.t[:, h, 64:128], True, True,
                       [KDT, QDTB], [b3])
            tt(SC.t[0:R, :, 0:R], _split(b3.t[0:R, :], 4, 128)[:, :, 0:R], _mid_bc(cmask.t[0:R, 0:R], 4), ALU.mult,
               [b3, cmask], [SC])
            for h in range(4):
                hc = slice(h * 128, (h + 1) * 128)
                mm(bank[5].t[:, hc], KL.t[0:RA, hc], Vb.t[0:RA, hc], True, True, [KL, Vb], [bank[5]])
                if R == 128:
                    mm(bank[6].t[:, hc], KLB.t[:, hc], Vb.t[:, hc], True, True, [KLB, Vb], [bank[6]])
            yield "A"
            for h in range(4):
                hc = slice(h * 128, (h + 1) * 128)
                mm(b4.t[0:R, hc], SC.t[0:R, h, 0:R], Vb.t[0:R, hc], h == 0, False, [SC, Vb], [b4], sg=True)
                mm(b4.t[0:R, hc], QDTA.t[:, h, 0:R], SbE.t[:, h, :], False, R < 128, [QDTA, SbE], [b4], sg=True)
            for h in range(4):
                hc = slice(h * 128, (h + 1) * 128)
                stt(Sf.t[:, h, :], Sf.t[:, h, :], smv("dS", 128, 2 * h, 2 * h + 1), bank[5].t[:, hc], ALU.mult, ALU.add,
                    [Sf, smb("dS"), bank[5]], [Sf])
            if R == 128:
                act(SbM.t[:], Sf.t[:], AF.Copy, [Sf], [SbM])
                for h in range(4):
                    hc = slice(h * 128, (h + 1) * 128)
                    mm(b4.t[0:R, hc], QDTB.t[:, h, :], SbM.t[:, h, :], False, True, [QDTB, SbM], [b4], sg=True)
                for h in range(4):
                    hc = slice(h * 128, (h + 1) * 128)
                    stt(Sf.t[:, h, :], Sf.t[:, h, :], smv("dS", 128, 2 * h + 1, 2 * h + 2), bank[6].t[:, hc], ALU.mult,
                        ALU.add, [Sf, smb("dS"), bank[6]], [Sf])
            act(SbE.t[:], Sf.t[:], AF.Copy, [Sf], [SbE])
            yield "A"
            if need_out:
                for h in range(4):
                    hc = slice(h * 128, (h + 1) * 128)
                    act(Tv(4, R)[:, hc], b4.t[0:R, hc], AF.Square, [b4], [Tb[4], smb("ssq4")],
                        accum=smv("ssq4", R, h, h + 1))
                rsqrt_small(("rs4", 0, 4), ("ssq4", 0, 4), R, 1.0 / 128.0, EPS)
                for h in range(4):
                    hc = slice(h * 128, (h + 1) * 128)
                    stt(Rb.t[0:R, hc], b4.t[0:R, hc], smv("rs4", R, h, h + 1), Tv(3, R)[:, hc], ALU.mult, ALU.mult,
                        [b4, smb("rs4"), Tb[3]], [Rb])
                for h in range(4):
                    tr(tbv(h * 128, R), Rb.t[0:R, h * 128:(h + 1) * 128], R, [Rb], [tbB])
                act(arT.t[:, 4:8, 0:R], t8[:, 0:4, 0:R], AF.Copy, [tbB], [arT])
            if taps and i == taps.get("_tile", 1) and l == taps.get("_layer", 0):
                tap("hgrn_o", b4, b4.t[0:R, :], [R, 512])
                tap("caug", caugB[i], caug.t[0:R, i, :], [R, 130])
                tap("qlT", qlT, qlT.t[:, :, :], [128, 8, 128])
                tap("kiT", kiTB[i], kiT.t[:, i * 128:(i + 1) * 128], [128, 128])
                tap("qiT", qiT, qiT.t[:, :, :], [128, 4, 128])
                tap("craw", craw, craw.t[:, :], [128, NTC])
                tap("gaT", gaT, gaT.t[:, :, :], [128, 4, 128])

            ck("l%dt%d_hgrn" % (l, i))
            yield "A_done"
            if not need_out:
                return
            use_thr = (i >= 1) and ((qb + 1) * 128 > ktop)
            if use_thr:
                nk = (qb + 1) * 128
                for b_ in sidxB:
                    b_.last_w = None
                    b_.readers = {}
                    P.alias(b_, [sidx])
                for kb0 in range(0, qb + 1, 2):
                    kbs = [kb for kb in (kb0, kb0 + 1) if kb <= qb]
                    for kb in kbs:
                        bkI = bank[kb % 2]
                        for h in range(4):
                            mm(bkI.t[:, h * 128:(h + 1) * 128], qiT.t[:, h, :],
                               kiT.t[:, (kb + 1) * 128:(kb + 2) * 128], True, True, [qiT, kiTB[kb + 1]], [bkI])
                    for kb in kbs:
                        bkI = bank[kb % 2]
                        act(itmpP[kb % 2].t[:, :, :], _split(bkI.t[:, :], 4, 128), AF.Relu, [bkI], [itmpP[kb % 2]])
                    for h in range(4):
                        for kb in kbs:
                            it_ = itmpP[kb % 2]
                            sv = sidx.t[:, kb * 128:(kb + 1) * 128]
                            sxb = sidxB[kb % 2]
                            if h == 0:
                                ts(sv, it_.t[:, 0, :], smv("wsgn", 128, 0, 1), ALU.mult, [it_, smb("wsgn")], [sxb])
                            else:
                                stt(sv, it_.t[:, h, :], smv("wsgn", 128, h, h + 1), sv, ALU.mult, ALU.add,
                                    [it_, smb("wsgn"), sxb], [sxb])
                    yield "idx"
                lw = [b_.last_w for b_ in sidxB if b_.last_w is not None]
                assert all(c[0] is lw[0][0] for c in lw)
                sidx.last_w = max(lw, key=lambda c: c[1])
                sidx.readers = {}
                yield "idx_done"
                sa = sidx.t[:, 0:nk]
                op("dve", lambda e, a=sa: e.tensor_reduce(out=smv("mx"), in_=a, axis=AX.X, op=ALU.max), [sidx], [smb("mx")])
                op("dve", lambda e, a=sa: e.tensor_reduce(out=smv("lo"), in_=a, axis=AX.X, op=ALU.min), [sidx], [smb("lo")])
                tt(smv("step"), smv("mx"), smv("lo"), ALU.subtract, [smb("mx"), smb("lo")], [smb("step")])
                dv = sidx.t[:, qb * 128:(qb + 1) * 128]
                tt(dv, dv, causneg.t[:], ALU.add, [sidx, causneg], [sidx])
                cD = int(nk * BIS_DVE_FRAC)
                thr_c = float(ktop) - 0.5 - (nk - cD) / 2.0
                stt(smv("mid"), smv("step"), 0.5, smv("lo"), ALU.mult, ALU.add, [smb("step"), smb("lo")], [smb("mid")])
                op("dve", lambda e, v_=thr_c: e.memset(smv("thrc"), v_), [], [smb("thrc")])
                for k in range(1, nit + 1):
                    f = 2.0 ** (-k)
                    act(_zs(jkA.t[:, 0:1], nk - cD), sidx.t[:, cD:nk], AF.Sign, [sidx, smb("mid")], [jkA, smb("sg")],
                        scale=-1.0, bias=smv("mid"), accum=smv("sg"))
                    ts(_zs(jkD.t[:, 0:1], cD), sidx.t[:, 0:cD], smv("mid"), ALU.is_ge, [sidx, smb("mid")], [jkD, smb("cnt")],
                       s2=0.0, op1=ALU.add, accum=smv("cnt"))
                    act(smv("thrA"), smv("sg"), AF.Identity, [smb("sg"), smb("thrc")], [smb("thrA")], scale=0.5, bias=smv("thrc"))
                    stt(smv("base"), smv("step"), -0.5 * f, smv("mid"), ALU.mult, ALU.add, [smb("step"), smb("mid")],
                        [smb("base")])
                    ts(smv("fl"), smv("cnt"), smv("thrA"), ALU.is_ge, [smb("cnt"), smb("thrA")], [smb("fl")], s2=f, op1=ALU.mult)
                    stt(smv("mid"), smv("fl"), smv("step"), smv("base"), ALU.mult, ALU.add,
                        [smb("fl"), smb("step"), smb("base")], [smb("mid")])
                    yield "bis"
                stt(smv("lo"), smv("step"), -(2.0 ** (-nit - 1)), smv("mid"), ALU.mult, ALU.add, [smb("step"), smb("mid")],
                    [smb("lo")])
                ts(mball.t[:, 0:nk], sa, smv("lo"), ALU.is_ge, [sidx, smb("lo")], [mball])
            else:
                yield "idx_done"
            yield "bis_done"
            ck("l%dt%d_thr" % (l, i))
            if i == 0:
                kblocks = [(0, NMETA, (identb.t[0:NMETA, 0:NMETA], B0, NMETA), False, None)]
            else:
                if qb == 0:
                    kblocks = [(0, NMETA, (shiftm.t[:, :], B1, 128), False, None)]
                else:
                    kblocks = [(0, NMETA, None, False, None)]
                for kb in range(qb + 1):
                    if kb == qb:
                        bias = (identb.t[:, :], B0, 128)
                    elif kb == qb - 1:
                        bias = (identb.t[:, :], B1, 128)
                    else:
                        bias = None
                    kblocks.append((kb + 1, 128, bias, use_thr, kb))
            nblk = len(kblocks)
            def emit_pc(bi_, kt_, KR_, E_):
                for h in range(8):
                    mm(pc_ap(h, R), E_.t[0:KR_, h, 0:R], caug.t[0:KR_, kt_, 0:129], bi_ == 0 and h % 3 == 0, bi_ == nblk - 1,
                       [E_, caugB[kt_]], [bank[5 + h // 3]], sg=True)

            pend = None
            for b_ in tbM:
                b_.last_w, b_.readers = None, {}
                P.alias(b_, [tbB])
            for bi, (kt, KR, bias, masked, kb) in enumerate(kblocks):
                E = Eb[bi % 2]
                if masked:
                    mb = maskb[bi % 2]
                    mT = maskT[bi % 2]
                    act(mb.t[:, :], mball.t[:, kb * 128:(kb + 1) * 128], AF.Copy, [mball], [mb])
                    tr(tbv((bi % 2) * 128, 128), mb.t[:, :], 128, [mb], [tbM[bi % 2]])
                    act(mT.t[:, :], tbv((bi % 2) * 128, 128), AF.Copy, [tbM[bi % 2]], [mT])
                for half in range(2):
                    bk = bank[3 + half]
                    lgv = _split(bk.t[0:KR, 0:4 * R], 4, R)
                    nacc = 1 + (2 if bias is not None else 0)
                    na = [0]

                    def acc(lhsT, rhs, rd):
                        na[0] += 1
                        mm(bk.t[0:KR, 0:4 * R], lhsT, rhs, na[0] == 1, na[0] == nacc, rd, [bk])
                    acc(cT.t[:, kt * 128:kt * 128 + KR], qlT.t[:, 4 * half:4 * half + 4, 0:R], [cTB[kt], qlT])
                    if bias is not None:
                        bl, bt, bk_rows = bias
                        for hl in range(2):
                            acc(bl, bt.t[0:bk_rows, hl, 4 * half:4 * half + 4, 0:R], [bt, identb, shiftm])
                    act(E.t[0:KR, 4 * half:4 * half + 4, 0:R], lgv, AF.Exp, [bk], [E])
                if masked:
                    tt(E.t[:, :, :], E.t[:, :, :], _mid_bc(mT.t[:, :], 8), ALU.mult, [E, mT], [E], eng="pool")
                if pend is not None:
                    emit_pc(*pend)
                pend = (bi, kt, KR, E)
                yield "post"
            emit_pc(*pend)
            P.alias(tbB, tbM)
            for g in range(3):
                nh = 3 if g < 2 else 2
                dn = bank[5 + g].t[0:R, 0:nh * 129]
                dnv = bass.AP(dn.tensor, dn.offset + 128, [list(dn.ap[0]), [129, nh]])
                op("dve", lambda e, a=dnv, o_=smv("rden", R, 3 * g, 3 * g + nh): e.reciprocal(out=o_, in_=a),
                   [bank[5 + g]], [smb("rden")])
            for h in range(8):
                if h % 2 == 0:
                    act(olat.t[0:R, h, :], pc_ap(h, R, 128), AF.Copy, [bank[5 + h // 3], smb("rden")], [olat],
                        scale=smv("rden", R, h, h + 1))
                else:
                    ts(olat.t[0:R, h, :], pc_ap(h, R, 128), smv("rden", R, h, h + 1), ALU.mult,
                       [bank[5 + h // 3], smb("rden")], [olat])
            for h in range(8):
                tr(tbv(h * 128, R), olat.t[0:R, h, :], R, [olat], [tbB])
            act(olatT.t[:, :, 0:R], t8[:, :, 0:R], AF.Copy, [tbB], [olatT])
            yield "post"
            for j in range(4):
                mm(b0.t[:, j * 128:j * 128 + R], wuv.t[:, 2 * j, :], olatT.t[:, 2 * j, 0:R], True, False, [wuv, olatT], [b0])
                mm(b0.t[:, j * 128:j * 128 + R], wuv.t[:, 2 * j + 1, :], olatT.t[:, 2 * j + 1, 0:R], False, True,
                   [wuv, olatT], [b0])
            tt(arT.t[:, 0:4, 0:R], _split(b0.t[:, :], 4, 128)[:, :, 0:R], gaT.t[:, :, 0:R], ALU.mult, [b0, gaT], [arT])

            yield "post"
            ck("l%dt%d_attn" % (l, i))
            for half in range(2):
                bk = bank[half]
                for j in range(8):
                    mm(bk.t[0:R, :], arT.t[:, j, 0:R], wout.t[:, j, half * 512:(half + 1) * 512], j == 0, j == 7,
                       [arT, woutB[j]], [bk])
            for half in range(2):
                hv = hres.t[0:R, half * 512:(half + 1) * 512]
                stt(hv, hv, DN_ALPHA, bank[half].t[0:R, :], ALU.mult, ALU.add, [hres, bank[half]], [hres])
            stB = Buf("stats_", stats.t)
            for half in range(2):
                op("dve", lambda e, hf=half: e.bn_stats(out=stats.t[0:R, hf * 6:(hf + 1) * 6],
                                                        in_=hres.t[0:R, hf * 512:(hf + 1) * 512]), [hres], [stB])
            op("dve", lambda e: e.bn_aggr(out=smv("mv", R), in_=stats.t[0:R, :]), [stB], [smb("mv")])
            ts(smv("lnr", R), smv("mv", R, 1, 2), EPS, ALU.add, [smb("mv")], [smb("lnr")])
            act(smv("lnr", R), smv("lnr", R), AF.Ln, [smb("lnr")], [smb("lnr")])
            act(smv("lnr", R), smv("lnr", R), AF.Exp, [smb("lnr")], [smb("lnr")], scale=-0.5)
            ts(smv("lnm", R), smv("mv", R, 0, 1), -1.0, ALU.mult, [smb("mv"), smb("lnr")], [smb("lnm")],
               s2=smv("lnr", R), op1=ALU.mult)
            act(hres.t[0:R, :], hres.t[0:R, :], AF.Identity, [hres, smb("lnr"), smb("lnm")], [hres],
                scale=smv("lnr", R), bias=smv("lnm", R))
            tt(hres.t[0:R, :], hres.t[0:R, :], lnG.t[0:R, :], ALU.mult, [hres, lnG], [hres])
            tt(hres.t[0:R, :], hres.t[0:R, :], lnBt.t[0:R, :], ALU.add, [hres, lnBt], [hres])
            if last_layer:
                P.dma("pool", out_d[qb * 128:(qb + 1) * 128, :], hres.t[0:R, :], reads=[hres], is_output=True)
            else:
                P.dma("pool", h1_d[i * 128:i * 128 + R, :], hres.t[0:R, :], reads=[hres], writes=[h1B[i]])
        def step(g):
            try:
                return next(g)
            except StopIteration:
                return None

        def run_to(g, marker):
            while True:
                m = step(g)
                if m is None or m == marker:
                    return m

        gens = [tile_gen(i) for i in range(NTT)]
        run_to(gens[0], "A_done")
        run_to(gens[0], "bis_done")
        if NTT > 1:
            run_to(gens[1], "A_done")
        for i in range(NTT):
            s1 = gens[i]
            s2 = gens[i + 1] if i + 1 < NTT else None
            s3 = gens[i + 2] if i + 2 < NTT else None
            l1, l2, l3 = True, s2 is not None, s3 is not None
            if l3:
                step(s3)
            while l1 or l2 or l3:
                if l2:
                    m = step(s2)
                    if m is None or m == "bis_done":
                        l2 = False
                if l1:
                    if step(s1) is None:
                        l1 = False
                elif l3:
                    m = step(s3)
                    if m is None or m == "A_done":
                        l3 = False
    P.finish()
    return P, tap_out


def prep_shared(inp):
    w_in = np.asarray(inp["w_in"], np.float32)
    o = {"q": (0, 512), "c": (512, 640), "qi": (640, 896), "ki": (896, 960), "wi": (960, 964), "ga": (964, 1476),
         "qh": (1476, 1988), "fh": (1988, 2500), "ih": (2500, 3012), "gh": (3012, 3524)}
    order = ["q", "ki", "ki", "ga", "c", "wi", "qi", "qh", "fh", "ih", "gh"]
    w_perm = np.ascontiguousarray(np.concatenate([w_in[:, :, o[k][0]:o[k][1]] for k in order], axis=2))
    assert w_perm.shape[2] == NCOL
    w_uk = np.asarray(inp["w_uk"], np.float32)
    wuk_l = np.zeros((2, 128, 8, 128), np.float32)
    for h in range(8):
        wuk_l[:, (h % 2) * 64:(h % 2) * 64 + 64, h, :] = w_uk[:, h]
    w_uv = np.asarray(inp["w_uv"], np.float32)
    wuv_l = np.zeros((2, 128, 8, 128), np.float32)
    for h in range(8):
        wuv_l[:, :, h, (h % 2) * 64:(h % 2) * 64 + 64] = w_uv[:, h]
    bc = lambda a, n: np.ascontiguousarray(np.broadcast_to(a[:, None, :], (a.shape[0], n, a.shape[1])))
    ghn = np.tile(np.asarray(inp["hgrn_norm_g"], np.float32), (1, 4))
    rb = np.asarray(inp["rel_bias"], np.float32)
    lbraw = np.asarray(inp["hgrn_lb_raw"], np.float32).reshape(1, 1024)
    d = {
        "meta": np.ascontiguousarray(np.asarray(inp["meta_tokens"], np.float32)),
        "w_in": w_perm,
        "w_out": np.ascontiguousarray(np.asarray(inp["w_out"], np.float32)),
        "w_uk": wuk_l.reshape(2, 128, 1024),
        "w_uv": wuv_l.reshape(2, 128, 1024),
        "gkv": bc(np.asarray(inp["kv_norm_g"], np.float32), 128),
        "ghn": bc(ghn, 128),
        "lng": bc(np.asarray(inp["ln_g"], np.float32), 128),
        "lnb": bc(np.asarray(inp["ln_b"], np.float32), 128),
        "lbraw": np.ascontiguousarray(np.broadcast_to(lbraw, (128, 1024))),
        "rb": np.ascontiguousarray(rb),
        "rb31": np.ascontiguousarray(rb[31].reshape(8, 1)),
    }
    d.update(host_constants())
    return d


NT_FULL = 32
NIT = 18
BIS_DVE_FRAC = 0.45


def kernel(**inputs):
    shared = prep_shared(inputs)
    x = np.asarray(inputs["x"], np.float32)
    B = x.shape[0]
    nc = bass.Bass("TRN2", target_bir_lowering=False)
    build_program(nc, NT_FULL, layers=(0, 1), ktop=256, nit=NIT, final_layer=1)
    in_maps = []
    for b in range(B):
        m = dict(shared)
        m["x"] = np.ascontiguousarray(x[b])
        in_maps.append(m)
    res = run_bass_kernel_spmd(nc, in_maps, core_ids=list(range(B)))
    out = np.stack([np.asarray(r["out"], np.float32) for r in res.results], axis=0)
    return out
```

```python
from contextlib import ExitStack
import numpy as np
import concourse.bass as bass
import concourse.mybir as mybir
from concourse.bass_utils import run_bass_kernel_spmd

F32 = mybir.dt.float32
BF16 = mybir.dt.bfloat16
U8 = mybir.dt.uint8
AF = mybir.ActivationFunctionType
ALU = mybir.AluOpType
AX = mybir.AxisListType

NDS = 16


class _Sem:
    def __init__(self, h, name):
        self.h = h
        self.name = name


class Buf:
    def __init__(self, name, t=None):
        self.name = name
        self.t = t
        self.last_w = None
        self.readers = {}


class _Eng:
    def __init__(self, name, sem):
        self.name = name
        self.sem = sem
        self.n = 0
        self.seen = {}
        self.q = []
        self.ndma = 0
        self.dsems = []


class Prog:
    ENG = ("pe", "act", "dve", "pool", "sp")

    def __init__(self, nc):
        self.nc = nc
        self.es = ExitStack()
        self.eng = {}
        for n in self.ENG:
            s = _Sem(self.es.enter_context(nc.semaphore("s_" + n)), n)
            self.eng[n] = _Eng(n, s)
        for n in ("sp", "pool", "act"):
            self.eng[n].dsems = [_Sem(self.es.enter_context(nc.semaphore("d_%s_%d" % (n, i))), "d%s%d" % (n, i))
                                 for i in range(NDS)]
        self.out_clocks = []
        self.nwait = 0

    def sb(self, name, shape, dt):
        return Buf(name, self.es.enter_context(self.nc.sbuf_tensor("s_" + name, list(shape), dt)))

    def ps(self, name, shape, dt):
        return Buf(name, self.es.enter_context(self.nc.psum_tensor("p_" + name, list(shape), dt)))

    def view(self, name, t):
        return Buf(name, t)

    def _need(self, E, clock, strict_same):
        sem, val = clock
        if sem is E.sem and not strict_same:
            return
        if E.seen.get(sem, 0) >= val:
            return
        E.seen[sem] = val
        E.q.append(("w", sem, val))
        self.nwait += 1

    def _deps(self, E, reads, writes, is_dma=False):
        strict = is_dma or E.name != "pe"
        for b in reads:
            if b.last_w is not None:
                self._need(E, b.last_w, True)
        for b in writes:
            if b.last_w is not None:
                self._need(E, b.last_w, strict)
            for s, v in b.readers.items():
                self._need(E, (s, v), strict)

    def _mark(self, clock, reads, writes):
        sem, val = clock
        for b in writes:
            b.last_w = clock
            b.readers = {}
        for b in reads:
            if b.readers.get(sem, 0) < val:
                b.readers[sem] = val

    def op(self, eng, fn, reads=(), writes=()):
        E = self.eng[eng]
        self._deps(E, reads, writes)
        E.n += 1
        E.q.append(("o", fn))
        self._mark((E.sem, E.n), reads, writes)

    def dma(self, queue, out, in_, reads=(), writes=(), is_output=False):
        E = self.eng[queue]
        j = E.ndma
        E.ndma += 1
        ds = E.dsems[j % NDS]
        prev = 16 * (j // NDS)
        if prev > 0:
            self._need(E, (ds, prev), True)
        self._deps(E, reads, writes, is_dma=True)
        E.q.append(("d", out, in_, ds))
        clock = (ds, prev + 16)
        self._mark(clock, reads, writes)
        if is_output:
            self.out_clocks.append(clock)
        return clock

    def finish(self, final_eng="sp"):
        E = self.eng[final_eng]
        last = {}
        for s, v in self.out_clocks:
            if last.get(s, (None, 0))[1] < v:
                last[s] = (s, v)
        for s, v in last.values():
            E.q.append(("w", s, v))
        nc = self.nc
        engs = self.eng

        def replay(E, e):
            for it in E.q:
                k = it[0]
                if k == "w":
                    e.wait_ge(it[1].h, it[2])
                elif k == "o":
                    it[1](e).then_inc(E.sem.h, 1)
                else:
                    e.dma_start(out=it[1], in_=it[2]).then_inc(it[3].h, 16)

        with nc.Block() as block:
            @block.tensor
            def _(e):
                replay(engs["pe"], e)

            @block.scalar
            def _(e):
                replay(engs["act"], e)

            @block.vector
            def _(e):
                replay(engs["dve"], e)

            @block.gpsimd
            def _(e):
                replay(engs["pool"], e)

            @block.sync
            def _(e):
                replay(engs["sp"], e)
        self.es.close()

    def alias(self, dst, srcs):
        for s in srcs:
            if s.last_w is not None:
                c = s.last_w
                if dst.readers.get(c[0], 0) < c[1]:
                    dst.readers[c[0]] = c[1]
            for k, v in s.readers.items():
                if dst.readers.get(k, 0) < v:
                    dst.readers[k] = v


D = 1024
NMETA = 16
FQ, FK, FG, TC, TQ, TF, TI, TG = 0, 512, 640, 1152, 1540, 2052, 2564, 3076
NTC = 388
NCOL = 3588
DN_ALPHA = 4.0 ** 0.25
EPS = 1e-6
NEG = -30000.0


def _mid_bc(ap, n):
    a = [list(x) for x in ap.ap]
    return bass.AP(ap.tensor, ap.offset, [a[0], [0, n]] + a[1:])


def host_constants():
    c = {}
    idx = np.arange(128)
    c["identf"] = np.eye(128, dtype=np.float32)
    same = (idx[:, None] // 64) == (idx[None, :] // 64)
    c["m1t"] = (same & (idx[:, None] <= idx[None, :])).astype(np.float32)
    c["m3t"] = (same & (idx[:, None] > idx[None, :])).astype(np.float32)
    sel = np.zeros((128, 2), np.float32)
    sel[:64, 0] = 1.0
    sel[64:, 1] = 1.0
    c["sel"] = sel
    c["causneg"] = np.where(idx[None, :] <= idx[:, None], 0.0, -1e30).astype(np.float32)
    c["cmask"] = (same & (idx[:, None] <= idx[None, :])).astype(np.float32)
    c["j128"] = np.fliplr(np.eye(128, dtype=np.float32)).copy()
    sh = np.zeros((128, 16), np.float32)
    sh[112 + np.arange(16), np.arange(16)] = 1.0
    c["shiftm"] = sh
    d = np.arange(384) - 127
    n = np.maximum(d, 0)
    nf = np.maximum(n, 1).astype(np.float32)
    large = 16 + (np.log(nf / np.float32(16)) / np.float32(np.log(128 / 16)) * np.float32(16)).astype(np.int32)
    large = np.minimum(large, 31)
    bucket = np.where(n < 16, n, large)
    oh = np.zeros((32, 384), np.float32)
    for j in range(384):
        if d[j] >= 0:
            oh[bucket[j], j] = 1.0
    c["ohd"] = oh
    c["negrow"] = np.broadcast_to(np.where(d >= 0, 0.0, NEG).astype(np.float32)[None, :], (8, 384)).copy()
    return c


CONST_SHAPES = {"identf": [128, 128], "m1t": [128, 128], "m3t": [128, 128], "sel": [128, 2],
                "causneg": [128, 128], "cmask": [128, 128], "j128": [128, 128], "shiftm": [128, 16],
                "ohd": [32, 384], "negrow": [8, 384]}


def _resplit(ap, a, b):
    return bass.AP(ap.tensor, ap.offset, [list(ap.ap[0]), [b, a], [1, b]])


def _zs(ap, n):
    return bass.AP(ap.tensor, ap.offset, [list(ap.ap[0]), [0, n]])


def _split(ap, a, b):
    p = list(ap.ap[0])
    st = ap.ap[-1][0]
    return bass.AP(ap.tensor, ap.offset, [p, [b * st, a], [st, b]])


class _StopBuild(Exception):
    pass


def build_program(nc, NT, layers=(0, 1), ktop=256, nit=24, taps=None, final_layer=1, stop_at=None):
    try:
        return _build_program(nc, NT, layers, ktop, nit, taps, final_layer, stop_at)
    except _StopBuild as e:
        P = e.args[0]
        P.finish()
        return P, {}


def _build_program(nc, NT, layers, ktop, nit, taps, final_layer, stop_at):
    NTT = NT + 1
    S = NT * 128
    P = Prog(nc)
    tap_out = {}

    def ck(name):
        if stop_at == name:
            raise _StopBuild(P)

    def dr(name, shape, kind="ExternalInput"):
        return nc.dram_tensor(name, list(shape), F32, kind=kind).ap()

    x_d = dr("x", [S, D])
    meta_d = dr("meta", [NMETA, D])
    out_d = dr("out", [S, D], "ExternalOutput")
    win_d = dr("w_in", [2, D, NCOL])
    wout_d = dr("w_out", [2, D, D])
    wuk_d = dr("w_uk", [2, 128, 1024])
    wuv_d = dr("w_uv", [2, 128, 1024])
    gkv_d = dr("gkv", [2, 128, 128])
    ghn_d = dr("ghn", [2, 128, 512])
    lng_d = dr("lng", [2, 128, D])
    lnb_d = dr("lnb", [2, 128, D])
    lbraw_d = dr("lbraw", [128, 1024])
    rb_d = dr("rb", [32, 8])
    rb31_d = dr("rb31", [8, 1])
    const_d = {k: dr(k, v) for k, v in CONST_SHAPES.items()}
    h1_d = dr("h1s", [NTT * 128, D], "Internal")
    vd_d = dr("vds", [8, 384], "Internal")
    h1B = [Buf("h1_%d" % i) for i in range(NTT)]
    vdB = Buf("vd")

    def tap(name, buf, ap, shape):
        if taps is None or name not in taps:
            return
        d = dr("tap_" + name, shape, "ExternalOutput")
        tap_out[name] = shape
        P.dma("pool", d, ap, reads=[buf], is_output=True)

    sb = P.sb
    win = sb("win", [128, 8, NCOL], BF16)
    winB = [Buf("win%d" % k, win.t) for k in range(8)]
    wout = sb("wout", [128, 8, D], BF16)
    woutB = [Buf("wout%d" % k, wout.t) for k in range(8)]
    wuk = sb("wuk", [128, 8, 128], BF16)
    wuv = sb("wuv", [128, 8, 128], BF16)
    caug = sb("caug", [128, NTT, 130], BF16)
    caugB = [Buf("caug%d" % i, caug.t) for i in range(NTT)]
    cT = sb("cT", [128, NTT * 128], BF16)
    cTB = [Buf("cT%d" % i, cT.t) for i in range(NTT)]
    kiT = sb("kiT", [128, NTT * 128], BF16)
    kiTB = [Buf("kiT%d" % i, kiT.t) for i in range(NTT)]
    B0 = sb("B0", [128, 2, 8, 128], BF16)
    B1 = sb("B1", [128, 2, 8, 128], BF16)
    shiftm = sb("shiftm", [128, 16], BF16)
    gkvB = sb("gkvB", [128, 128], F32)
    ghnB = sb("ghnB", [128, 512], F32)
    lnG = sb("lnG", [128, D], F32)
    lnBt = sb("lnB", [128, D], F32)
    lbB = sb("lbB", [128, 512], F32)
    omlB = sb("omlB", [128, 512], F32)
    identb = sb("identb", [128, 128], BF16)
    hresP = [sb("hres%d" % k, [128, D], F32) for k in range(2)]
    hres = hresP[0]
    hb = sb("hb", [128, D], BF16)
    hT = sb("hT", [128, 8, 128], BF16)
    qT = sb("qT", [128, 4, 128], BF16)
    qlTP = [sb("qlT%d" % k, [128, 8, 128], BF16) for k in range(2)]
    qlT = qlTP[0]
    qis = sb("qis", [128, 256], BF16)
    qiT = sb("qiT", [128, 4, 128], BF16)
    gaTP = [sb("gaT%d" % k, [128, 4, 128], BF16) for k in range(2)]
    gaT = gaTP[0]
    craw = sb("craw", [128, NTC], F32)
    sm = sb("sm", [128, 64], F32)
    smB = {}

    def small(name, c0, n):
        smB[name] = (Buf("sm_" + name, sm.t), c0, n)

    small("wabs", 0, 4); small("wsgn", 4, 4); small("cs", 8, 1); small("crs", 9, 1)
    small("lo", 10, 1); small("step", 11, 1); small("mid", 12, 1); small("cnt", 13, 1); small("fl", 14, 1)
    small("mx", 15, 1); small("rden", 16, 8); small("ssq4", 24, 4); small("rs4", 28, 4)
    small("sg", 36, 1); small("thrA", 37, 1); small("thrc", 38, 1); small("base", 39, 1); small("mv", 32, 2); small("lnr", 34, 1); small("lnm", 35, 1); small("dS", 40, 8)

    def smv(name, R=128, a=None, b=None):
        bf, c0, n = smB[name]
        a = 0 if a is None else a
        b = n if b is None else b
        return sm.t[0:R, c0 + a:c0 + b]

    def smb(name):
        return smB[name][0]

    stats = sb("stats", [128, 12], F32)
    sidx = sb("sidx", [128, 4096], F32)
    stgB = [Buf("stg0", sidx.t), Buf("stg1", sidx.t)]
    itmp = sb("itmp", [128, 4, 128], F32)
    itmpP = [itmp, sb("itmp2", [128, 4, 128], F32)]
    sidxB = [Buf("sidxA"), Buf("sidxB")]
    mball = sb("mball", [128, 4096], U8)
    jkD = sb("jkD", [128, 8], U8)
    jkA = sb("jkA", [128, 8], mybir.dt.int8)
    _iv = itmp.t[0:8, 0:3, :]
    vs = Buf("vs", None)
    vs_ap = bass.AP(_iv.tensor, _iv.offset, [list(_iv.ap[0]), [1, 384]])
    maskb = [sb("maskb%d" % i, [128, 128], BF16) for i in range(2)]
    E2 = sb("E2", [128, 2, 8, 128], BF16)
    Eb = [Buf("E%d" % k, E2.t[:, k]) for k in range(2)]
    _e0 = E2.t[:, 0, 0, :]
    olat = sb("olat", [128, 8, 128], BF16)
    olatT = sb("olatT", [128, 8, 128], BF16)
    arTP = [sb("arT%d" % k, [128, 8, 128], BF16) for k in range(2)]
    arT = arTP[0]
    TT = sb("TT", [128, 5, 512], F32)
    Tb = [Buf("T%d" % k, TT.t) for k in range(5)]

    def Tv(k, R=128):
        return TT.t[0:R, k, :]
    Vb = sb("Vb", [128, 512], BF16)
    QD = sb("QD", [128, 512], BF16)
    KD = sb("KD", [128, 512], BF16)
    KL = sb("KL", [128, 512], BF16)
    KLB = sb("KLB", [128, 512], BF16)
    QDTA = sb("QDTA", [128, 4, 128], BF16)
    QDTB = sb("QDTB", [128, 4, 128], BF16)
    KDT = sb("KDT", [128, 4, 128], BF16)
    SC = sb("SC", [128, 4, 128], BF16)
    Rb = QD
    Sf = sb("Sf", [128, 4, 128], F32)
    SbE = sb("SbE", [128, 4, 128], BF16)
    SbM = sb("SbM", [128, 4, 128], BF16)
    rbs = sb("rbs", [32, 8], F32)
    rb31 = sb("rb31", [8, 1], F32)

    _e2f = bass.AP(_e0.tensor, _e0.offset, [list(_e0.ap[0]), [1, 2048]]).bitcast(F32)
    cviews = {"ohd": _e2f[0:32, 0:384], "negrow": _e2f[0:8, 384:768], "j128": _e2f[:, 768:896],
              "shiftm": _e2f[:, 896:912], "identf": itmp.t[:, 3, :]}
    cst = {}
    for k, shp in CONST_SHAPES.items():
        if k in cviews:
            cst[k] = Buf("c_" + k, cviews[k])
        else:
            cst[k] = sb("c_" + k, shp, F32)
        P.dma("sp", cst[k].t[:], const_d[k], writes=[cst[k]])
    identf, m1t, m3t, sel, causneg, cmask, j128 = (cst[k] for k in
                                                   ("identf", "m1t", "m3t", "sel", "causneg", "cmask", "j128"))
    bank = [P.ps("bk%d" % i, [128, 512], F32) if i != 2 else P.ps("bk2", [128, 1024], BF16) for i in range(8)]
    tbB = bank[2]
    tb = bank[2].t[:, :]

    def tbv(c0, n, R=128):
        return tb[0:R, c0:c0 + n]

    op = P.op

    def act(out, in_, func, reads, writes, scale=None, bias=None, accum=None, eng="act"):
        kw = {}
        if scale is not None:
            kw["scale"] = scale
        if bias is not None:
            kw["bias"] = bias
        if accum is not None:
            kw["accum_out"] = accum
        op(eng, lambda e: e.activation(out=out, in_=in_, func=func, **kw), reads, writes)

    def ts(out, in0, s1, op0, reads, writes, s2=None, op1=None, accum=None, eng="dve"):
        kw = {}
        if op1 is not None:
            kw["op1"] = op1
        if accum is not None:
            kw["accum_out"] = accum
        op(eng, lambda e: e.tensor_scalar(out=out, in0=in0, scalar1=s1, scalar2=s2, op0=op0, **kw), reads, writes)

    def tt(out, in0, in1, o, reads, writes, eng="dve"):
        op(eng, lambda e: e.tensor_tensor(out=out, in0=in0, in1=in1, op=o), reads, writes)

    def stt(out, in0, scalar, in1, op0, op1, reads, writes):
        op("dve", lambda e: e.scalar_tensor_tensor(out=out, in0=in0, scalar=scalar, in1=in1, op0=op0, op1=op1),
           reads, writes)

    def cp(out, in_, reads, writes, eng="dve"):
        op(eng, lambda e: e.tensor_copy(out=out, in_=in_), reads, writes)

    def mm(out, lhsT, rhs, start, stop, reads, writes, sg=False):
        op("pe", lambda e: e.matmul(out, lhsT=lhsT, rhs=rhs, start=start, stop=stop, skip_group_check=sg), reads, writes)

    def tr(out, in_, R, reads, writes):
        op("pe", lambda e: e.transpose(out=out, in_=in_, identity=identb.t[0:R, 0:R]), list(reads) + [identb], writes)

    def sigm(dst, src, src_bufs, dbuf):
        act(dst, src, AF.Exp, src_bufs, [dbuf], scale=-1.0)
        act(dst, dst, AF.Ln, [dbuf], [dbuf], bias=1.0)
        act(dst, dst, AF.Exp, [dbuf], [dbuf], scale=-1.0)

    def rsqrt_small(dst, src, R, mul, add):
        dv = smv(dst[0], R, dst[1], dst[2])
        sv = smv(src[0], R, src[1], src[2])
        ts(dv, sv, mul, ALU.mult, [smb(src[0])], [smb(dst[0])], s2=add, op1=ALU.add)
        act(dv, dv, AF.Ln, [smb(dst[0])], [smb(dst[0])])
        act(dv, dv, AF.Exp, [smb(dst[0])], [smb(dst[0])], scale=-0.5)

    cp(identb.t[:], identf.t[:], [identf], [identb])
    op("dve", lambda e: e.memset(caug.t[:, :, 128:130], 1.0), [], caugB)
    op("dve", lambda e: e.memset(QDTA.t[:], 0.0), [], [QDTA])
    op("dve", lambda e: e.memset(QDTB.t[:], 0.0), [], [QDTB])
    op("dve", lambda e: e.memset(qiT.t[:], 0.0), [], [qiT])
    op("dve", lambda e: e.memset(KLB.t[:], 0.0), [], [KLB])
    P.dma("sp", rbs.t[:], rb_d, writes=[rbs])
    P.dma("sp", rb31.t[:], rb31_d, writes=[rb31])
    ohd = cst["ohd"]
    mm(bank[0].t[0:8, 0:384], rbs.t[:], ohd.t[:], True, True, [rbs, ohd], [bank[0]])
    ts(vs_ap, bank[0].t[0:8, 0:384], rb31.t[:, 0:1], ALU.subtract, [bank[0], rb31], [vs])
    tt(vs_ap, vs_ap, cst["negrow"].t[:], ALU.add, [vs, cst["negrow"]], [vs])
    P.dma("sp", vd_d, vs_ap, reads=[vs], writes=[vdB])
    P.alias(itmp, [vs])
    P.dma("sp", _split(hres.t[:, :], 8, 128), bass.AP(vd_d.tensor, 0, [[1, 128], [384, 8], [1, 128]]),
          reads=[vdB], writes=[hres])
    P.dma("sp", _resplit(TT.t[:, 0:2, :], 8, 128),
          bass.AP(vd_d.tensor, 128, [[1, 128], [384, 8], [1, 128]]), reads=[vdB], writes=[Tb[0], Tb[1]])
    for half in range(2):
        mm(bank[half].t[:, :], j128.t[:], hres.t[:, half * 512:(half + 1) * 512], True, True, [j128, hres], [bank[half]])
        act(B0.t[:, 0, half * 4:(half + 1) * 4, :], _split(bank[half].t[:, :], 4, 128), AF.Copy, [bank[half]], [B0])
        tt(B0.t[:, 1, half * 4:(half + 1) * 4, :], _split(bank[half].t[:, :], 4, 128), B0.t[:, 0, half * 4:(half + 1) * 4, :],
           ALU.subtract, [bank[half], B0], [B0])
    for half in range(2):
        mm(bank[half].t[:, :], j128.t[:], TT.t[:, half, :], True, True, [j128, Tb[half]], [bank[half]])
        act(B1.t[:, 0, half * 4:(half + 1) * 4, :], _split(bank[half].t[:, :], 4, 128), AF.Copy, [bank[half]], [B1])
        tt(B1.t[:, 1, half * 4:(half + 1) * 4, :], _split(bank[half].t[:, :], 4, 128), B1.t[:, 0, half * 4:(half + 1) * 4, :],
           ALU.subtract, [bank[half], B1], [B1])
    cp(shiftm.t[:], cst["shiftm"].t[:], [cst["shiftm"]], [shiftm])
    P.dma("sp", TT.t[:, 2:4, :], _split(lbraw_d, 2, 512), writes=[Tb[2], Tb[3]])
    tt(Tv(4), Tv(2), Tv(3), ALU.subtract, [Tb[2], Tb[3]], [Tb[4]])
    act(Tv(4), Tv(4), AF.Exp, [Tb[4]], [Tb[4]])
    ts(Tv(4), Tv(4), 1.0, ALU.add, [Tb[4]], [Tb[4]])
    op("dve", lambda e: e.reciprocal(out=lbB.t[:], in_=Tv(4)), [Tb[4]], [lbB])
    ts(omlB.t[:], lbB.t[:], -1.0, ALU.mult, [lbB], [omlB], s2=1.0, op1=ALU.add)

    for b_ in Eb:
        P.alias(b_, [cst["ohd"], cst["negrow"], cst["j128"], cst["shiftm"]])
    P.alias(itmp, [identf, vs])
    ck("prologue")
    SCALE_Q = 0.125
    SCALE_I = 1.0 / 16.0
    SCALE_H = 128.0 ** -0.5

    castn = [0]

    def cast(out, in_, reads, writes):
        e = ("pool", "dve", "act")[castn[0] % 3]
        castn[0] += 1
        if e == "act":
            act(out, in_, AF.Copy, reads, writes)
        else:
            cp(out, in_, reads, writes, eng=e)

    def pc_ap(h, R, n=129):
        return bank[5 + h // 3].t[0:R, (h % 3) * 129:(h % 3) * 129 + n]

    for l in layers:
        for b_ in stgB:
            P.alias(b_, [sidx])
        HW = NCOL // 2
        n = 0
        for kc in range(8):
            for half in range(2):
                st = stgB[n % 2]
                so = (n % 2) * 2048
                n += 1
                P.dma("sp", sidx.t[:, so:so + HW], win_d[l, kc * 128:(kc + 1) * 128, half * HW:(half + 1) * HW], writes=[st])
                cast(win.t[:, kc, half * HW:(half + 1) * HW], sidx.t[:, so:so + HW], [st], [winB[kc]])
        for j in range(8):
            st = stgB[n % 2]
            so = (n % 2) * 2048
            n += 1
            P.dma("sp", sidx.t[:, so:so + D], wout_d[l, j * 128:(j + 1) * 128, :], writes=[st])
            cast(wout.t[:, j, :], sidx.t[:, so:so + D], [st], [woutB[j]])
        st = stgB[n % 2]; so = (n % 2) * 2048; n += 1
        P.dma("sp", sidx.t[:, so:so + 1024], wuk_d[l], writes=[st])
        cast(wuk.t[:, :, :], _split(sidx.t[:, so:so + 1024], 8, 128), [st], [wuk])
        st = stgB[n % 2]; so = (n % 2) * 2048; n += 1
        P.dma("sp", sidx.t[:, so:so + 1024], wuv_d[l], writes=[st])
        cast(wuv.t[:, :, :], _split(sidx.t[:, so:so + 1024], 8, 128), [st], [wuv])
        P.alias(sidx, stgB)
        P.dma("sp", gkvB.t[:], gkv_d[l], writes=[gkvB])
        P.dma("sp", ghnB.t[:], ghn_d[l], writes=[ghnB])
        P.dma("sp", lnG.t[:], lng_d[l], writes=[lnG])
        P.dma("sp", lnBt.t[:], lnb_d[l], writes=[lnBt])
        op("dve", lambda e: e.memset(Sf.t[:], 0.0), [], [Sf])
        op("dve", lambda e: e.memset(SbE.t[:], 0.0), [], [SbE])
        last_layer = (l == final_layer)
        ck("weights%d" % l)

        def tile_gen(i, l=l, last_layer=last_layer):
            R = NMETA if i == 0 else 128
            RA = min(R, 64)
            qb = i - 1
            need_out = not (last_layer and i == 0)
            hres, qlT, gaT, arT = hresP[i % 2], qlTP[i % 2], gaTP[i % 2], arTP[i % 2]
            if l == 0:
                src, srcB = (meta_d if i == 0 else x_d[qb * 128:(qb + 1) * 128, :]), []
            else:
                src, srcB = h1_d[i * 128:i * 128 + R, :], [h1B[i]]
            P.dma("pool", hb.t[0:R, :], src, reads=srcB, writes=[hb])
            yield "A0"
            for kc in range(8):
                tr(tbv(kc * 128, R), hb.t[0:R, kc * 128:(kc + 1) * 128], R, [hb], [tbB])
            act(hT.t[:, :, 0:R], _split(tb[:, :], 8, 128)[:, :, 0:R], AF.Copy, [tbB], [hT])

            ck("l%dt%d_load" % (l, i))
            def fm_group(bk, col0, nchunk):
                for j in range(nchunk):
                    for kc in range(8):
                        mm(bk.t[:, j * 128:j * 128 + R], win.t[:, kc, col0 + j * 128:col0 + (j + 1) * 128],
                           hT.t[:, kc, 0:R], kc == 0, kc == 7, [winB[kc], hT], [bk])

            def tm_group(bk, col0, ncol):
                for kc in range(8):
                    mm(bk.t[0:R, 0:ncol], hT.t[:, kc, 0:R], win.t[:, kc, col0:col0 + ncol], kc == 0, kc == 7,
                       [winB[kc], hT], [bk])

            yield "A"
            b0, b1 = bank[0], bank[1]
            bq = bank[0]
            fm_group(bq, FQ, 4)
            act(qT.t[:, :, 0:R], _split(bq.t[:, :], 4, 128)[:, :, 0:R], AF.Copy, [bq], [qT], scale=SCALE_Q)
            yield "A"
            bq = bank[1]
            fm_group(bq, FK, 1)
            act(kiT.t[:, i * 128:i * 128 + R], bq.t[:, 0:R], AF.Copy, [bq], [kiTB[i]])
            yield "A"
            bq = bank[3]
            fm_group(bq, FG, 4)
            g4v = _split(bq.t[:, :], 4, 128)[:, :, 0:R]
            t4v = _split(TT.t[:, 4, :], 4, 128)[:, :, 0:R]
            sigm(t4v, g4v, [bq], Tb[4])
            tt(gaT.t[:, :, 0:R], g4v, t4v, ALU.mult, [bq, Tb[4]], [gaT])
            yield "A"
            bq = bank[4]
            tm_group(bq, TC, NTC)
            act(craw.t[0:R, :], bq.t[0:R, 0:NTC], AF.Copy, [bq], [craw])
            yield "A"
            bq = bank[5]
            tm_group(bq, TQ, 512)
            sigm(Tv(0, R), bq.t[0:R, :], [bq], Tb[0])
            stt(Tv(0, R), bq.t[0:R, :], SCALE_H, Tv(0, R), ALU.mult, ALU.mult, [bq, Tb[0]], [Tb[0]])
            yield "A"
            b1 = bank[6]
            tm_group(b1, TF, 512)
            act(Tv(1, R), b1.t[0:R, :], AF.Exp, [b1], [Tb[1]], scale=-1.0)
            act(Tv(2, R), Tv(1, R), AF.Ln, [Tb[1]], [Tb[2]], bias=1.0)
            act(Tv(1, R), Tv(2, R), AF.Exp, [Tb[2]], [Tb[1]], scale=-1.0)
            gs = -1.0
            if l > 0:
                gs = 1.0
                tt(Tv(1, R), Tv(1, R), omlB.t[0:R, :], ALU.mult, [Tb[1], omlB], [Tb[1]])
                tt(Tv(1, R), Tv(1, R), lbB.t[0:R, :], ALU.add, [Tb[1], lbB], [Tb[1]])
                act(Tv(2, R), Tv(1, R), AF.Ln, [Tb[1]], [Tb[2]])
            ts(Tv(1, R), Tv(1, R), -1.0, ALU.mult, [Tb[1]], [Tb[1]], s2=1.0, op1=ALU.add)
            yield "A"
            bq = bank[7]
            tm_group(bq, TI, 512)
            act(Vb.t[0:R, :], bq.t[0:R, :], AF.Copy, [bq], [Vb])
            yield "A"
            b1 = bank[0]
            tm_group(b1, TG, 512)
            sigm(Tv(3, R), b1.t[0:R, :], [b1], Tb[3])
            tt(Tv(3, R), b1.t[0:R, :], Tv(3, R), ALU.mult, [b1, Tb[3]], [Tb[3]])
            tt(Tv(3, R), Tv(3, R), ghnB.t[0:R, :], ALU.mult, [Tb[3], ghnB], [Tb[3]])

            b0, b1 = bank[0], bank[1]
            P.dma("sp", hres.t[0:R, :], src, reads=srcB, writes=[hres])
            yield "A"
            ck("l%dt%d_inproj" % (l, i))
            act(Tv(4, R)[:, 0:128], craw.t[0:R, 0:128], AF.Square, [craw], [Tb[4], smb("cs")], accum=smv("cs", R))
            rsqrt_small(("crs", 0, 1), ("cs", 0, 1), R, 1.0 / 128.0, EPS)
            stt(caug.t[0:R, i, 0:128], craw.t[0:R, 0:128], smv("crs", R), gkvB.t[0:R, :], ALU.mult, ALU.mult,
                [craw, smb("crs"), gkvB], [caugB[i]])
            tr(tbv(0, R), caug.t[0:R, i, 0:128], R, [caugB[i]], [tbB])
            ck("l%dt%d_c0" % (l, i))
            cp(cT.t[:, i * 128:i * 128 + R], tbv(0, R), [tbB], [cTB[i]])
            ck("l%dt%d_c" % (l, i))
            if i >= 1:
                wv = craw.t[0:R, 128:132]
                ts(smv("wsgn", R), wv, 0.0, ALU.is_ge, [craw], [smb("wsgn")], s2=2.0, op1=ALU.mult)
                ts(smv("wsgn", R), smv("wsgn", R), -1.0, ALU.add, [smb("wsgn")], [smb("wsgn")])
                tt(smv("wabs", R), wv, smv("wsgn", R), ALU.mult, [craw, smb("wsgn")], [smb("wabs")])
                ts(smv("wabs", R), smv("wabs", R), SCALE_I, ALU.mult, [smb("wabs")], [smb("wabs")])
                wb = smv("wabs", R)
                wbc = bass.AP(wb.tensor, wb.offset, [list(wb.ap[0]), [1, 4], [0, 64]])
                tt(_split(qis.t[0:R, :], 4, 64), _split(craw.t[0:R, 132:388], 4, 64), wbc, ALU.mult,
                   [craw, smb("wabs")], [qis])
                for j in range(2):
                    tr(tbv(j * 128, R), qis.t[0:R, j * 128:(j + 1) * 128], R, [qis], [tbB])
                for h in range(4):
                    pb = (h % 2) * 64
                    cp(qiT.t[pb:pb + 64, h, 0:R], tb[pb:pb + 64, (h // 2) * 128:(h // 2) * 128 + R], [tbB], [qiT])
            for h in range(8):
                bk = bank[3 + h // 4]
                mm(bk.t[:, (h % 4) * 128:(h % 4) * 128 + R], wuk.t[:, h, :], qT.t[:, h // 2, 0:R],
                   True, True, [wuk, qT], [bk])
            ck("l%dt%d_ql" % (l, i))
            act(qlT.t[:, 0:4, 0:R], _split(bank[3].t[:, :], 4, 128)[:, :, 0:R], AF.Copy, [bank[3]], [qlT])
            cp(qlT.t[:, 4:8, 0:R], _split(bank[4].t[:, :], 4, 128)[:, :, 0:R], [bank[4]], [qlT])

            yield "A"
            ck("l%dt%d_prep" % (l, i))
            mm(b0.t[0:R, :], m1t.t[0:R, 0:R], Tv(2, R), True, True, [m1t, Tb[2]], [b0])
            mm(b1.t[0:R, :], m3t.t[0:R, 0:R], Tv(2, R), True, True, [m3t, Tb[2]], [b1])
            for h in range(4):
                mm(bank[7].t[:, 2 * h:2 * h + 2], TT.t[0:R, 2, h * 128:(h + 1) * 128], sel.t[0:R, :], True, True,
                   [Tb[2], sel], [bank[7]])
            act(smv("dS"), bank[7].t[:, 0:8], AF.Exp, [bank[7]], [smb("dS")], scale=gs)
            act(Tv(4, R), b0.t[0:R, :], AF.Exp, [b0], [Tb[4]], scale=gs)
            tt(QD.t[0:R, :], Tv(0, R), Tv(4, R), ALU.mult, [Tb[0], Tb[4]], [QD])
            act(Tv(4, R), b0.t[0:R, :], AF.Exp, [b0], [Tb[4]], scale=-gs)
            tt(KD.t[0:R, :], Tv(1, R), Tv(4, R), ALU.mult, [Tb[1], Tb[4]], [KD])
            act(Tv(4, R), b1.t[0:R, :], AF.Exp, [b1], [Tb[4]], scale=gs)
            tt(KL.t[0:RA, :], Tv(1, RA), Tv(4, RA), ALU.mult, [Tb[1], Tb[4]], [KL])
            if R == 128:
                tt(KLB.t[64:128, :], TT.t[64:128, 1, :], TT.t[64:128, 4, :], ALU.mult, [Tb[1], Tb[4]], [KLB])
            yield "A"
            for h in range(4):
                tr(tbv(h * 128, R), QD.t[0:R, h * 128:(h + 1) * 128], R, [QD], [tbB])
            for h in range(4):
                tr(tbv(512 + h * 128, R), KD.t[0:R, h * 128:(h + 1) * 128], R, [KD], [tbB])
            t8 = _split(tb[:, :], 8, 128)
            act(QDTA.t[:, :, 0:RA], t8[:, 0:4, 0:RA], AF.Copy, [tbB], [QDTA])
            if R == 128:
                cp(QDTB.t[:, :, 64:128], t8[:, 0:4, 64:128], [tbB], [QDTB])
            act(KDT.t[:, :, 0:R], t8[:, 4:8, 0:R], AF.Copy, [tbB], [KDT])
            yield "A"
            b3, b4 = bank[3], bank[4]
            for h in range(4):
                mm(b3.t[0:R, h * 128:h * 128 + RA], KDT.t[:, h, 0:R], QDTA.t[:, h, 0:RA], True, True, [KDT, QDTA], [b3])
                if R == 128:
                    mm(b3.t[0:R, h * 128 + 64:h * 128 + 128], KDT.t[:, h, 0:R], QDTB.t[:, h, 64:128], True, True,
                       [KDT, QDTB], [b3])
            tt(SC.t[0:R, :, 0:R], _split(b3.t[0:R, :], 4, 128)[:, :, 0:R], _mid_bc(cmask.t[0:R, 0:R], 4), ALU.mult,
               [b3, cmask], [SC])
            for h in range(4):
                hc = slice(h * 128, (h + 1) * 128)
                mm(bank[5].t[:, hc], KL.t[0:RA, hc], Vb.t[0:RA, hc], True, True, [KL, Vb], [bank[5]])
                if R == 128:
                    mm(bank[6].t[:, hc], KLB.t[:, hc], Vb.t[:, hc], True, True, [KLB, Vb], [bank[6]])
            yield "A"
            for h in range(4):
                hc = slice(h * 128, (h + 1) * 128)
                mm(b4.t[0:R, hc], SC.t[0:R, h, 0:R], Vb.t[0:R, hc], h == 0, False, [SC, Vb], [b4], sg=True)
                mm(b4.t[0:R, hc], QDTA.t[:, h, 0:R], SbE.t[:, h, :], False, R < 128, [QDTA, SbE], [b4], sg=True)
            for h in range(4):
                hc = slice(h * 128, (h + 1) * 128)
                stt(Sf.t[:, h, :], Sf.t[:, h, :], smv("dS", 128, 2 * h, 2 * h + 1), bank[5].t[:, hc], ALU.mult, ALU.add,
                    [Sf, smb("dS"), bank[5]], [Sf])
            if R == 128:
                act(SbM.t[:], Sf.t[:], AF.Copy, [Sf], [SbM])
                for h in range(4):
                    hc = slice(h * 128, (h + 1) * 128)
                    mm(b4.t[0:R, hc], QDTB.t[:, h, :], SbM.t[:, h, :], False, True, [QDTB, SbM], [b4], sg=True)
                for h in range(4):
                    hc = slice(h * 128, (h + 1) * 128)
                    stt(Sf.t[:, h, :], Sf.t[:, h, :], smv("dS", 128, 2 * h + 1, 2 * h + 2), bank[6].t[:, hc], ALU.mult,
                        ALU.add, [Sf, smb("dS"), bank[6]], [Sf])
            act(SbE.t[:], Sf.t[:], AF.Copy, [Sf], [SbE])
            yield "A"
            if need_out:
                for h in range(4):
                    hc = slice(h * 128, (h + 1) * 128)
                    act(Tv(4, R)[:, hc], b4.t[0:R, hc], AF.Square, [b4], [Tb[4], smb("ssq4")],
                        accum=smv("ssq4", R, h, h + 1))
                rsqrt_small(("rs4", 0, 4), ("ssq4", 0, 4), R, 1.0 / 128.0, EPS)
                for h in range(4):
                    hc = slice(h * 128, (h + 1) * 128)
                    stt(Rb.t[0:R, hc], b4.t[0:R, hc], smv("rs4", R, h, h + 1), Tv(3, R)[:, hc], ALU.mult, ALU.mult,
                        [b4, smb("rs4"), Tb[3]], [Rb])
                for h in range(4):
                    tr(tbv(h * 128, R), Rb.t[0:R, h * 128:(h + 1) * 128], R, [Rb], [tbB])
                act(arT.t[:, 4:8, 0:R], t8[:, 0:4, 0:R], AF.Copy, [tbB], [arT])
            if taps and i == taps.get("_tile", 1) and l == taps.get("_layer", 0):
                tap("hgrn_o", b4, b4.t[0:R, :], [R, 512])
                tap("caug", caugB[i], caug.t[0:R, i, :], [R, 130])
                tap("qlT", qlT, qlT.t[:, :, :], [128, 8, 128])
                tap("kiT", kiTB[i], kiT.t[:, i * 128:(i + 1) * 128], [128, 128])
                tap("qiT", qiT, qiT.t[:, :, :], [128, 4, 128])
                tap("craw", craw, craw.t[:, :], [128, NTC])
                tap("gaT", gaT, gaT.t[:, :, :], [128, 4, 128])

            ck("l%dt%d_hgrn" % (l, i))
            yield "A_done"
            if not need_out:
                return
            use_thr = (i >= 1) and ((qb + 1) * 128 > ktop)
            if use_thr:
                nk = (qb + 1) * 128
                for b_ in sidxB:
                    b_.last_w = None
                    b_.readers = {}
                    P.alias(b_, [sidx])
                for kb0 in range(0, qb + 1, 2):
                    kbs = [kb for kb in (kb0, kb0 + 1) if kb <= qb]
                    for kb in kbs:
                        bkI = bank[kb % 2]
                        for h in range(4):
                            mm(bkI.t[:, h * 128:(h + 1) * 128], qiT.t[:, h, :],
                               kiT.t[:, (kb + 1) * 128:(kb + 2) * 128], True, True, [qiT, kiTB[kb + 1]], [bkI])
                    for kb in kbs:
                        bkI = bank[kb % 2]
                        act(itmpP[kb % 2].t[:, :, :], _split(bkI.t[:, :], 4, 128), AF.Relu, [bkI], [itmpP[kb % 2]])
                    for h in range(4):
                        for kb in kbs:
                            it_ = itmpP[kb % 2]
                            sv = sidx.t[:, kb * 128:(kb + 1) * 128]
                            sxb = sidxB[kb % 2]
                            if h == 0:
                                ts(sv, it_.t[:, 0, :], smv("wsgn", 128, 0, 1), ALU.mult, [it_, smb("wsgn")], [sxb])
                            else:
                                stt(sv, it_.t[:, h, :], smv("wsgn", 128, h, h + 1), sv, ALU.mult, ALU.add,
                                    [it_, smb("wsgn"), sxb], [sxb])
                    yield "idx"
                lw = [b_.last_w for b_ in sidxB if b_.last_w is not None]
                assert all(c[0] is lw[0][0] for c in lw)
                sidx.last_w = max(lw, key=lambda c: c[1])
                sidx.readers = {}
                yield "idx_done"
                sa = sidx.t[:, 0:nk]
                op("dve", lambda e, a=sa: e.tensor_reduce(out=smv("mx"), in_=a, axis=AX.X, op=ALU.max), [sidx], [smb("mx")])
                op("dve", lambda e, a=sa: e.tensor_reduce(out=smv("lo"), in_=a, axis=AX.X, op=ALU.min), [sidx], [smb("lo")])
                tt(smv("step"), smv("mx"), smv("lo"), ALU.subtract, [smb("mx"), smb("lo")], [smb("step")])
                dv = sidx.t[:, qb * 128:(qb + 1) * 128]
                tt(dv, dv, causneg.t[:], ALU.add, [sidx, causneg], [sidx])
                cD = int(nk * BIS_DVE_FRAC)
                thr_c = float(ktop) - 0.5 - (nk - cD) / 2.0
                stt(smv("mid"), smv("step"), 0.5, smv("lo"), ALU.mult, ALU.add, [smb("step"), smb("lo")], [smb("mid")])
                op("dve", lambda e, v_=thr_c: e.memset(smv("thrc"), v_), [], [smb("thrc")])
                for k in range(1, nit + 1):
                    f = 2.0 ** (-k)
                    act(_zs(jkA.t[:, 0:1], nk - cD), sidx.t[:, cD:nk], AF.Sign, [sidx, smb("mid")], [jkA, smb("sg")],
                        scale=-1.0, bias=smv("mid"), accum=smv("sg"))
                    ts(_zs(jkD.t[:, 0:1], cD), sidx.t[:, 0:cD], smv("mid"), ALU.is_ge, [sidx, smb("mid")], [jkD, smb("cnt")],
                       s2=0.0, op1=ALU.add, accum=smv("cnt"))
                    act(smv("thrA"), smv("sg"), AF.Identity, [smb("sg"), smb("thrc")], [smb("thrA")], scale=0.5, bias=smv("thrc"))
                    stt(smv("base"), smv("step"), -0.5 * f, smv("mid"), ALU.mult, ALU.add, [smb("step"), smb("mid")],
                        [smb("base")])
                    ts(smv("fl"), smv("cnt"), smv("thrA"), ALU.is_ge, [smb("cnt"), smb("thrA")], [smb("fl")], s2=f, op1=ALU.mult)
                    stt(smv("mid"), smv("fl"), smv("step"), smv("base"), ALU.mult, ALU.add,
                        [smb("fl"), smb("step"), smb("base")], [smb("mid")])
                    yield "bis"
                stt(smv("lo"), smv("step"), -(2.0 ** (-nit - 1)), smv("mid"), ALU.mult, ALU.add, [smb("step"), smb("mid")],
                    [smb("lo")])
                ts(mball.t[:, 0:nk], sa, smv("lo"), ALU.is_lt, [sidx, smb("lo")], [mball])
            else:
                yield "idx_done"
            yield "bis_done"
            ck("l%dt%d_thr" % (l, i))
            if i == 0:
                kblocks = [(0, NMETA, (identb.t[0:NMETA, 0:NMETA], B0, NMETA), False, None)]
            else:
                if qb == 0:
                    kblocks = [(0, NMETA, (shiftm.t[:, :], B1, 128), False, None)]
                else:
                    kblocks = [(0, NMETA, None, False, None)]
                for kb in range(qb + 1):
                    if kb == qb:
                        bias = (identb.t[:, :], B0, 128)
                    elif kb == qb - 1:
                        bias = (identb.t[:, :], B1, 128)
                    else:
                        bias = None
                    kblocks.append((kb + 1, 128, bias, use_thr, kb))
            nblk = len(kblocks)
            def emit_pc(bi_, kt_, KR_, E_):
                for h in range(8):
                    mm(pc_ap(h, R), E_.t[0:KR_, h, 0:R], caug.t[0:KR_, kt_, 0:129], bi_ == 0 and h % 3 == 0, bi_ == nblk - 1,
                       [E_, caugB[kt_]], [bank[5 + h // 3]], sg=True)

            pend = None
            for bi, (kt, KR, bias, masked, kb) in enumerate(kblocks):
                E = Eb[bi % 2]
                if masked:
                    mb = maskb[bi % 2]
                    act(mb.t[:, :], mball.t[:, kb * 128:(kb + 1) * 128], AF.Copy, [mball], [mb], scale=NEG)
                for half in range(2):
                    bk = bank[3 + half]
                    lgv = _split(bk.t[0:KR, 0:4 * R], 4, R)
                    nacc = 1 + (2 if bias is not None else 0) + (1 if masked else 0)
                    na = [0]

                    def acc(lhsT, rhs, rd):
                        na[0] += 1
                        mm(bk.t[0:KR, 0:4 * R], lhsT, rhs, na[0] == 1, na[0] == nacc, rd, [bk])
                    acc(cT.t[:, kt * 128:kt * 128 + KR], qlT.t[:, 4 * half:4 * half + 4, 0:R], [cTB[kt], qlT])
                    if bias is not None:
                        bl, bt, bk_rows = bias
                        for hl in range(2):
                            acc(bl, bt.t[0:bk_rows, hl, 4 * half:4 * half + 4, 0:R], [bt, identb, shiftm])
                    if masked:
                        acc(mb.t[:, :], _mid_bc(identb.t[:, :], 4), [mb, identb])
                    act(E.t[0:KR, 4 * half:4 * half + 4, 0:R], lgv, AF.Exp, [bk], [E])
                if pend is not None:
                    emit_pc(*pend)
                pend = (bi, kt, KR, E)
                yield "post"
            emit_pc(*pend)
            for g in range(3):
                nh = 3 if g < 2 else 2
                dn = bank[5 + g].t[0:R, 0:nh * 129]
                dnv = bass.AP(dn.tensor, dn.offset + 128, [list(dn.ap[0]), [129, nh]])
                op("dve", lambda e, a=dnv, o_=smv("rden", R, 3 * g, 3 * g + nh): e.reciprocal(out=o_, in_=a),
                   [bank[5 + g]], [smb("rden")])
            for h in range(8):
                if h % 2 == 0:
                    act(olat.t[0:R, h, :], pc_ap(h, R, 128), AF.Copy, [bank[5 + h // 3], smb("rden")], [olat],
                        scale=smv("rden", R, h, h + 1))
                else:
                    ts(olat.t[0:R, h, :], pc_ap(h, R, 128), smv("rden", R, h, h + 1), ALU.mult,
                       [bank[5 + h // 3], smb("rden")], [olat])
            for h in range(8):
                tr(tbv(h * 128, R), olat.t[0:R, h, :], R, [olat], [tbB])
            act(olatT.t[:, :, 0:R], t8[:, :, 0:R], AF.Copy, [tbB], [olatT])
            yield "post"
            for j in range(4):
                mm(b0.t[:, j * 128:j * 128 + R], wuv.t[:, 2 * j, :], olatT.t[:, 2 * j, 0:R], True, False, [wuv, olatT], [b0])
                mm(b0.t[:, j * 128:j * 128 + R], wuv.t[:, 2 * j + 1, :], olatT.t[:, 2 * j + 1, 0:R], False, True,
                   [wuv, olatT], [b0])
            tt(arT.t[:, 0:4, 0:R], _split(b0.t[:, :], 4, 128)[:, :, 0:R], gaT.t[:, :, 0:R], ALU.mult, [b0, gaT], [arT])

            yield "post"
            ck("l%dt%d_attn" % (l, i))
            for half in range(2):
                bk = bank[half]
                for j in range(8):
                    mm(bk.t[0:R, :], arT.t[:, j, 0:R], wout.t[:, j, half * 512:(half + 1) * 512], j == 0, j == 7,
                       [arT, woutB[j]], [bk])
            for half in range(2):
                hv = hres.t[0:R, half * 512:(half + 1) * 512]
                stt(hv, hv, DN_ALPHA, bank[half].t[0:R, :], ALU.mult, ALU.add, [hres, bank[half]], [hres])
            stB = Buf("stats_", stats.t)
            for half in range(2):
                op("dve", lambda e, hf=half: e.bn_stats(out=stats.t[0:R, hf * 6:(hf + 1) * 6],
                                                        in_=hres.t[0:R, hf * 512:(hf + 1) * 512]), [hres], [stB])
            op("dve", lambda e: e.bn_aggr(out=smv("mv", R), in_=stats.t[0:R, :]), [stB], [smb("mv")])
            ts(smv("lnr", R), smv("mv", R, 1, 2), EPS, ALU.add, [smb("mv")], [smb("lnr")])
            act(smv("lnr", R), smv("lnr", R), AF.Ln, [smb("lnr")], [smb("lnr")])
            act(smv("lnr", R), smv("lnr", R), AF.Exp, [smb("lnr")], [smb("lnr")], scale=-0.5)
            ts(smv("lnm", R), smv("mv", R, 0, 1), -1.0, ALU.mult, [smb("mv"), smb("lnr")], [smb("lnm")],
               s2=smv("lnr", R), op1=ALU.mult)
            act(hres.t[0:R, :], hres.t[0:R, :], AF.Identity, [hres, smb("lnr"), smb("lnm")], [hres],
                scale=smv("lnr", R), bias=smv("lnm", R))
            tt(hres.t[0:R, :], hres.t[0:R, :], lnG.t[0:R, :], ALU.mult, [hres, lnG], [hres])
            tt(hres.t[0:R, :], hres.t[0:R, :], lnBt.t[0:R, :], ALU.add, [hres, lnBt], [hres])
            if last_layer:
                P.dma("pool", out_d[qb * 128:(qb + 1) * 128, :], hres.t[0:R, :], reads=[hres], is_output=True)
            else:
                P.dma("pool", h1_d[i * 128:i * 128 + R, :], hres.t[0:R, :], reads=[hres], writes=[h1B[i]])
        def step(g):
            try:
                return next(g)
            except StopIteration:
                return None

        def run_to(g, marker):
            while True:
                m = step(g)
                if m is None or m == marker:
                    return m

        gens = [tile_gen(i) for i in range(NTT)]
        run_to(gens[0], "A_done")
        run_to(gens[0], "bis_done")
        if NTT > 1:
            run_to(gens[1], "A_done")
        for i in range(NTT):
            s1 = gens[i]
            s2 = gens[i + 1] if i + 1 < NTT else None
            s3 = gens[i + 2] if i + 2 < NTT else None
            l1, l2, l3 = True, s2 is not None, s3 is not None
            if l3:
                step(s3)
            while l1 or l2 or l3:
                if l2:
                    m = step(s2)
                    if m is None or m == "bis_done":
                        l2 = False
                if l1:
                    for _ in range(SCHED_POST_STEPS):
                        if step(s1) is None:
                            l1 = False
                            break
                elif l3:
                    m = step(s3)
                    if m is None or m == "A_done":
                        l3 = False
    P.finish()
    return P, tap_out


def prep_shared(inp):
    w_in = np.asarray(inp["w_in"], np.float32)
    o = {"q": (0, 512), "c": (512, 640), "qi": (640, 896), "ki": (896, 960), "wi": (960, 964), "ga": (964, 1476),
         "qh": (1476, 1988), "fh": (1988, 2500), "ih": (2500, 3012), "gh": (3012, 3524)}
    order = ["q", "ki", "ki", "ga", "c", "wi", "qi", "qh", "fh", "ih", "gh"]
    w_perm = np.ascontiguousarray(np.concatenate([w_in[:, :, o[k][0]:o[k][1]] for k in order], axis=2))
    assert w_perm.shape[2] == NCOL
    w_uk = np.asarray(inp["w_uk"], np.float32)
    wuk_l = np.zeros((2, 128, 8, 128), np.float32)
    for h in range(8):
        wuk_l[:, (h % 2) * 64:(h % 2) * 64 + 64, h, :] = w_uk[:, h]
    w_uv = np.asarray(inp["w_uv"], np.float32)
    wuv_l = np.zeros((2, 128, 8, 128), np.float32)
    for h in range(8):
        wuv_l[:, :, h, (h % 2) * 64:(h % 2) * 64 + 64] = w_uv[:, h]
    bc = lambda a, n: np.ascontiguousarray(np.broadcast_to(a[:, None, :], (a.shape[0], n, a.shape[1])))
    ghn = np.tile(np.asarray(inp["hgrn_norm_g"], np.float32), (1, 4))
    rb = np.asarray(inp["rel_bias"], np.float32)
    lbraw = np.asarray(inp["hgrn_lb_raw"], np.float32).reshape(1, 1024)
    d = {
        "meta": np.ascontiguousarray(np.asarray(inp["meta_tokens"], np.float32)),
        "w_in": w_perm,
        "w_out": np.ascontiguousarray(np.asarray(inp["w_out"], np.float32)),
        "w_uk": wuk_l.reshape(2, 128, 1024),
        "w_uv": wuv_l.reshape(2, 128, 1024),
        "gkv": bc(np.asarray(inp["kv_norm_g"], np.float32), 128),
        "ghn": bc(ghn, 128),
        "lng": bc(np.asarray(inp["ln_g"], np.float32), 128),
        "lnb": bc(np.asarray(inp["ln_b"], np.float32), 128),
        "lbraw": np.ascontiguousarray(np.broadcast_to(lbraw, (128, 1024))),
        "rb": np.ascontiguousarray(rb),
        "rb31": np.ascontiguousarray(rb[31].reshape(8, 1)),
    }
    d.update(host_constants())
    return d


NT_FULL = 32
NIT = 18
BIS_DVE_FRAC = 0.45
SCHED_POST_STEPS = 2


def kernel(**inputs):
    shared = prep_shared(inputs)
    x = np.asarray(inputs["x"], np.float32)
    B = x.shape[0]
    nc = bass.Bass("TRN2", target_bir_lowering=False)
    build_program(nc, NT_FULL, layers=(0, 1), ktop=256, nit=NIT, final_layer=1)
    in_maps = []
    for b in range(B):
        m = dict(shared)
        m["x"] = np.ascontiguousarray(x[b])
        in_maps.append(m)
    res = run_bass_kernel_spmd(nc, in_maps, core_ids=list(range(B)))
    out = np.stack([np.asarray(r["out"], np.float32) for r in res.results], axis=0)
    return out
```

```python
from contextlib import ExitStack
import numpy as np
import concourse.bass as bass
import concourse.mybir as mybir
from concourse.bass_utils import run_bass_kernel_spmd

F32 = mybir.dt.float32
BF16 = mybir.dt.bfloat16
U8 = mybir.dt.uint8
AF = mybir.ActivationFunctionType
ALU = mybir.AluOpType
AX = mybir.AxisListType

NDS = 16


class _Sem:
    def __init__(self, h, name):
        self.h = h
        self.name = name


class Buf:
    def __init__(self, name, t=None):
        self.name = name
        self.t = t
        self.last_w = None
        self.readers = {}


class _Eng:
    def __init__(self, name, sem):
        self.name = name
        self.sem = sem
        self.n = 0
        self.seen = {}
        self.q = []
        self.ndma = 0
        self.dsems = []


class Prog:
    ENG = ("pe", "act", "dve", "pool", "sp")

    def __init__(self, nc):
        self.nc = nc
        self.es = ExitStack()
        self.eng = {}
        for n in self.ENG:
            s = _Sem(self.es.enter_context(nc.semaphore("s_" + n)), n)
            self.eng[n] = _Eng(n, s)
        for n in ("sp", "pool", "act"):
            self.eng[n].dsems = [_Sem(self.es.enter_context(nc.semaphore("d_%s_%d" % (n, i))), "d%s%d" % (n, i))
                                 for i in range(NDS)]
        self.out_clocks = []
        self.nwait = 0

    def sb(self, name, shape, dt):
        return Buf(name, self.es.enter_context(self.nc.sbuf_tensor("s_" + name, list(shape), dt)))

    def ps(self, name, shape, dt):
        return Buf(name, self.es.enter_context(self.nc.psum_tensor("p_" + name, list(shape), dt)))

    def view(self, name, t):
        return Buf(name, t)

    def _need(self, E, clock, strict_same):
        sem, val = clock
        if sem is E.sem and not strict_same:
            return
        if E.seen.get(sem, 0) >= val:
            return
        E.seen[sem] = val
        E.q.append(("w", sem, val))
        self.nwait += 1

    def _deps(self, E, reads, writes, is_dma=False):
        strict = is_dma or E.name != "pe"
        for b in reads:
            if b.last_w is not None:
                self._need(E, b.last_w, True)
        for b in writes:
            if b.last_w is not None:
                self._need(E, b.last_w, strict)
            for s, v in b.readers.items():
                self._need(E, (s, v), strict)

    def _mark(self, clock, reads, writes):
        sem, val = clock
        for b in writes:
            b.last_w = clock
            b.readers = {}
        for b in reads:
            if b.readers.get(sem, 0) < val:
                b.readers[sem] = val

    def op(self, eng, fn, reads=(), writes=()):
        E = self.eng[eng]
        self._deps(E, reads, writes)
        E.n += 1
        E.q.append(("o", fn))
        self._mark((E.sem, E.n), reads, writes)

    def dma(self, queue, out, in_, reads=(), writes=(), is_output=False):
        E = self.eng[queue]
        j = E.ndma
        E.ndma += 1
        ds = E.dsems[j % NDS]
        prev = 16 * (j // NDS)
        if prev > 0:
            self._need(E, (ds, prev), True)
        self._deps(E, reads, writes, is_dma=True)
        E.q.append(("d", out, in_, ds))
        clock = (ds, prev + 16)
        self._mark(clock, reads, writes)
        if is_output:
            self.out_clocks.append(clock)
        return clock

    def finish(self, final_eng="sp"):
        E = self.eng[final_eng]
        last = {}
        for s, v in self.out_clocks:
            if last.get(s, (None, 0))[1] < v:
                last[s] = (s, v)
        for s, v in last.values():
            E.q.append(("w", s, v))
        nc = self.nc
        engs = self.eng

        def replay(E, e):
            for it in E.q:
                k = it[0]
                if k == "w":
                    e.wait_ge(it[1].h, it[2])
                elif k == "o":
                    it[1](e).then_inc(E.sem.h, 1)
                else:
                    e.dma_start(out=it[1], in_=it[2]).then_inc(it[3].h, 16)

        with nc.Block() as block:
            @block.tensor
            def _(e):
                replay(engs["pe"], e)

            @block.scalar
            def _(e):
                replay(engs["act"], e)

            @block.vector
            def _(e):
                replay(engs["dve"], e)

            @block.gpsimd
            def _(e):
                replay(engs["pool"], e)

            @block.sync
            def _(e):
                replay(engs["sp"], e)
        self.es.close()

    def alias(self, dst, srcs):
        for s in srcs:
            if s.last_w is not None:
                c = s.last_w
                if dst.readers.get(c[0], 0) < c[1]:
                    dst.readers[c[0]] = c[1]
            for k, v in s.readers.items():
                if dst.readers.get(k, 0) < v:
                    dst.readers[k] = v


D = 1024
NMETA = 16
FQ, FK, FG, TC, TQ, TF, TI, TG = 0, 512, 640, 1152, 1540, 2052, 2564, 3076
NTC = 388
NCOL = 3588
DN_ALPHA = 4.0 ** 0.25
EPS = 1e-6
NEG = -30000.0


def _mid_bc(ap, n):
    a = [list(x) for x in ap.ap]
    return bass.AP(ap.tensor, ap.offset, [a[0], [0, n]] + a[1:])


def host_constants():
    c = {}
    idx = np.arange(128)
    c["identf"] = np.eye(128, dtype=np.float32)
    same = (idx[:, None] // 64) == (idx[None, :] // 64)
    c["m1t"] = (same & (idx[:, None] <= idx[None, :])).astype(np.float32)
    c["m3t"] = (same & (idx[:, None] > idx[None, :])).astype(np.float32)
    sel = np.zeros((128, 2), np.float32)
    sel[:64, 0] = 1.0
    sel[64:, 1] = 1.0
    c["sel"] = sel
    c["causneg"] = np.where(idx[None, :] <= idx[:, None], 0.0, -1e30).astype(np.float32)
    c["cmask"] = (same & (idx[:, None] <= idx[None, :])).astype(np.float32)
    c["j128"] = np.fliplr(np.eye(128, dtype=np.float32)).copy()
    sh = np.zeros((128, 16), np.float32)
    sh[112 + np.arange(16), np.arange(16)] = 1.0
    c["shiftm"] = sh
    d = np.arange(384) - 127
    n = np.maximum(d, 0)
    nf = np.maximum(n, 1).astype(np.float32)
    large = 16 + (np.log(nf / np.float32(16)) / np.float32(np.log(128 / 16)) * np.float32(16)).astype(np.int32)
    large = np.minimum(large, 31)
    bucket = np.where(n < 16, n, large)
    oh = np.zeros((32, 384), np.float32)
    for j in range(384):
        if d[j] >= 0:
            oh[bucket[j], j] = 1.0
    c["ohd"] = oh
    c["negrow"] = np.broadcast_to(np.where(d >= 0, 0.0, NEG).astype(np.float32)[None, :], (8, 384)).copy()
    return c


CONST_SHAPES = {"identf": [128, 128], "m1t": [128, 128], "m3t": [128, 128], "sel": [128, 2],
                "causneg": [128, 128], "cmask": [128, 128], "j128": [128, 128], "shiftm": [128, 16],
                "ohd": [32, 384], "negrow": [8, 384]}


def _resplit(ap, a, b):
    return bass.AP(ap.tensor, ap.offset, [list(ap.ap[0]), [b, a], [1, b]])


def _zs(ap, n):
    return bass.AP(ap.tensor, ap.offset, [list(ap.ap[0]), [0, n]])


def _split(ap, a, b):
    p = list(ap.ap[0])
    st = ap.ap[-1][0]
    return bass.AP(ap.tensor, ap.offset, [p, [b * st, a], [st, b]])


class _StopBuild(Exception):
    pass


def build_program(nc, NT, layers=(0, 1), ktop=256, nit=24, taps=None, final_layer=1, stop_at=None):
    try:
        return _build_program(nc, NT, layers, ktop, nit, taps, final_layer, stop_at)
    except _StopBuild as e:
        P = e.args[0]
        P.finish()
        return P, {}


def _build_program(nc, NT, layers, ktop, nit, taps, final_layer, stop_at):
    NTT = NT + 1
    S = NT * 128
    P = Prog(nc)
    tap_out = {}

    def ck(name):
        if stop_at == name:
            raise _StopBuild(P)

    def dr(name, shape, kind="ExternalInput"):
        return nc.dram_tensor(name, list(shape), F32, kind=kind).ap()

    x_d = dr("x", [S, D])
    meta_d = dr("meta", [NMETA, D])
    out_d = dr("out", [S, D], "ExternalOutput")
    win_d = dr("w_in", [2, D, NCOL])
    wout_d = dr("w_out", [2, D, D])
    wuk_d = dr("w_uk", [2, 128, 1024])
    wuv_d = dr("w_uv", [2, 128, 1024])
    gkv_d = dr("gkv", [2, 128, 128])
    ghn_d = dr("ghn", [2, 128, 512])
    lng_d = dr("lng", [2, 128, D])
    lnb_d = dr("lnb", [2, 128, D])
    lbraw_d = dr("lbraw", [128, 1024])
    rb_d = dr("rb", [32, 8])
    rb31_d = dr("rb31", [8, 1])
    const_d = {k: dr(k, v) for k, v in CONST_SHAPES.items()}
    h1_d = dr("h1s", [NTT * 128, D], "Internal")
    vd_d = dr("vds", [8, 384], "Internal")
    h1B = [Buf("h1_%d" % i) for i in range(NTT)]
    vdB = Buf("vd")

    def tap(name, buf, ap, shape):
        if taps is None or name not in taps:
            return
        d = dr("tap_" + name, shape, "ExternalOutput")
        tap_out[name] = shape
        P.dma("pool", d, ap, reads=[buf], is_output=True)

    sb = P.sb
    win = sb("win", [128, 8, NCOL], BF16)
    winB = [Buf("win%d" % k, win.t) for k in range(8)]
    wout = sb("wout", [128, 8, D], BF16)
    woutB = [Buf("wout%d" % k, wout.t) for k in range(8)]
    wuk = sb("wuk", [128, 8, 128], BF16)
    wuv = sb("wuv", [128, 8, 128], BF16)
    caug = sb("caug", [128, NTT, 130], BF16)
    caugB = [Buf("caug%d" % i, caug.t) for i in range(NTT)]
    cT = sb("cT", [128, NTT * 128], BF16)
    cTB = [Buf("cT%d" % i, cT.t) for i in range(NTT)]
    kiT = sb("kiT", [128, NTT * 128], BF16)
    kiTB = [Buf("kiT%d" % i, kiT.t) for i in range(NTT)]
    B0 = sb("B0", [128, 2, 8, 128], BF16)
    B1 = sb("B1", [128, 2, 8, 128], BF16)
    shiftm = sb("shiftm", [128, 16], BF16)
    gkvB = sb("gkvB", [128, 128], F32)
    ghnB = sb("ghnB", [128, 512], F32)
    lnG = sb("lnG", [128, D], F32)
    lnBt = sb("lnB", [128, D], F32)
    lbB = sb("lbB", [128, 512], F32)
    omlB = sb("omlB", [128, 512], F32)
    identb = sb("identb", [128, 128], BF16)
    hresP = [sb("hres%d" % k, [128, D], F32) for k in range(2)]
    hres = hresP[0]
    hb = sb("hb", [128, D], BF16)
    hT = sb("hT", [128, 8, 128], BF16)
    qT = sb("qT", [128, 4, 128], BF16)
    qlTP = [sb("qlT%d" % k, [128, 8, 128], BF16) for k in range(2)]
    qlT = qlTP[0]
    qis = sb("qis", [128, 256], BF16)
    qiT = sb("qiT", [128, 4, 128], BF16)
    gaTP = [sb("gaT%d" % k, [128, 4, 128], BF16) for k in range(2)]
    gaT = gaTP[0]
    craw = sb("craw", [128, NTC], F32)
    sm = sb("sm", [128, 64], F32)
    smB = {}

    def small(name, c0, n):
        smB[name] = (Buf("sm_" + name, sm.t), c0, n)

    small("wabs", 0, 4); small("wsgn", 4, 4); small("cs", 8, 1); small("crs", 9, 1)
    small("lo", 10, 1); small("step", 11, 1); small("mid", 12, 1); small("cnt", 13, 1); small("fl", 14, 1)
    small("mx", 15, 1); small("rden", 16, 8); small("ssq4", 24, 4); small("rs4", 28, 4)
    small("sg", 36, 1); small("thrA", 37, 1); small("thrc", 38, 1); small("base", 39, 1); small("mv", 32, 2); small("lnr", 34, 1); small("lnm", 35, 1); small("dS", 40, 8)

    def smv(name, R=128, a=None, b=None):
        bf, c0, n = smB[name]
        a = 0 if a is None else a
        b = n if b is None else b
        return sm.t[0:R, c0 + a:c0 + b]

    def smb(name):
        return smB[name][0]

    stats = sb("stats", [128, 12], F32)
    sidx = sb("sidx", [128, 4096], F32)
    stgB = [Buf("stg0", sidx.t), Buf("stg1", sidx.t)]
    itmp = sb("itmp", [128, 4, 128], F32)
    itmpP = [itmp, sb("itmp2", [128, 4, 128], F32)]
    sidxB = [Buf("sidxA"), Buf("sidxB")]
    mball = sb("mball", [128, 4096], U8)
    jkD = sb("jkD", [128, 8], U8)
    jkA = sb("jkA", [128, 8], mybir.dt.int8)
    _iv = itmp.t[0:8, 0:3, :]
    vs = Buf("vs", None)
    vs_ap = bass.AP(_iv.tensor, _iv.offset, [list(_iv.ap[0]), [1, 384]])
    maskb = [sb("maskb%d" % i, [128, 128], BF16) for i in range(2)]
    E2 = sb("E2", [128, 2, 8, 128], BF16)
    Eb = [Buf("E%d" % k, E2.t[:, k]) for k in range(2)]
    _e0 = E2.t[:, 0, 0, :]
    olat = sb("olat", [128, 8, 128], BF16)
    olatT = sb("olatT", [128, 8, 128], BF16)
    arTP = [sb("arT%d" % k, [128, 8, 128], BF16) for k in range(2)]
    arT = arTP[0]
    TT = sb("TT", [128, 5, 512], F32)
    Tb = [Buf("T%d" % k, TT.t) for k in range(5)]

    def Tv(k, R=128):
        return TT.t[0:R, k, :]
    Vb = sb("Vb", [128, 512], BF16)
    QD = sb("QD", [128, 512], BF16)
    KD = sb("KD", [128, 512], BF16)
    KL = sb("KL", [128, 512], BF16)
    KLB = sb("KLB", [128, 512], BF16)
    QDTA = sb("QDTA", [128, 4, 128], BF16)
    QDTB = sb("QDTB", [128, 4, 128], BF16)
    KDT = sb("KDT", [128, 4, 128], BF16)
    SC = sb("SC", [128, 4, 128], BF16)
    Rb = QD
    Sf = sb("Sf", [128, 4, 128], F32)
    SbE = sb("SbE", [128, 4, 128], BF16)
    SbM = sb("SbM", [128, 4, 128], BF16)
    rbs = sb("rbs", [32, 8], F32)
    rb31 = sb("rb31", [8, 1], F32)

    _e2f = bass.AP(_e0.tensor, _e0.offset, [list(_e0.ap[0]), [1, 2048]]).bitcast(F32)
    cviews = {"ohd": _e2f[0:32, 0:384], "negrow": _e2f[0:8, 384:768], "j128": _e2f[:, 768:896],
              "shiftm": _e2f[:, 896:912], "identf": itmp.t[:, 3, :]}
    cst = {}
    for k, shp in CONST_SHAPES.items():
        if k in cviews:
            cst[k] = Buf("c_" + k, cviews[k])
        else:
            cst[k] = sb("c_" + k, shp, F32)
        P.dma("sp", cst[k].t[:], const_d[k], writes=[cst[k]])
    identf, m1t, m3t, sel, causneg, cmask, j128 = (cst[k] for k in
                                                   ("identf", "m1t", "m3t", "sel", "causneg", "cmask", "j128"))
    bank = [P.ps("bk%d" % i, [128, 512], F32) if i != 2 else P.ps("bk2", [128, 1024], BF16) for i in range(8)]
    tbB = bank[2]
    tb = bank[2].t[:, :]

    def tbv(c0, n, R=128):
        return tb[0:R, c0:c0 + n]

    op = P.op

    def act(out, in_, func, reads, writes, scale=None, bias=None, accum=None, eng="act"):
        kw = {}
        if scale is not None:
            kw["scale"] = scale
        if bias is not None:
            kw["bias"] = bias
        if accum is not None:
            kw["accum_out"] = accum
        op(eng, lambda e: e.activation(out=out, in_=in_, func=func, **kw), reads, writes)

    def ts(out, in0, s1, op0, reads, writes, s2=None, op1=None, accum=None, eng="dve"):
        kw = {}
        if op1 is not None:
            kw["op1"] = op1
        if accum is not None:
            kw["accum_out"] = accum
        op(eng, lambda e: e.tensor_scalar(out=out, in0=in0, scalar1=s1, scalar2=s2, op0=op0, **kw), reads, writes)

    def tt(out, in0, in1, o, reads, writes, eng="dve"):
        op(eng, lambda e: e.tensor_tensor(out=out, in0=in0, in1=in1, op=o), reads, writes)

    def stt(out, in0, scalar, in1, op0, op1, reads, writes):
        op("dve", lambda e: e.scalar_tensor_tensor(out=out, in0=in0, scalar=scalar, in1=in1, op0=op0, op1=op1),
           reads, writes)

    def cp(out, in_, reads, writes, eng="dve"):
        op(eng, lambda e: e.tensor_copy(out=out, in_=in_), reads, writes)

    def mm(out, lhsT, rhs, start, stop, reads, writes, sg=False):
        op("pe", lambda e: e.matmul(out, lhsT=lhsT, rhs=rhs, start=start, stop=stop, skip_group_check=sg), reads, writes)

    def tr(out, in_, R, reads, writes):
        op("pe", lambda e: e.transpose(out=out, in_=in_, identity=identb.t[0:R, 0:R]), list(reads) + [identb], writes)

    def sigm(dst, src, src_bufs, dbuf):
        act(dst, src, AF.Exp, src_bufs, [dbuf], scale=-1.0)
        act(dst, dst, AF.Ln, [dbuf], [dbuf], bias=1.0)
        act(dst, dst, AF.Exp, [dbuf], [dbuf], scale=-1.0)

    def rsqrt_small(dst, src, R, mul, add):
        dv = smv(dst[0], R, dst[1], dst[2])
        sv = smv(src[0], R, src[1], src[2])
        ts(dv, sv, mul, ALU.mult, [smb(src[0])], [smb(dst[0])], s2=add, op1=ALU.add)
        act(dv, dv, AF.Ln, [smb(dst[0])], [smb(dst[0])])
        act(dv, dv, AF.Exp, [smb(dst[0])], [smb(dst[0])], scale=-0.5)

    cp(identb.t[:], identf.t[:], [identf], [identb])
    op("dve", lambda e: e.memset(caug.t[:, :, 128:130], 1.0), [], caugB)
    op("dve", lambda e: e.memset(QDTA.t[:], 0.0), [], [QDTA])
    op("dve", lambda e: e.memset(QDTB.t[:], 0.0), [], [QDTB])
    op("dve", lambda e: e.memset(qiT.t[:], 0.0), [], [qiT])
    op("dve", lambda e: e.memset(KLB.t[:], 0.0), [], [KLB])
    P.dma("sp", rbs.t[:], rb_d, writes=[rbs])
    P.dma("sp", rb31.t[:], rb31_d, writes=[rb31])
    ohd = cst["ohd"]
    mm(bank[0].t[0:8, 0:384], rbs.t[:], ohd.t[:], True, True, [rbs, ohd], [bank[0]])
    ts(vs_ap, bank[0].t[0:8, 0:384], rb31.t[:, 0:1], ALU.subtract, [bank[0], rb31], [vs])
    tt(vs_ap, vs_ap, cst["negrow"].t[:], ALU.add, [vs, cst["negrow"]], [vs])
    P.dma("sp", vd_d, vs_ap, reads=[vs], writes=[vdB])
    P.alias(itmp, [vs])
    P.dma("sp", _split(hres.t[:, :], 8, 128), bass.AP(vd_d.tensor, 0, [[1, 128], [384, 8], [1, 128]]),
          reads=[vdB], writes=[hres])
    P.dma("sp", _resplit(TT.t[:, 0:2, :], 8, 128),
          bass.AP(vd_d.tensor, 128, [[1, 128], [384, 8], [1, 128]]), reads=[vdB], writes=[Tb[0], Tb[1]])
    for half in range(2):
        mm(bank[half].t[:, :], j128.t[:], hres.t[:, half * 512:(half + 1) * 512], True, True, [j128, hres], [bank[half]])
        act(B0.t[:, 0, half * 4:(half + 1) * 4, :], _split(bank[half].t[:, :], 4, 128), AF.Copy, [bank[half]], [B0])
        tt(B0.t[:, 1, half * 4:(half + 1) * 4, :], _split(bank[half].t[:, :], 4, 128), B0.t[:, 0, half * 4:(half + 1) * 4, :],
           ALU.subtract, [bank[half], B0], [B0])
    for half in range(2):
        mm(bank[half].t[:, :], j128.t[:], TT.t[:, half, :], True, True, [j128, Tb[half]], [bank[half]])
        act(B1.t[:, 0, half * 4:(half + 1) * 4, :], _split(bank[half].t[:, :], 4, 128), AF.Copy, [bank[half]], [B1])
        tt(B1.t[:, 1, half * 4:(half + 1) * 4, :], _split(bank[half].t[:, :], 4, 128), B1.t[:, 0, half * 4:(half + 1) * 4, :],
           ALU.subtract, [bank[half], B1], [B1])
    cp(shiftm.t[:], cst["shiftm"].t[:], [cst["shiftm"]], [shiftm])
    P.dma("sp", TT.t[:, 2:4, :], _split(lbraw_d, 2, 512), writes=[Tb[2], Tb[3]])
    tt(Tv(4), Tv(2), Tv(3), ALU.subtract, [Tb[2], Tb[3]], [Tb[4]])
    act(Tv(4), Tv(4), AF.Exp, [Tb[4]], [Tb[4]])
    ts(Tv(4), Tv(4), 1.0, ALU.add, [Tb[4]], [Tb[4]])
    op("dve", lambda e: e.reciprocal(out=lbB.t[:], in_=Tv(4)), [Tb[4]], [lbB])
    ts(omlB.t[:], lbB.t[:], -1.0, ALU.mult, [lbB], [omlB], s2=1.0, op1=ALU.add)

    for b_ in Eb:
        P.alias(b_, [cst["ohd"], cst["negrow"], cst["j128"], cst["shiftm"]])
    P.alias(itmp, [identf, vs])
    ck("prologue")
    SCALE_Q = 0.125
    SCALE_I = 1.0 / 16.0
    SCALE_H = 128.0 ** -0.5

    castn = [0]

    def cast(out, in_, reads, writes):
        e = ("pool", "dve", "act")[castn[0] % 3]
        castn[0] += 1
        if e == "act":
            act(out, in_, AF.Copy, reads, writes)
        else:
            cp(out, in_, reads, writes, eng=e)

    def pc_ap(h, R, n=129):
        return bank[5 + h // 3].t[0:R, (h % 3) * 129:(h % 3) * 129 + n]

    for l in layers:
        for b_ in stgB:
            P.alias(b_, [sidx])
        HW = NCOL // 2
        n = 0
        for kc in range(8):
            for half in range(2):
                st = stgB[n % 2]
                so = (n % 2) * 2048
                n += 1
                P.dma("sp", sidx.t[:, so:so + HW], win_d[l, kc * 128:(kc + 1) * 128, half * HW:(half + 1) * HW], writes=[st])
                cast(win.t[:, kc, half * HW:(half + 1) * HW], sidx.t[:, so:so + HW], [st], [winB[kc]])
        for j in range(8):
            st = stgB[n % 2]
            so = (n % 2) * 2048
            n += 1
            P.dma("sp", sidx.t[:, so:so + D], wout_d[l, j * 128:(j + 1) * 128, :], writes=[st])
            cast(wout.t[:, j, :], sidx.t[:, so:so + D], [st], [woutB[j]])
        st = stgB[n % 2]; so = (n % 2) * 2048; n += 1
        P.dma("sp", sidx.t[:, so:so + 1024], wuk_d[l], writes=[st])
        cast(wuk.t[:, :, :], _split(sidx.t[:, so:so + 1024], 8, 128), [st], [wuk])
        st = stgB[n % 2]; so = (n % 2) * 2048; n += 1
        P.dma("sp", sidx.t[:, so:so + 1024], wuv_d[l], writes=[st])
        cast(wuv.t[:, :, :], _split(sidx.t[:, so:so + 1024], 8, 128), [st], [wuv])
        P.alias(sidx, stgB)
        P.dma("sp", gkvB.t[:], gkv_d[l], writes=[gkvB])
        P.dma("sp", ghnB.t[:], ghn_d[l], writes=[ghnB])
        P.dma("sp", lnG.t[:], lng_d[l], writes=[lnG])
        P.dma("sp", lnBt.t[:], lnb_d[l], writes=[lnBt])
        op("dve", lambda e: e.memset(Sf.t[:], 0.0), [], [Sf])
        op("dve", lambda e: e.memset(SbE.t[:], 0.0), [], [SbE])
        last_layer = (l == final_layer)
        ck("weights%d" % l)

        def tile_gen(i, l=l, last_layer=last_layer):
            R = NMETA if i == 0 else 128
            RA = min(R, 64)
            qb = i - 1
            need_out = not (last_layer and i == 0)
            hres, qlT, gaT, arT = hresP[i % 2], qlTP[i % 2], gaTP[i % 2], arTP[i % 2]
            if l == 0:
                src, srcB = (meta_d if i == 0 else x_d[qb * 128:(qb + 1) * 128, :]), []
            else:
                src, srcB = h1_d[i * 128:i * 128 + R, :], [h1B[i]]
            P.dma("pool", hb.t[0:R, :], src, reads=srcB, writes=[hb])
            yield "A0"
            for kc in range(8):
                tr(tbv(kc * 128, R), hb.t[0:R, kc * 128:(kc + 1) * 128], R, [hb], [tbB])
            act(hT.t[:, :, 0:R], _split(tb[:, :], 8, 128)[:, :, 0:R], AF.Copy, [tbB], [hT])

            ck("l%dt%d_load" % (l, i))
            def fm_group(bk, col0, nchunk):
                for j in range(nchunk):
                    for kc in range(8):
                        mm(bk.t[:, j * 128:j * 128 + R], win.t[:, kc, col0 + j * 128:col0 + (j + 1) * 128],
                           hT.t[:, kc, 0:R], kc == 0, kc == 7, [winB[kc], hT], [bk])

            def tm_group(bk, col0, ncol):
                for kc in range(8):
                    mm(bk.t[0:R, 0:ncol], hT.t[:, kc, 0:R], win.t[:, kc, col0:col0 + ncol], kc == 0, kc == 7,
                       [winB[kc], hT], [bk])

            yield "A"
            b0, b1 = bank[0], bank[1]
            bq = bank[0]
            fm_group(bq, FQ, 4)
            act(qT.t[:, :, 0:R], _split(bq.t[:, :], 4, 128)[:, :, 0:R], AF.Copy, [bq], [qT], scale=SCALE_Q)
            yield "A"
            bq = bank[1]
            fm_group(bq, FK, 1)
            act(kiT.t[:, i * 128:i * 128 + R], bq.t[:, 0:R], AF.Copy, [bq], [kiTB[i]])
            yield "A"
            bq = bank[3]
            fm_group(bq, FG, 4)
            g4v = _split(bq.t[:, :], 4, 128)[:, :, 0:R]
            t4v = _split(TT.t[:, 4, :], 4, 128)[:, :, 0:R]
            sigm(t4v, g4v, [bq], Tb[4])
            tt(gaT.t[:, :, 0:R], g4v, t4v, ALU.mult, [bq, Tb[4]], [gaT])
            yield "A"
            bq = bank[4]
            tm_group(bq, TC, NTC)
            act(craw.t[0:R, :], bq.t[0:R, 0:NTC], AF.Copy, [bq], [craw])
            yield "A"
            bq = bank[5]
            tm_group(bq, TQ, 512)
            sigm(Tv(0, R), bq.t[0:R, :], [bq], Tb[0])
            stt(Tv(0, R), bq.t[0:R, :], SCALE_H, Tv(0, R), ALU.mult, ALU.mult, [bq, Tb[0]], [Tb[0]])
            yield "A"
            b1 = bank[6]
            tm_group(b1, TF, 512)
            act(Tv(1, R), b1.t[0:R, :], AF.Exp, [b1], [Tb[1]], scale=-1.0)
            act(Tv(2, R), Tv(1, R), AF.Ln, [Tb[1]], [Tb[2]], bias=1.0)
            act(Tv(1, R), Tv(2, R), AF.Exp, [Tb[2]], [Tb[1]], scale=-1.0)
            gs = -1.0
            if l > 0:
                gs = 1.0
                tt(Tv(1, R), Tv(1, R), omlB.t[0:R, :], ALU.mult, [Tb[1], omlB], [Tb[1]])
                tt(Tv(1, R), Tv(1, R), lbB.t[0:R, :], ALU.add, [Tb[1], lbB], [Tb[1]])
                act(Tv(2, R), Tv(1, R), AF.Ln, [Tb[1]], [Tb[2]])
            ts(Tv(1, R), Tv(1, R), -1.0, ALU.mult, [Tb[1]], [Tb[1]], s2=1.0, op1=ALU.add)
            yield "A"
            bq = bank[7]
            tm_group(bq, TI, 512)
            act(Vb.t[0:R, :], bq.t[0:R, :], AF.Copy, [bq], [Vb])
            yield "A"
            b1 = bank[0]
            tm_group(b1, TG, 512)
            sigm(Tv(3, R), b1.t[0:R, :], [b1], Tb[3])
            tt(Tv(3, R), b1.t[0:R, :], Tv(3, R), ALU.mult, [b1, Tb[3]], [Tb[3]])
            tt(Tv(3, R), Tv(3, R), ghnB.t[0:R, :], ALU.mult, [Tb[3], ghnB], [Tb[3]])

            b0, b1 = bank[0], bank[1]
            P.dma("sp", hres.t[0:R, :], src, reads=srcB, writes=[hres])
            yield "A"
            ck("l%dt%d_inproj" % (l, i))
            act(Tv(4, R)[:, 0:128], craw.t[0:R, 0:128], AF.Square, [craw], [Tb[4], smb("cs")], accum=smv("cs", R))
            rsqrt_small(("crs", 0, 1), ("cs", 0, 1), R, 1.0 / 128.0, EPS)
            stt(caug.t[0:R, i, 0:128], craw.t[0:R, 0:128], smv("crs", R), gkvB.t[0:R, :], ALU.mult, ALU.mult,
                [craw, smb("crs"), gkvB], [caugB[i]])
            tr(tbv(0, R), caug.t[0:R, i, 0:128], R, [caugB[i]], [tbB])
            ck("l%dt%d_c0" % (l, i))
            cp(cT.t[:, i * 128:i * 128 + R], tbv(0, R), [tbB], [cTB[i]])
            ck("l%dt%d_c" % (l, i))
            if i >= 1:
                wv = craw.t[0:R, 128:132]
                ts(smv("wsgn", R), wv, 0.0, ALU.is_ge, [craw], [smb("wsgn")], s2=2.0, op1=ALU.mult)
                ts(smv("wsgn", R), smv("wsgn", R), -1.0, ALU.add, [smb("wsgn")], [smb("wsgn")])
                tt(smv("wabs", R), wv, smv("wsgn", R), ALU.mult, [craw, smb("wsgn")], [smb("wabs")])
                ts(smv("wabs", R), smv("wabs", R), SCALE_I, ALU.mult, [smb("wabs")], [smb("wabs")])
                wb = smv("wabs", R)
                wbc = bass.AP(wb.tensor, wb.offset, [list(wb.ap[0]), [1, 4], [0, 64]])
                tt(_split(qis.t[0:R, :], 4, 64), _split(craw.t[0:R, 132:388], 4, 64), wbc, ALU.mult,
                   [craw, smb("wabs")], [qis])
                for j in range(2):
                    tr(tbv(j * 128, R), qis.t[0:R, j * 128:(j + 1) * 128], R, [qis], [tbB])
                for h in range(4):
                    pb = (h % 2) * 64
                    cp(qiT.t[pb:pb + 64, h, 0:R], tb[pb:pb + 64, (h // 2) * 128:(h // 2) * 128 + R], [tbB], [qiT])
            for h in range(8):
                bk = bank[3 + h // 4]
                mm(bk.t[:, (h % 4) * 128:(h % 4) * 128 + R], wuk.t[:, h, :], qT.t[:, h // 2, 0:R],
                   True, True, [wuk, qT], [bk])
            ck("l%dt%d_ql" % (l, i))
            act(qlT.t[:, 0:4, 0:R], _split(bank[3].t[:, :], 4, 128)[:, :, 0:R], AF.Copy, [bank[3]], [qlT])
            cp(qlT.t[:, 4:8, 0:R], _split(bank[4].t[:, :], 4, 128)[:, :, 0:R], [bank[4]], [qlT])

            yield "A"
            ck("l%dt%d_prep" % (l, i))
            mm(b0.t[0:R, :], m1t.t[0:R, 0:R], Tv(2, R), True, True, [m1t, Tb[2]], [b0])
            mm(b1.t[0:R, :], m3t.t[0:R, 0:R], Tv(2, R), True, True, [m3t, Tb[2]], [b1])
            for h in range(4):
                mm(bank[7].t[:, 2 * h:2 * h + 2], TT.t[0:R, 2, h * 128:(h + 1) * 128], sel.t[0:R, :], True, True,
                   [Tb[2], sel], [bank[7]])
            act(smv("dS"), bank[7].t[:, 0:8], AF.Exp, [bank[7]], [smb("dS")], scale=gs)
            act(Tv(4, R), b0.t[0:R, :], AF.Exp, [b0], [Tb[4]], scale=gs)
            tt(QD.t[0:R, :], Tv(0, R), Tv(4, R), ALU.mult, [Tb[0], Tb[4]], [QD])
            act(Tv(4, R), b0.t[0:R, :], AF.Exp, [b0], [Tb[4]], scale=-gs)
            tt(KD.t[0:R, :], Tv(1, R), Tv(4, R), ALU.mult, [Tb[1], Tb[4]], [KD])
            act(Tv(4, R), b1.t[0:R, :], AF.Exp, [b1], [Tb[4]], scale=gs)
            tt(KL.t[0:RA, :], Tv(1, RA), Tv(4, RA), ALU.mult, [Tb[1], Tb[4]], [KL])
            if R == 128:
                tt(KLB.t[64:128, :], TT.t[64:128, 1, :], TT.t[64:128, 4, :], ALU.mult, [Tb[1], Tb[4]], [KLB])
            yield "A"
            for h in range(4):
                tr(tbv(h * 128, R), QD.t[0:R, h * 128:(h + 1) * 128], R, [QD], [tbB])
            for h in range(4):
                tr(tbv(512 + h * 128, R), KD.t[0:R, h * 128:(h + 1) * 128], R, [KD], [tbB])
            t8 = _split(tb[:, :], 8, 128)
            act(QDTA.t[:, :, 0:RA], t8[:, 0:4, 0:RA], AF.Copy, [tbB], [QDTA])
            if R == 128:
                cp(QDTB.t[:, :, 64:128], t8[:, 0:4, 64:128], [tbB], [QDTB])
            act(KDT.t[:, :, 0:R], t8[:, 4:8, 0:R], AF.Copy, [tbB], [KDT])
            yield "A"
            b3, b4 = bank[3], bank[4]
            for h in range(4):
                mm(b3.t[0:R, h * 128:h * 128 + RA], KDT.t[:, h, 0:R], QDTA.t[:, h, 0:RA], True, True, [KDT, QDTA], [b3])
                if R == 128:
                    mm(b3.t[0:R, h * 128 + 64:h * 128 + 128], KDT.t[:, h, 0:R], QDTB.t[:, h, 64:128], True, True,
                       [KDT, QDTB], [b3])
            tt(SC.t[0:R, :, 0:R], _split(b3.t[0:R, :], 4, 128)[:, :, 0:R], _mid_bc(cmask.t[0:R, 0:R], 4), ALU.mult,
               [b3, cmask], [SC])
            for h in range(4):
                hc = slice(h * 128, (h + 1) * 128)
                mm(bank[5].t[:, hc], KL.t[0:RA, hc], Vb.t[0:RA, hc], True, True, [KL, Vb], [bank[5]])
                if R == 128:
                    mm(bank[6].t[:, hc], KLB.t[:, hc], Vb.t[:, hc], True, True, [KLB, Vb], [bank[6]])
            yield "A"
            for h in range(4):
                hc = slice(h * 128, (h + 1) * 128)
                mm(b4.t[0:R, hc], SC.t[0:R, h, 0:R], Vb.t[0:R, hc], h == 0, False, [SC, Vb], [b4], sg=True)
                mm(b4.t[0:R, hc], QDTA.t[:, h, 0:R], SbE.t[:, h, :], False, R < 128, [QDTA, SbE], [b4], sg=True)
            for h in range(4):
                hc = slice(h * 128, (h + 1) * 128)
                stt(Sf.t[:, h, :], Sf.t[:, h, :], smv("dS", 128, 2 * h, 2 * h + 1), bank[5].t[:, hc], ALU.mult, ALU.add,
                    [Sf, smb("dS"), bank[5]], [Sf])
            if R == 128:
                act(SbM.t[:], Sf.t[:], AF.Copy, [Sf], [SbM])
                for h in range(4):
                    hc = slice(h * 128, (h + 1) * 128)
                    mm(b4.t[0:R, hc], QDTB.t[:, h, :], SbM.t[:, h, :], False, True, [QDTB, SbM], [b4], sg=True)
                for h in range(4):
                    hc = slice(h * 128, (h + 1) * 128)
                    stt(Sf.t[:, h, :], Sf.t[:, h, :], smv("dS", 128, 2 * h + 1, 2 * h + 2), bank[6].t[:, hc], ALU.mult,
                        ALU.add, [Sf, smb("dS"), bank[6]], [Sf])
            act(SbE.t[:], Sf.t[:], AF.Copy, [Sf], [SbE])
            yield "A"
            if need_out:
                for h in range(4):
                    hc = slice(h * 128, (h + 1) * 128)
                    act(Tv(4, R)[:, hc], b4.t[0:R, hc], AF.Square, [b4], [Tb[4], smb("ssq4")],
                        accum=smv("ssq4", R, h, h + 1))
                rsqrt_small(("rs4", 0, 4), ("ssq4", 0, 4), R, 1.0 / 128.0, EPS)
                for h in range(4):
                    hc = slice(h * 128, (h + 1) * 128)
                    stt(Rb.t[0:R, hc], b4.t[0:R, hc], smv("rs4", R, h, h + 1), Tv(3, R)[:, hc], ALU.mult, ALU.mult,
                        [b4, smb("rs4"), Tb[3]], [Rb])
                for h in range(4):
                    tr(tbv(h * 128, R), Rb.t[0:R, h * 128:(h + 1) * 128], R, [Rb], [tbB])
                act(arT.t[:, 4:8, 0:R], t8[:, 0:4, 0:R], AF.Copy, [tbB], [arT])
            if taps and i == taps.get("_tile", 1) and l == taps.get("_layer", 0):
                tap("hgrn_o", b4, b4.t[0:R, :], [R, 512])
                tap("caug", caugB[i], caug.t[0:R, i, :], [R, 130])
                tap("qlT", qlT, qlT.t[:, :, :], [128, 8, 128])
                tap("kiT", kiTB[i], kiT.t[:, i * 128:(i + 1) * 128], [128, 128])
                tap("qiT", qiT, qiT.t[:, :, :], [128, 4, 128])
                tap("craw", craw, craw.t[:, :], [128, NTC])
                tap("gaT", gaT, gaT.t[:, :, :], [128, 4, 128])

            ck("l%dt%d_hgrn" % (l, i))
            yield "A_done"
            if not need_out:
                return
            use_thr = (i >= 1) and ((qb + 1) * 128 > ktop)
            if use_thr:
                nk = (qb + 1) * 128
                for b_ in sidxB:
                    b_.last_w = None
                    b_.readers = {}
                    P.alias(b_, [sidx])
                for kb0 in range(0, qb + 1, 2):
                    kbs = [kb for kb in (kb0, kb0 + 1) if kb <= qb]
                    for kb in kbs:
                        bkI = bank[kb % 2]
                        for h in range(4):
                            mm(bkI.t[:, h * 128:(h + 1) * 128], qiT.t[:, h, :],
                               kiT.t[:, (kb + 1) * 128:(kb + 2) * 128], True, True, [qiT, kiTB[kb + 1]], [bkI])
                    for kb in kbs:
                        bkI = bank[kb % 2]
                        act(itmpP[kb % 2].t[:, :, :], _split(bkI.t[:, :], 4, 128), AF.Relu, [bkI], [itmpP[kb % 2]])
                    for h in range(4):
                        for kb in kbs:
                            it_ = itmpP[kb % 2]
                            sv = sidx.t[:, kb * 128:(kb + 1) * 128]
                            sxb = sidxB[kb % 2]
                            if h == 0:
                                ts(sv, it_.t[:, 0, :], smv("wsgn", 128, 0, 1), ALU.mult, [it_, smb("wsgn")], [sxb])
                            else:
                                stt(sv, it_.t[:, h, :], smv("wsgn", 128, h, h + 1), sv, ALU.mult, ALU.add,
                                    [it_, smb("wsgn"), sxb], [sxb])
                    yield "idx"
                lw = [b_.last_w for b_ in sidxB if b_.last_w is not None]
                assert all(c[0] is lw[0][0] for c in lw)
                sidx.last_w = max(lw, key=lambda c: c[1])
                sidx.readers = {}
                yield "idx_done"
                sa = sidx.t[:, 0:nk]
                op("dve", lambda e, a=sa: e.tensor_reduce(out=smv("mx"), in_=a, axis=AX.X, op=ALU.max), [sidx], [smb("mx")])
                op("dve", lambda e, a=sa: e.tensor_reduce(out=smv("lo"), in_=a, axis=AX.X, op=ALU.min), [sidx], [smb("lo")])
                tt(smv("step"), smv("mx"), smv("lo"), ALU.subtract, [smb("mx"), smb("lo")], [smb("step")])
                dv = sidx.t[:, qb * 128:(qb + 1) * 128]
                tt(dv, dv, causneg.t[:], ALU.add, [sidx, causneg], [sidx])
                cD = int(nk * BIS_DVE_FRAC)
                thr_c = float(ktop) - 0.5 - (nk - cD) / 2.0
                stt(smv("mid"), smv("step"), 0.5, smv("lo"), ALU.mult, ALU.add, [smb("step"), smb("lo")], [smb("mid")])
                op("dve", lambda e, v_=thr_c: e.memset(smv("thrc"), v_), [], [smb("thrc")])
                for k in range(1, nit + 1):
                    f = 2.0 ** (-k)
                    act(_zs(jkA.t[:, 0:1], nk - cD), sidx.t[:, cD:nk], AF.Sign, [sidx, smb("mid")], [jkA, smb("sg")],
                        scale=-1.0, bias=smv("mid"), accum=smv("sg"))
                    ts(_zs(jkD.t[:, 0:1], cD), sidx.t[:, 0:cD], smv("mid"), ALU.is_ge, [sidx, smb("mid")], [jkD, smb("cnt")],
                       s2=0.0, op1=ALU.add, accum=smv("cnt"))
                    act(smv("thrA"), smv("sg"), AF.Identity, [smb("sg"), smb("thrc")], [smb("thrA")], scale=0.5, bias=smv("thrc"))
                    stt(smv("base"), smv("step"), -0.5 * f, smv("mid"), ALU.mult, ALU.add, [smb("step"), smb("mid")],
                        [smb("base")])
                    ts(smv("fl"), smv("cnt"), smv("thrA"), ALU.is_ge, [smb("cnt"), smb("thrA")], [smb("fl")], s2=f, op1=ALU.mult)
                    stt(smv("mid"), smv("fl"), smv("step"), smv("base"), ALU.mult, ALU.add,
                        [smb("fl"), smb("step"), smb("base")], [smb("mid")])
                    yield "bis"
                stt(smv("lo"), smv("step"), -(2.0 ** (-nit - 1)), smv("mid"), ALU.mult, ALU.add, [smb("step"), smb("mid")],
                    [smb("lo")])
                ts(mball.t[:, 0:nk], sa, smv("lo"), ALU.is_lt, [sidx, smb("lo")], [mball])
            else:
                yield "idx_done"
            yield "bis_done"
            ck("l%dt%d_thr" % (l, i))
            if i == 0:
                kblocks = [(0, NMETA, (identb.t[0:NMETA, 0:NMETA], B0, NMETA), False, None)]
            else:
                if qb == 0:
                    kblocks = [(0, NMETA, (shiftm.t[:, :], B1, 128), False, None)]
                else:
                    kblocks = [(0, NMETA, None, False, None)]
                for kb in range(qb + 1):
                    if kb == qb:
                        bias = (identb.t[:, :], B0, 128)
                    elif kb == qb - 1:
                        bias = (identb.t[:, :], B1, 128)
                    else:
                        bias = None
                    kblocks.append((kb + 1, 128, bias, use_thr, kb))
            nblk = len(kblocks)
            def emit_pc(bi_, kt_, KR_, E_):
                for h in range(8):
                    mm(pc_ap(h, R), E_.t[0:KR_, h, 0:R], caug.t[0:KR_, kt_, 0:129], bi_ == 0 and h % 3 == 0, bi_ == nblk - 1,
                       [E_, caugB[kt_]], [bank[5 + h // 3]], sg=True)

            pend = None
            for bi, (kt, KR, bias, masked, kb) in enumerate(kblocks):
                E = Eb[bi % 2]
                if masked:
                    mb = maskb[bi % 2]
                    ts(mb.t[:, :], mball.t[:, kb * 128:(kb + 1) * 128], NEG, ALU.mult, [mball], [mb], eng="pool")
                for half in range(2):
                    bk = bank[3 + half]
                    lgv = _split(bk.t[0:KR, 0:4 * R], 4, R)
                    nacc = 1 + (2 if bias is not None else 0) + (1 if masked else 0)
                    na = [0]

                    def acc(lhsT, rhs, rd):
                        na[0] += 1
                        mm(bk.t[0:KR, 0:4 * R], lhsT, rhs, na[0] == 1, na[0] == nacc, rd, [bk])
                    acc(cT.t[:, kt * 128:kt * 128 + KR], qlT.t[:, 4 * half:4 * half + 4, 0:R], [cTB[kt], qlT])
                    if bias is not None:
                        bl, bt, bk_rows = bias
                        for hl in range(2):
                            acc(bl, bt.t[0:bk_rows, hl, 4 * half:4 * half + 4, 0:R], [bt, identb, shiftm])
                    if masked:
                        acc(mb.t[:, :], _mid_bc(identb.t[:, :], 4), [mb, identb])
                    act(E.t[0:KR, 4 * half:4 * half + 4, 0:R], lgv, AF.Exp, [bk], [E])
                if pend is not None:
                    emit_pc(*pend)
                pend = (bi, kt, KR, E)
                yield "post"
            emit_pc(*pend)
            for g in range(3):
                nh = 3 if g < 2 else 2
                dn = bank[5 + g].t[0:R, 0:nh * 129]
                dnv = bass.AP(dn.tensor, dn.offset + 128, [list(dn.ap[0]), [129, nh]])
                op("dve", lambda e, a=dnv, o_=smv("rden", R, 3 * g, 3 * g + nh): e.reciprocal(out=o_, in_=a),
                   [bank[5 + g]], [smb("rden")])
            for h in range(8):
                if h % 2 == 0:
                    act(olat.t[0:R, h, :], pc_ap(h, R, 128), AF.Copy, [bank[5 + h // 3], smb("rden")], [olat],
                        scale=smv("rden", R, h, h + 1))
                else:
                    ts(olat.t[0:R, h, :], pc_ap(h, R, 128), smv("rden", R, h, h + 1), ALU.mult,
                       [bank[5 + h // 3], smb("rden")], [olat])
            for h in range(8):
                tr(tbv(h * 128, R), olat.t[0:R, h, :], R, [olat], [tbB])
            act(olatT.t[:, :, 0:R], t8[:, :, 0:R], AF.Copy, [tbB], [olatT])
            yield "post"
            for j in range(4):
                mm(b0.t[:, j * 128:j * 128 + R], wuv.t[:, 2 * j, :], olatT.t[:, 2 * j, 0:R], True, False, [wuv, olatT], [b0])
                mm(b0.t[:, j * 128:j * 128 + R], wuv.t[:, 2 * j + 1, :], olatT.t[:, 2 * j + 1, 0:R], False, True,
                   [wuv, olatT], [b0])
            tt(arT.t[:, 0:4, 0:R], _split(b0.t[:, :], 4, 128)[:, :, 0:R], gaT.t[:, :, 0:R], ALU.mult, [b0, gaT], [arT])

            yield "post"
            ck("l%dt%d_attn" % (l, i))
            for half in range(2):
                bk = bank[half]
                for j in range(8):
                    mm(bk.t[0:R, :], arT.t[:, j, 0:R], wout.t[:, j, half * 512:(half + 1) * 512], j == 0, j == 7,
                       [arT, woutB[j]], [bk])
            for half in range(2):
                hv = hres.t[0:R, half * 512:(half + 1) * 512]
                stt(hv, hv, DN_ALPHA, bank[half].t[0:R, :], ALU.mult, ALU.add, [hres, bank[half]], [hres])
            stB = Buf("stats_", stats.t)
            for half in range(2):
                op("dve", lambda e, hf=half: e.bn_stats(out=stats.t[0:R, hf * 6:(hf + 1) * 6],
                                                        in_=hres.t[0:R, hf * 512:(hf + 1) * 512]), [hres], [stB])
            op("dve", lambda e: e.bn_aggr(out=smv("mv", R), in_=stats.t[0:R, :]), [stB], [smb("mv")])
            ts(smv("lnr", R), smv("mv", R, 1, 2), EPS, ALU.add, [smb("mv")], [smb("lnr")])
            act(smv("lnr", R), smv("lnr", R), AF.Ln, [smb("lnr")], [smb("lnr")])
            act(smv("lnr", R), smv("lnr", R), AF.Exp, [smb("lnr")], [smb("lnr")], scale=-0.5)
            ts(smv("lnm", R), smv("mv", R, 0, 1), -1.0, ALU.mult, [smb("mv"), smb("lnr")], [smb("lnm")],
               s2=smv("lnr", R), op1=ALU.mult)
            act(hres.t[0:R, :], hres.t[0:R, :], AF.Identity, [hres, smb("lnr"), smb("lnm")], [hres],
                scale=smv("lnr", R), bias=smv("lnm", R))
            tt(hres.t[0:R, :], hres.t[0:R, :], lnG.t[0:R, :], ALU.mult, [hres, lnG], [hres])
            tt(hres.t[0:R, :], hres.t[0:R, :], lnBt.t[0:R, :], ALU.add, [hres, lnBt], [hres])
            if last_layer:
                P.dma("pool", out_d[qb * 128:(qb + 1) * 128, :], hres.t[0:R, :], reads=[hres], is_output=True)
            else:
                P.dma("pool", h1_d[i * 128:i * 128 + R, :], hres.t[0:R, :], reads=[hres], writes=[h1B[i]])
        def step(g):
            try:
                return next(g)
            except StopIteration:
                return None

        def run_to(g, marker):
            while True:
                m = step(g)
                if m is None or m == marker:
                    return m

        gens = [tile_gen(i) for i in range(NTT)]
        run_to(gens[0], "A_done")
        run_to(gens[0], "bis_done")
        if NTT > 1:
            run_to(gens[1], "A_done")
        for i in range(NTT):
            s1 = gens[i]
            s2 = gens[i + 1] if i + 1 < NTT else None
            s3 = gens[i + 2] if i + 2 < NTT else None
            l1, l2, l3 = True, s2 is not None, s3 is not None
            if l3:
                step(s3)
            while l1 or l2 or l3:
                if l2:
                    m = step(s2)
                    if m is None or m == "bis_done":
                        l2 = False
                if l1:
                    for _ in range(SCHED_POST_STEPS):
                        if step(s1) is None:
                            l1 = False
                            break
                elif l3:
                    m = step(s3)
                    if m is None or m == "A_done":
                        l3 = False
    P.finish()
    return P, tap_out


def prep_shared(inp):
    w_in = np.asarray(inp["w_in"], np.float32)
    o = {"q": (0, 512), "c": (512, 640), "qi": (640, 896), "ki": (896, 960), "wi": (960, 964), "ga": (964, 1476),
         "qh": (1476, 1988), "fh": (1988, 2500), "ih": (2500, 3012), "gh": (3012, 3524)}
    order = ["q", "ki", "ki", "ga", "c", "wi", "qi", "qh", "fh", "ih", "gh"]
    w_perm = np.ascontiguousarray(np.concatenate([w_in[:, :, o[k][0]:o[k][1]] for k in order], axis=2))
    assert w_perm.shape[2] == NCOL
    w_uk = np.asarray(inp["w_uk"], np.float32)
    wuk_l = np.zeros((2, 128, 8, 128), np.float32)
    for h in range(8):
        wuk_l[:, (h % 2) * 64:(h % 2) * 64 + 64, h, :] = w_uk[:, h]
    w_uv = np.asarray(inp["w_uv"], np.float32)
    wuv_l = np.zeros((2, 128, 8, 128), np.float32)
    for h in range(8):
        wuv_l[:, :, h, (h % 2) * 64:(h % 2) * 64 + 64] = w_uv[:, h]
    bc = lambda a, n: np.ascontiguousarray(np.broadcast_to(a[:, None, :], (a.shape[0], n, a.shape[1])))
    ghn = np.tile(np.asarray(inp["hgrn_norm_g"], np.float32), (1, 4))
    rb = np.asarray(inp["rel_bias"], np.float32)
    lbraw = np.asarray(inp["hgrn_lb_raw"], np.float32).reshape(1, 1024)
    d = {
        "meta": np.ascontiguousarray(np.asarray(inp["meta_tokens"], np.float32)),
        "w_in": w_perm,
        "w_out": np.ascontiguousarray(np.asarray(inp["w_out"], np.float32)),
        "w_uk": wuk_l.reshape(2, 128, 1024),
        "w_uv": wuv_l.reshape(2, 128, 1024),
        "gkv": bc(np.asarray(inp["kv_norm_g"], np.float32), 128),
        "ghn": bc(ghn, 128),
        "lng": bc(np.asarray(inp["ln_g"], np.float32), 128),
        "lnb": bc(np.asarray(inp["ln_b"], np.float32), 128),
        "lbraw": np.ascontiguousarray(np.broadcast_to(lbraw, (128, 1024))),
        "rb": np.ascontiguousarray(rb),
        "rb31": np.ascontiguousarray(rb[31].reshape(8, 1)),
    }
    d.update(host_constants())
    return d


NT_FULL = 32
NIT = 16
BIS_DVE_FRAC = 0.5
SCHED_POST_STEPS = 1


def kernel(**inputs):
    shared = prep_shared(inputs)
    x = np.asarray(inputs["x"], np.float32)
    B = x.shape[0]
    nc = bass.Bass("TRN2", target_bir_lowering=False)
    build_program(nc, NT_FULL, layers=(0, 1), ktop=256, nit=NIT, final_layer=1)
    in_maps = []
    for b in range(B):
        m = dict(shared)
        m["x"] = np.ascontiguousarray(x[b])
        in_maps.append(m)
    res = run_bass_kernel_spmd(nc, in_maps, core_ids=list(range(B)))
    out = np.stack([np.asarray(r["out"], np.float32) for r in res.results], axis=0)
    return out
```

```python
from contextlib import ExitStack
import numpy as np
import concourse.bass as bass
import concourse.mybir as mybir
from concourse.bass_utils import run_bass_kernel_spmd

F32 = mybir.dt.float32
BF16 = mybir.dt.bfloat16
U8 = mybir.dt.uint8
AF = mybir.ActivationFunctionType
ALU = mybir.AluOpType
AX = mybir.AxisListType

NDS = 16


class _Sem:
    def __init__(self, h, name):
        self.h = h
        self.name = name


class Buf:
    def __init__(self, name, t=None):
        self.name = name
        self.t = t
        self.last_w = None
        self.readers = {}


class _Eng:
    def __init__(self, name, sem):
        self.name = name
        self.sem = sem
        self.n = 0
        self.seen = {}
        self.q = []
        self.ndma = 0
        self.dsems = []


class Prog:
    ENG = ("pe", "act", "dve", "pool", "sp")

    def __init__(self, nc):
        self.nc = nc
        self.es = ExitStack()
        self.eng = {}
        for n in self.ENG:
            s = _Sem(self.es.enter_context(nc.semaphore("s_" + n)), n)
            self.eng[n] = _Eng(n, s)
        for n in ("sp", "pool", "act"):
            self.eng[n].dsems = [_Sem(self.es.enter_context(nc.semaphore("d_%s_%d" % (n, i))), "d%s%d" % (n, i))
                                 for i in range(NDS)]
        self.out_clocks = []
        self.nwait = 0

    def sb(self, name, shape, dt):
        return Buf(name, self.es.enter_context(self.nc.sbuf_tensor("s_" + name, list(shape), dt)))

    def ps(self, name, shape, dt):
        return Buf(name, self.es.enter_context(self.nc.psum_tensor("p_" + name, list(shape), dt)))

    def view(self, name, t):
        return Buf(name, t)

    def _need(self, E, clock, strict_same):
        sem, val = clock
        if sem is E.sem and not strict_same:
            return
        if E.seen.get(sem, 0) >= val:
            return
        E.seen[sem] = val
        E.q.append(("w", sem, val))
        self.nwait += 1

    def _deps(self, E, reads, writes, is_dma=False):
        strict = is_dma or E.name != "pe"
        for b in reads:
            if b.last_w is not None:
                self._need(E, b.last_w, True)
        for b in writes:
            if b.last_w is not None:
                self._need(E, b.last_w, strict)
            for s, v in b.readers.items():
                self._need(E, (s, v), strict)

    def _mark(self, clock, reads, writes):
        sem, val = clock
        for b in writes:
            b.last_w = clock
            b.readers = {}
        for b in reads:
            if b.readers.get(sem, 0) < val:
                b.readers[sem] = val

    def op(self, eng, fn, reads=(), writes=()):
        E = self.eng[eng]
        self._deps(E, reads, writes)
        E.n += 1
        E.q.append(("o", fn))
        self._mark((E.sem, E.n), reads, writes)

    def dma(self, queue, out, in_, reads=(), writes=(), is_output=False):
        E = self.eng[queue]
        j = E.ndma
        E.ndma += 1
        ds = E.dsems[j % NDS]
        prev = 16 * (j // NDS)
        if prev > 0:
            self._need(E, (ds, prev), True)
        self._deps(E, reads, writes, is_dma=True)
        E.q.append(("d", out, in_, ds))
        clock = (ds, prev + 16)
        self._mark(clock, reads, writes)
        if is_output:
            self.out_clocks.append(clock)
        return clock

    def finish(self, final_eng="sp"):
        E = self.eng[final_eng]
        last = {}
        for s, v in self.out_clocks:
            if last.get(s, (None, 0))[1] < v:
                last[s] = (s, v)
        for s, v in last.values():
            E.q.append(("w", s, v))
        nc = self.nc
        engs = self.eng

        def replay(E, e):
            for it in E.q:
                k = it[0]
                if k == "w":
                    e.wait_ge(it[1].h, it[2])
                elif k == "o":
                    it[1](e).then_inc(E.sem.h, 1)
                else:
                    e.dma_start(out=it[1], in_=it[2]).then_inc(it[3].h, 16)

        with nc.Block() as block:
            @block.tensor
            def _(e):
                replay(engs["pe"], e)

            @block.scalar
            def _(e):
                replay(engs["act"], e)

            @block.vector
            def _(e):
                replay(engs["dve"], e)

            @block.gpsimd
            def _(e):
                replay(engs["pool"], e)

            @block.sync
            def _(e):
                replay(engs["sp"], e)
        self.es.close()

    def alias(self, dst, srcs):
        for s in srcs:
            if s.last_w is not None:
                c = s.last_w
                if dst.readers.get(c[0], 0) < c[1]:
                    dst.readers[c[0]] = c[1]
            for k, v in s.readers.items():
                if dst.readers.get(k, 0) < v:
                    dst.readers[k] = v


D = 1024
NMETA = 16
FQ, FK, FG, TC, TQ, TF, TI, TG = 0, 512, 640, 1152, 1540, 2052, 2564, 3076
NTC = 388
NCOL = 3588
DN_ALPHA = 4.0 ** 0.25
EPS = 1e-6
NEG = -30000.0


def _mid_bc(ap, n):
    a = [list(x) for x in ap.ap]
    return bass.AP(ap.tensor, ap.offset, [a[0], [0, n]] + a[1:])


def host_constants():
    c = {}
    idx = np.arange(128)
    c["identf"] = np.eye(128, dtype=np.float32)
    same = (idx[:, None] // 64) == (idx[None, :] // 64)
    c["m1t"] = (same & (idx[:, None] <= idx[None, :])).astype(np.float32)
    c["m3t"] = (same & (idx[:, None] > idx[None, :])).astype(np.float32)
    sel = np.zeros((128, 2), np.float32)
    sel[:64, 0] = 1.0
    sel[64:, 1] = 1.0
    c["sel"] = sel
    c["causneg"] = np.where(idx[None, :] <= idx[:, None], 0.0, -1e30).astype(np.float32)
    c["cmask"] = (same & (idx[:, None] <= idx[None, :])).astype(np.float32)
    c["j128"] = np.fliplr(np.eye(128, dtype=np.float32)).copy()
    sh = np.zeros((128, 16), np.float32)
    sh[112 + np.arange(16), np.arange(16)] = 1.0
    c["shiftm"] = sh
    d = np.arange(384) - 127
    n = np.maximum(d, 0)
    nf = np.maximum(n, 1).astype(np.float32)
    large = 16 + (np.log(nf / np.float32(16)) / np.float32(np.log(128 / 16)) * np.float32(16)).astype(np.int32)
    large = np.minimum(large, 31)
    bucket = np.where(n < 16, n, large)
    oh = np.zeros((32, 384), np.float32)
    for j in range(384):
        if d[j] >= 0:
            oh[bucket[j], j] = 1.0
    c["ohd"] = oh
    c["negrow"] = np.broadcast_to(np.where(d >= 0, 0.0, NEG).astype(np.float32)[None, :], (8, 384)).copy()
    return c


CONST_SHAPES = {"identf": [128, 128], "m1t": [128, 128], "m3t": [128, 128], "sel": [128, 2],
                "causneg": [128, 128], "cmask": [128, 128], "j128": [128, 128], "shiftm": [128, 16],
                "ohd": [32, 384], "negrow": [8, 384]}


def _resplit(ap, a, b):
    return bass.AP(ap.tensor, ap.offset, [list(ap.ap[0]), [b, a], [1, b]])


def _zs(ap, n):
    return bass.AP(ap.tensor, ap.offset, [list(ap.ap[0]), [0, n]])


def _split(ap, a, b):
    p = list(ap.ap[0])
    st = ap.ap[-1][0]
    return bass.AP(ap.tensor, ap.offset, [p, [b * st, a], [st, b]])


class _StopBuild(Exception):
    pass


def build_program(nc, NT, layers=(0, 1), ktop=256, nit=24, taps=None, final_layer=1, stop_at=None):
    try:
        return _build_program(nc, NT, layers, ktop, nit, taps, final_layer, stop_at)
    except _StopBuild as e:
        P = e.args[0]
        P.finish()
        return P, {}


def _build_program(nc, NT, layers, ktop, nit, taps, final_layer, stop_at):
    NTT = NT + 1
    S = NT * 128
    P = Prog(nc)
    tap_out = {}

    def ck(name):
        if stop_at == name:
            raise _StopBuild(P)

    def dr(name, shape, kind="ExternalInput"):
        return nc.dram_tensor(name, list(shape), F32, kind=kind).ap()

    x_d = dr("x", [S, D])
    meta_d = dr("meta", [NMETA, D])
    out_d = dr("out", [S, D], "ExternalOutput")
    win_d = dr("w_in", [2, D, NCOL])
    wout_d = dr("w_out", [2, D, D])
    wuk_d = dr("w_uk", [2, 128, 1024])
    wuv_d = dr("w_uv", [2, 128, 1024])
    gkv_d = dr("gkv", [2, 128, 128])
    ghn_d = dr("ghn", [2, 128, 512])
    lng_d = dr("lng", [2, 128, D])
    lnb_d = dr("lnb", [2, 128, D])
    lbraw_d = dr("lbraw", [128, 1024])
    rb_d = dr("rb", [32, 8])
    rb31_d = dr("rb31", [8, 1])
    const_d = {k: dr(k, v) for k, v in CONST_SHAPES.items()}
    h1_d = dr("h1s", [NTT * 128, D], "Internal")
    vd_d = dr("vds", [8, 384], "Internal")
    h1B = [Buf("h1_%d" % i) for i in range(NTT)]
    vdB = Buf("vd")

    def tap(name, buf, ap, shape):
        if taps is None or name not in taps:
            return
        d = dr("tap_" + name, shape, "ExternalOutput")
        tap_out[name] = shape
        P.dma("pool", d, ap, reads=[buf], is_output=True)

    sb = P.sb
    win = sb("win", [128, 8, NCOL], BF16)
    winB = [Buf("win%d" % k, win.t) for k in range(8)]
    wout = sb("wout", [128, 8, D], BF16)
    woutB = [Buf("wout%d" % k, wout.t) for k in range(8)]
    wuk = sb("wuk", [128, 8, 128], BF16)
    wuv = sb("wuv", [128, 8, 128], BF16)
    caug = sb("caug", [128, NTT, 130], BF16)
    caugB = [Buf("caug%d" % i, caug.t) for i in range(NTT)]
    cT = sb("cT", [128, NTT * 128], BF16)
    cTB = [Buf("cT%d" % i, cT.t) for i in range(NTT)]
    kiT = sb("kiT", [128, NTT * 128], BF16)
    kiTB = [Buf("kiT%d" % i, kiT.t) for i in range(NTT)]
    B0 = sb("B0", [128, 2, 8, 128], BF16)
    B1 = sb("B1", [128, 2, 8, 128], BF16)
    shiftm = sb("shiftm", [128, 16], BF16)
    gkvB = sb("gkvB", [128, 128], F32)
    ghnB = sb("ghnB", [128, 512], F32)
    lnG = sb("lnG", [128, D], F32)
    lnBt = sb("lnB", [128, D], F32)
    lbB = sb("lbB", [128, 512], F32)
    omlB = sb("omlB", [128, 512], F32)
    identb = sb("identb", [128, 128], BF16)
    hresP = [sb("hres%d" % k, [128, D], F32) for k in range(2)]
    hres = hresP[0]
    hb = sb("hb", [128, D], BF16)
    hT = sb("hT", [128, 8, 128], BF16)
    qT = sb("qT", [128, 4, 128], BF16)
    qlTP = [sb("qlT%d" % k, [128, 8, 128], BF16) for k in range(2)]
    qlT = qlTP[0]
    qis = sb("qis", [128, 256], BF16)
    qiT = sb("qiT", [128, 4, 128], BF16)
    gaTP = [sb("gaT%d" % k, [128, 4, 128], BF16) for k in range(2)]
    gaT = gaTP[0]
    craw = sb("craw", [128, NTC], F32)
    sm = sb("sm", [128, 64], F32)
    smB = {}

    def small(name, c0, n):
        smB[name] = (Buf("sm_" + name, sm.t), c0, n)

    small("wabs", 0, 4); small("wsgn", 4, 4); small("cs", 8, 1); small("crs", 9, 1)
    small("lo", 10, 1); small("step", 11, 1); small("mid", 12, 1); small("cnt", 13, 1); small("fl", 14, 1)
    small("mx", 15, 1); small("rden", 16, 8); small("ssq4", 24, 4); small("rs4", 28, 4)
    small("sg", 36, 1); small("thrA", 37, 1); small("thrc", 38, 1); small("base", 39, 1); small("mv", 32, 2); small("lnr", 34, 1); small("lnm", 35, 1); small("dS", 40, 8)

    def smv(name, R=128, a=None, b=None):
        bf, c0, n = smB[name]
        a = 0 if a is None else a
        b = n if b is None else b
        return sm.t[0:R, c0 + a:c0 + b]

    def smb(name):
        return smB[name][0]

    stats = sb("stats", [128, 12], F32)
    sidx = sb("sidx", [128, 4096], F32)
    stgB = [Buf("stg0", sidx.t), Buf("stg1", sidx.t)]
    itmp = sb("itmp", [128, 4, 128], F32)
    itmpP = [itmp, sb("itmp2", [128, 4, 128], F32)]
    sidxB = [Buf("sidxA"), Buf("sidxB")]
    mball = sb("mball", [128, 4096], U8)
    jkD = sb("jkD", [128, 8], U8)
    jkA = sb("jkA", [128, 8], mybir.dt.int8)
    _iv = itmp.t[0:8, 0:3, :]
    vs = Buf("vs", None)
    vs_ap = bass.AP(_iv.tensor, _iv.offset, [list(_iv.ap[0]), [1, 384]])
    maskb = [sb("maskb%d" % i, [128, 128], BF16) for i in range(2)]
    E2 = sb("E2", [128, 2, 8, 128], BF16)
    Eb = [Buf("E%d" % k, E2.t[:, k]) for k in range(2)]
    _e0 = E2.t[:, 0, 0, :]
    olat = sb("olat", [128, 8, 128], BF16)
    olatT = sb("olatT", [128, 8, 128], BF16)
    arTP = [sb("arT%d" % k, [128, 8, 128], BF16) for k in range(2)]
    arT = arTP[0]
    TT = sb("TT", [128, 5, 512], F32)
    Tb = [Buf("T%d" % k, TT.t) for k in range(5)]

    def Tv(k, R=128):
        return TT.t[0:R, k, :]
    Vb = sb("Vb", [128, 512], BF16)
    QD = sb("QD", [128, 512], BF16)
    KD = sb("KD", [128, 512], BF16)
    KL = sb("KL", [128, 512], BF16)
    KLB = sb("KLB", [128, 512], BF16)
    QDTA = sb("QDTA", [128, 4, 128], BF16)
    QDTB = sb("QDTB", [128, 4, 128], BF16)
    KDT = sb("KDT", [128, 4, 128], BF16)
    SC = sb("SC", [128, 4, 128], BF16)
    Rb = QD
    Sf = sb("Sf", [128, 4, 128], F32)
    SbE = sb("SbE", [128, 4, 128], BF16)
    SbM = sb("SbM", [128, 4, 128], BF16)
    rbs = sb("rbs", [32, 8], F32)
    rb31 = sb("rb31", [8, 1], F32)

    _e2f = bass.AP(_e0.tensor, _e0.offset, [list(_e0.ap[0]), [1, 2048]]).bitcast(F32)
    cviews = {"ohd": _e2f[0:32, 0:384], "negrow": _e2f[0:8, 384:768], "j128": _e2f[:, 768:896],
              "shiftm": _e2f[:, 896:912], "identf": itmp.t[:, 3, :]}
    cst = {}
    for k, shp in CONST_SHAPES.items():
        if k in cviews:
            cst[k] = Buf("c_" + k, cviews[k])
        else:
            cst[k] = sb("c_" + k, shp, F32)
        P.dma("sp", cst[k].t[:], const_d[k], writes=[cst[k]])
    identf, m1t, m3t, sel, causneg, cmask, j128 = (cst[k] for k in
                                                   ("identf", "m1t", "m3t", "sel", "causneg", "cmask", "j128"))
    bank = [P.ps("bk%d" % i, [128, 512], F32) if i != 2 else P.ps("bk2", [128, 1024], BF16) for i in range(8)]
    tbB = bank[2]
    tb = bank[2].t[:, :]

    def tbv(c0, n, R=128):
        return tb[0:R, c0:c0 + n]

    op = P.op

    def act(out, in_, func, reads, writes, scale=None, bias=None, accum=None, eng="act"):
        kw = {}
        if scale is not None:
            kw["scale"] = scale
        if bias is not None:
            kw["bias"] = bias
        if accum is not None:
            kw["accum_out"] = accum
        op(eng, lambda e: e.activation(out=out, in_=in_, func=func, **kw), reads, writes)

    def ts(out, in0, s1, op0, reads, writes, s2=None, op1=None, accum=None, eng="dve"):
        kw = {}
        if op1 is not None:
            kw["op1"] = op1
        if accum is not None:
            kw["accum_out"] = accum
        op(eng, lambda e: e.tensor_scalar(out=out, in0=in0, scalar1=s1, scalar2=s2, op0=op0, **kw), reads, writes)

    def tt(out, in0, in1, o, reads, writes, eng="dve"):
        op(eng, lambda e: e.tensor_tensor(out=out, in0=in0, in1=in1, op=o), reads, writes)

    def stt(out, in0, scalar, in1, op0, op1, reads, writes):
        op("dve", lambda e: e.scalar_tensor_tensor(out=out, in0=in0, scalar=scalar, in1=in1, op0=op0, op1=op1),
           reads, writes)

    def cp(out, in_, reads, writes, eng="dve"):
        op(eng, lambda e: e.tensor_copy(out=out, in_=in_), reads, writes)

    def mm(out, lhsT, rhs, start, stop, reads, writes, sg=False):
        op("pe", lambda e: e.matmul(out, lhsT=lhsT, rhs=rhs, start=start, stop=stop, skip_group_check=sg), reads, writes)

    def tr(out, in_, R, reads, writes):
        op("pe", lambda e: e.transpose(out=out, in_=in_, identity=identb.t[0:R, 0:R]), list(reads) + [identb], writes)

    def sigm(dst, src, src_bufs, dbuf):
        act(dst, src, AF.Exp, src_bufs, [dbuf], scale=-1.0)
        act(dst, dst, AF.Ln, [dbuf], [dbuf], bias=1.0)
        act(dst, dst, AF.Exp, [dbuf], [dbuf], scale=-1.0)

    def rsqrt_small(dst, src, R, mul, add):
        dv = smv(dst[0], R, dst[1], dst[2])
        sv = smv(src[0], R, src[1], src[2])
        ts(dv, sv, mul, ALU.mult, [smb(src[0])], [smb(dst[0])], s2=add, op1=ALU.add)
        act(dv, dv, AF.Ln, [smb(dst[0])], [smb(dst[0])])
        act(dv, dv, AF.Exp, [smb(dst[0])], [smb(dst[0])], scale=-0.5)

    cp(identb.t[:], identf.t[:], [identf], [identb])
    op("dve", lambda e: e.memset(caug.t[:, :, 128:130], 1.0), [], caugB)
    op("dve", lambda e: e.memset(QDTA.t[:], 0.0), [], [QDTA])
    op("dve", lambda e: e.memset(QDTB.t[:], 0.0), [], [QDTB])
    op("dve", lambda e: e.memset(qiT.t[:], 0.0), [], [qiT])
    op("dve", lambda e: e.memset(KLB.t[:], 0.0), [], [KLB])
    P.dma("sp", rbs.t[:], rb_d, writes=[rbs])
    P.dma("sp", rb31.t[:], rb31_d, writes=[rb31])
    ohd = cst["ohd"]
    mm(bank[0].t[0:8, 0:384], rbs.t[:], ohd.t[:], True, True, [rbs, ohd], [bank[0]])
    ts(vs_ap, bank[0].t[0:8, 0:384], rb31.t[:, 0:1], ALU.subtract, [bank[0], rb31], [vs])
    tt(vs_ap, vs_ap, cst["negrow"].t[:], ALU.add, [vs, cst["negrow"]], [vs])
    P.dma("sp", vd_d, vs_ap, reads=[vs], writes=[vdB])
    P.alias(itmp, [vs])
    P.dma("sp", _split(hres.t[:, :], 8, 128), bass.AP(vd_d.tensor, 0, [[1, 128], [384, 8], [1, 128]]),
          reads=[vdB], writes=[hres])
    P.dma("sp", _resplit(TT.t[:, 0:2, :], 8, 128),
          bass.AP(vd_d.tensor, 128, [[1, 128], [384, 8], [1, 128]]), reads=[vdB], writes=[Tb[0], Tb[1]])
    for half in range(2):
        mm(bank[half].t[:, :], j128.t[:], hres.t[:, half * 512:(half + 1) * 512], True, True, [j128, hres], [bank[half]])
        act(B0.t[:, 0, half * 4:(half + 1) * 4, :], _split(bank[half].t[:, :], 4, 128), AF.Copy, [bank[half]], [B0])
        tt(B0.t[:, 1, half * 4:(half + 1) * 4, :], _split(bank[half].t[:, :], 4, 128), B0.t[:, 0, half * 4:(half + 1) * 4, :],
           ALU.subtract, [bank[half], B0], [B0])
    for half in range(2):
        mm(bank[half].t[:, :], j128.t[:], TT.t[:, half, :], True, True, [j128, Tb[half]], [bank[half]])
        act(B1.t[:, 0, half * 4:(half + 1) * 4, :], _split(bank[half].t[:, :], 4, 128), AF.Copy, [bank[half]], [B1])
        tt(B1.t[:, 1, half * 4:(half + 1) * 4, :], _split(bank[half].t[:, :], 4, 128), B1.t[:, 0, half * 4:(half + 1) * 4, :],
           ALU.subtract, [bank[half], B1], [B1])
    cp(shiftm.t[:], cst["shiftm"].t[:], [cst["shiftm"]], [shiftm])
    P.dma("sp", TT.t[:, 2:4, :], _split(lbraw_d, 2, 512), writes=[Tb[2], Tb[3]])
    tt(Tv(4), Tv(2), Tv(3), ALU.subtract, [Tb[2], Tb[3]], [Tb[4]])
    act(Tv(4), Tv(4), AF.Exp, [Tb[4]], [Tb[4]])
    ts(Tv(4), Tv(4), 1.0, ALU.add, [Tb[4]], [Tb[4]])
    op("dve", lambda e: e.reciprocal(out=lbB.t[:], in_=Tv(4)), [Tb[4]], [lbB])
    ts(omlB.t[:], lbB.t[:], -1.0, ALU.mult, [lbB], [omlB], s2=1.0, op1=ALU.add)

    for b_ in Eb:
        P.alias(b_, [cst["ohd"], cst["negrow"], cst["j128"], cst["shiftm"]])
    P.alias(itmp, [identf, vs])
    ck("prologue")
    SCALE_Q = 0.125
    SCALE_I = 1.0 / 16.0
    SCALE_H = 128.0 ** -0.5

    castn = [0]

    def cast(out, in_, reads, writes):
        e = ("pool", "dve", "act")[castn[0] % 3]
        castn[0] += 1
        if e == "act":
            act(out, in_, AF.Copy, reads, writes)
        else:
            cp(out, in_, reads, writes, eng=e)

    def pc_ap(h, R, n=129):
        return bank[5 + h // 3].t[0:R, (h % 3) * 129:(h % 3) * 129 + n]

    for l in layers:
        for b_ in stgB:
            P.alias(b_, [sidx])
        HW = NCOL // 2
        n = 0
        for kc in range(8):
            for half in range(2):
                st = stgB[n % 2]
                so = (n % 2) * 2048
                n += 1
                P.dma("sp", sidx.t[:, so:so + HW], win_d[l, kc * 128:(kc + 1) * 128, half * HW:(half + 1) * HW], writes=[st])
                cast(win.t[:, kc, half * HW:(half + 1) * HW], sidx.t[:, so:so + HW], [st], [winB[kc]])
        for j in range(8):
            st = stgB[n % 2]
            so = (n % 2) * 2048
            n += 1
            P.dma("sp", sidx.t[:, so:so + D], wout_d[l, j * 128:(j + 1) * 128, :], writes=[st])
            cast(wout.t[:, j, :], sidx.t[:, so:so + D], [st], [woutB[j]])
        st = stgB[n % 2]; so = (n % 2) * 2048; n += 1
        P.dma("sp", sidx.t[:, so:so + 1024], wuk_d[l], writes=[st])
        cast(wuk.t[:, :, :], _split(sidx.t[:, so:so + 1024], 8, 128), [st], [wuk])
        st = stgB[n % 2]; so = (n % 2) * 2048; n += 1
        P.dma("sp", sidx.t[:, so:so + 1024], wuv_d[l], writes=[st])
        cast(wuv.t[:, :, :], _split(sidx.t[:, so:so + 1024], 8, 128), [st], [wuv])
        P.alias(sidx, stgB)
        P.dma("sp", gkvB.t[:], gkv_d[l], writes=[gkvB])
        P.dma("sp", ghnB.t[:], ghn_d[l], writes=[ghnB])
        P.dma("sp", lnG.t[:], lng_d[l], writes=[lnG])
        P.dma("sp", lnBt.t[:], lnb_d[l], writes=[lnBt])
        op("dve", lambda e: e.memset(Sf.t[:], 0.0), [], [Sf])
        op("dve", lambda e: e.memset(SbE.t[:], 0.0), [], [SbE])
        last_layer = (l == final_layer)
        ck("weights%d" % l)

        def tile_gen(i, l=l, last_layer=last_layer):
            R = NMETA if i == 0 else 128
            RA = min(R, 64)
            qb = i - 1
            need_out = not (last_layer and i == 0)
            hres, qlT, gaT, arT = hresP[i % 2], qlTP[i % 2], gaTP[i % 2], arTP[i % 2]
            if l == 0:
                src, srcB = (meta_d if i == 0 else x_d[qb * 128:(qb + 1) * 128, :]), []
            else:
                src, srcB = h1_d[i * 128:i * 128 + R, :], [h1B[i]]
            P.dma("pool", hb.t[0:R, :], src, reads=srcB, writes=[hb])
            yield "A0"
            for kc in range(8):
                tr(tbv(kc * 128, R), hb.t[0:R, kc * 128:(kc + 1) * 128], R, [hb], [tbB])
            act(hT.t[:, :, 0:R], _split(tb[:, :], 8, 128)[:, :, 0:R], AF.Copy, [tbB], [hT])

            ck("l%dt%d_load" % (l, i))
            def fm_group(bk, col0, nchunk):
                for j in range(nchunk):
                    for kc in range(8):
                        mm(bk.t[:, j * 128:j * 128 + R], win.t[:, kc, col0 + j * 128:col0 + (j + 1) * 128],
                           hT.t[:, kc, 0:R], kc == 0, kc == 7, [winB[kc], hT], [bk])

            def tm_group(bk, col0, ncol):
                for kc in range(8):
                    mm(bk.t[0:R, 0:ncol], hT.t[:, kc, 0:R], win.t[:, kc, col0:col0 + ncol], kc == 0, kc == 7,
                       [winB[kc], hT], [bk])

            yield "A"
            b0, b1 = bank[0], bank[1]
            bq = bank[0]
            fm_group(bq, FQ, 4)
            act(qT.t[:, :, 0:R], _split(bq.t[:, :], 4, 128)[:, :, 0:R], AF.Copy, [bq], [qT], scale=SCALE_Q)
            yield "A"
            bq = bank[1]
            fm_group(bq, FK, 1)
            act(kiT.t[:, i * 128:i * 128 + R], bq.t[:, 0:R], AF.Copy, [bq], [kiTB[i]])
            yield "A"
            bq = bank[3]
            fm_group(bq, FG, 4)
            g4v = _split(bq.t[:, :], 4, 128)[:, :, 0:R]
            t4v = _split(TT.t[:, 4, :], 4, 128)[:, :, 0:R]
            sigm(t4v, g4v, [bq], Tb[4])
            tt(gaT.t[:, :, 0:R], g4v, t4v, ALU.mult, [bq, Tb[4]], [gaT])
            yield "A"
            bq = bank[4]
            tm_group(bq, TC, NTC)
            act(craw.t[0:R, :], bq.t[0:R, 0:NTC], AF.Copy, [bq], [craw])
            yield "A"
            bq = bank[5]
            tm_group(bq, TQ, 512)
            sigm(Tv(0, R), bq.t[0:R, :], [bq], Tb[0])
            stt(Tv(0, R), bq.t[0:R, :], SCALE_H, Tv(0, R), ALU.mult, ALU.mult, [bq, Tb[0]], [Tb[0]])
            yield "A"
            b1 = bank[6]
            tm_group(b1, TF, 512)
            act(Tv(1, R), b1.t[0:R, :], AF.Exp, [b1], [Tb[1]], scale=-1.0)
            act(Tv(2, R), Tv(1, R), AF.Ln, [Tb[1]], [Tb[2]], bias=1.0)
            act(Tv(1, R), Tv(2, R), AF.Exp, [Tb[2]], [Tb[1]], scale=-1.0)
            gs = -1.0
            if l > 0:
                gs = 1.0
                tt(Tv(1, R), Tv(1, R), omlB.t[0:R, :], ALU.mult, [Tb[1], omlB], [Tb[1]])
                tt(Tv(1, R), Tv(1, R), lbB.t[0:R, :], ALU.add, [Tb[1], lbB], [Tb[1]])
                act(Tv(2, R), Tv(1, R), AF.Ln, [Tb[1]], [Tb[2]])
            ts(Tv(1, R), Tv(1, R), -1.0, ALU.mult, [Tb[1]], [Tb[1]], s2=1.0, op1=ALU.add)
            yield "A"
            bq = bank[7]
            tm_group(bq, TI, 512)
            act(Vb.t[0:R, :], bq.t[0:R, :], AF.Copy, [bq], [Vb])
            yield "A"
            b1 = bank[0]
            tm_group(b1, TG, 512)
            sigm(Tv(3, R), b1.t[0:R, :], [b1], Tb[3])
            tt(Tv(3, R), b1.t[0:R, :], Tv(3, R), ALU.mult, [b1, Tb[3]], [Tb[3]])
            tt(Tv(3, R), Tv(3, R), ghnB.t[0:R, :], ALU.mult, [Tb[3], ghnB], [Tb[3]])

            b0, b1 = bank[0], bank[1]
            P.dma("sp", hres.t[0:R, :], src, reads=srcB, writes=[hres])
            yield "A"
            ck("l%dt%d_inproj" % (l, i))
            act(Tv(4, R)[:, 0:128], craw.t[0:R, 0:128], AF.Square, [craw], [Tb[4], smb("cs")], accum=smv("cs", R))
            rsqrt_small(("crs", 0, 1), ("cs", 0, 1), R, 1.0 / 128.0, EPS)
            stt(caug.t[0:R, i, 0:128], craw.t[0:R, 0:128], smv("crs", R), gkvB.t[0:R, :], ALU.mult, ALU.mult,
                [craw, smb("crs"), gkvB], [caugB[i]])
            tr(tbv(0, R), caug.t[0:R, i, 0:128], R, [caugB[i]], [tbB])
            ck("l%dt%d_c0" % (l, i))
            cp(cT.t[:, i * 128:i * 128 + R], tbv(0, R), [tbB], [cTB[i]])
            ck("l%dt%d_c" % (l, i))
            if i >= 1:
                wv = craw.t[0:R, 128:132]
                ts(smv("wsgn", R), wv, 0.0, ALU.is_ge, [craw], [smb("wsgn")], s2=2.0, op1=ALU.mult)
                ts(smv("wsgn", R), smv("wsgn", R), -1.0, ALU.add, [smb("wsgn")], [smb("wsgn")])
                tt(smv("wabs", R), wv, smv("wsgn", R), ALU.mult, [craw, smb("wsgn")], [smb("wabs")])
                ts(smv("wabs", R), smv("wabs", R), SCALE_I, ALU.mult, [smb("wabs")], [smb("wabs")])
                wb = smv("wabs", R)
                wbc = bass.AP(wb.tensor, wb.offset, [list(wb.ap[0]), [1, 4], [0, 64]])
                tt(_split(qis.t[0:R, :], 4, 64), _split(craw.t[0:R, 132:388], 4, 64), wbc, ALU.mult,
                   [craw, smb("wabs")], [qis])
                for j in range(2):
                    tr(tbv(j * 128, R), qis.t[0:R, j * 128:(j + 1) * 128], R, [qis], [tbB])
                for h in range(4):
                    pb = (h % 2) * 64
                    cp(qiT.t[pb:pb + 64, h, 0:R], tb[pb:pb + 64, (h // 2) * 128:(h // 2) * 128 + R], [tbB], [qiT])
            for h in range(8):
                bk = bank[3 + h // 4]
                mm(bk.t[:, (h % 4) * 128:(h % 4) * 128 + R], wuk.t[:, h, :], qT.t[:, h // 2, 0:R],
                   True, True, [wuk, qT], [bk])
            ck("l%dt%d_ql" % (l, i))
            act(qlT.t[:, 0:4, 0:R], _split(bank[3].t[:, :], 4, 128)[:, :, 0:R], AF.Copy, [bank[3]], [qlT])
            cp(qlT.t[:, 4:8, 0:R], _split(bank[4].t[:, :], 4, 128)[:, :, 0:R], [bank[4]], [qlT])

            yield "A"
            ck("l%dt%d_prep" % (l, i))
            mm(b0.t[0:R, :], m1t.t[0:R, 0:R], Tv(2, R), True, True, [m1t, Tb[2]], [b0])
            mm(b1.t[0:R, :], m3t.t[0:R, 0:R], Tv(2, R), True, True, [m3t, Tb[2]], [b1])
            for h in range(4):
                mm(bank[7].t[:, 2 * h:2 * h + 2], TT.t[0:R, 2, h * 128:(h + 1) * 128], sel.t[0:R, :], True, True,
                   [Tb[2], sel], [bank[7]])
            act(smv("dS"), bank[7].t[:, 0:8], AF.Exp, [bank[7]], [smb("dS")], scale=gs)
            act(Tv(4, R), b0.t[0:R, :], AF.Exp, [b0], [Tb[4]], scale=gs)
            tt(QD.t[0:R, :], Tv(0, R), Tv(4, R), ALU.mult, [Tb[0], Tb[4]], [QD])
            act(Tv(4, R), b0.t[0:R, :], AF.Exp, [b0], [Tb[4]], scale=-gs)
            tt(KD.t[0:R, :], Tv(1, R), Tv(4, R), ALU.mult, [Tb[1], Tb[4]], [KD])
            act(Tv(4, R), b1.t[0:R, :], AF.Exp, [b1], [Tb[4]], scale=gs)
            tt(KL.t[0:RA, :], Tv(1, RA), Tv(4, RA), ALU.mult, [Tb[1], Tb[4]], [KL])
            if R == 128:
                tt(KLB.t[64:128, :], TT.t[64:128, 1, :], TT.t[64:128, 4, :], ALU.mult, [Tb[1], Tb[4]], [KLB])
            yield "A"
            for h in range(4):
                tr(tbv(h * 128, R), QD.t[0:R, h * 128:(h + 1) * 128], R, [QD], [tbB])
            for h in range(4):
                tr(tbv(512 + h * 128, R), KD.t[0:R, h * 128:(h + 1) * 128], R, [KD], [tbB])
            t8 = _split(tb[:, :], 8, 128)
            act(QDTA.t[:, :, 0:RA], t8[:, 0:4, 0:RA], AF.Copy, [tbB], [QDTA])
            if R == 128:
                cp(QDTB.t[:, :, 64:128], t8[:, 0:4, 64:128], [tbB], [QDTB])
            act(KDT.t[:, :, 0:R], t8[:, 4:8, 0:R], AF.Copy, [tbB], [KDT])
            yield "A"
            b3, b4 = bank[3], bank[4]
            for h in range(4):
                mm(b3.t[0:R, h * 128:h * 128 + RA], KDT.t[:, h, 0:R], QDTA.t[:, h, 0:RA], True, True, [KDT, QDTA], [b3])
                if R == 128:
                    mm(b3.t[0:R, h * 128 + 64:h * 128 + 128], KDT.t[:, h, 0:R], QDTB.t[:, h, 64:128], True, True,
                       [KDT, QDTB], [b3])
            tt(SC.t[0:R, :, 0:R], _split(b3.t[0:R, :], 4, 128)[:, :, 0:R], _mid_bc(cmask.t[0:R, 0:R], 4), ALU.mult,
               [b3, cmask], [SC])
            for h in range(4):
                hc = slice(h * 128, (h + 1) * 128)
                mm(bank[5].t[:, hc], KL.t[0:RA, hc], Vb.t[0:RA, hc], True, True, [KL, Vb], [bank[5]])
                if R == 128:
                    mm(bank[6].t[:, hc], KLB.t[:, hc], Vb.t[:, hc], True, True, [KLB, Vb], [bank[6]])
            yield "A"
            for h in range(4):
                hc = slice(h * 128, (h + 1) * 128)
                mm(b4.t[0:R, hc], SC.t[0:R, h, 0:R], Vb.t[0:R, hc], h == 0, False, [SC, Vb], [b4], sg=True)
                mm(b4.t[0:R, hc], QDTA.t[:, h, 0:R], SbE.t[:, h, :], False, R < 128, [QDTA, SbE], [b4], sg=True)
            for h in range(4):
                hc = slice(h * 128, (h + 1) * 128)
                stt(Sf.t[:, h, :], Sf.t[:, h, :], smv("dS", 128, 2 * h, 2 * h + 1), bank[5].t[:, hc], ALU.mult, ALU.add,
                    [Sf, smb("dS"), bank[5]], [Sf])
            if R == 128:
                act(SbM.t[:], Sf.t[:], AF.Copy, [Sf], [SbM])
                for h in range(4):
                    hc = slice(h * 128, (h + 1) * 128)
                    mm(b4.t[0:R, hc], QDTB.t[:, h, :], SbM.t[:, h, :], False, True, [QDTB, SbM], [b4], sg=True)
                for h in range(4):
                    hc = slice(h * 128, (h + 1) * 128)
                    stt(Sf.t[:, h, :], Sf.t[:, h, :], smv("dS", 128, 2 * h + 1, 2 * h + 2), bank[6].t[:, hc], ALU.mult,
                        ALU.add, [Sf, smb("dS"), bank[6]], [Sf])
            act(SbE.t[:], Sf.t[:], AF.Copy, [Sf], [SbE])
            yield "A"
            if need_out:
                for h in range(4):
                    hc = slice(h * 128, (h + 1) * 128)
                    act(Tv(4, R)[:, hc], b4.t[0:R, hc], AF.Square, [b4], [Tb[4], smb("ssq4")],
                        accum=smv("ssq4", R, h, h + 1))
                rsqrt_small(("rs4", 0, 4), ("ssq4", 0, 4), R, 1.0 / 128.0, EPS)
                for h in range(4):
                    hc = slice(h * 128, (h + 1) * 128)
                    stt(Rb.t[0:R, hc], b4.t[0:R, hc], smv("rs4", R, h, h + 1), Tv(3, R)[:, hc], ALU.mult, ALU.mult,
                        [b4, smb("rs4"), Tb[3]], [Rb])
                for h in range(4):
                    tr(tbv(h * 128, R), Rb.t[0:R, h * 128:(h + 1) * 128], R, [Rb], [tbB])
                act(arT.t[:, 4:8, 0:R], t8[:, 0:4, 0:R], AF.Copy, [tbB], [arT])
            if taps and i == taps.get("_tile", 1) and l == taps.get("_layer", 0):
                tap("hgrn_o", b4, b4.t[0:R, :], [R, 512])
                tap("caug", caugB[i], caug.t[0:R, i, :], [R, 130])
                tap("qlT", qlT, qlT.t[:, :, :], [128, 8, 128])
                tap("kiT", kiTB[i], kiT.t[:, i * 128:(i + 1) * 128], [128, 128])
                tap("qiT", qiT, qiT.t[:, :, :], [128, 4, 128])
                tap("craw", craw, craw.t[:, :], [128, NTC])
                tap("gaT", gaT, gaT.t[:, :, :], [128, 4, 128])

            ck("l%dt%d_hgrn" % (l, i))
            yield "A_done"
            if not need_out:
                return
            use_thr = (i >= 1) and ((qb + 1) * 128 > ktop)
            if use_thr:
                nk = (qb + 1) * 128
                for b_ in sidxB:
                    b_.last_w = None
                    b_.readers = {}
                    P.alias(b_, [sidx])
                for kb0 in range(0, qb + 1, 2):
                    kbs = [kb for kb in (kb0, kb0 + 1) if kb <= qb]
                    for kb in kbs:
                        bkI = bank[kb % 2]
                        for h in range(4):
                            mm(bkI.t[:, h * 128:(h + 1) * 128], qiT.t[:, h, :],
                               kiT.t[:, (kb + 1) * 128:(kb + 2) * 128], True, True, [qiT, kiTB[kb + 1]], [bkI])
                    for kb in kbs:
                        bkI = bank[kb % 2]
                        act(itmpP[kb % 2].t[:, :, :], _split(bkI.t[:, :], 4, 128), AF.Relu, [bkI], [itmpP[kb % 2]])
                    for h in range(4):
                        for kb in kbs:
                            it_ = itmpP[kb % 2]
                            sv = sidx.t[:, kb * 128:(kb + 1) * 128]
                            sxb = sidxB[kb % 2]
                            if h == 0:
                                ts(sv, it_.t[:, 0, :], smv("wsgn", 128, 0, 1), ALU.mult, [it_, smb("wsgn")], [sxb])
                            else:
                                stt(sv, it_.t[:, h, :], smv("wsgn", 128, h, h + 1), sv, ALU.mult, ALU.add,
                                    [it_, smb("wsgn"), sxb], [sxb])
                    yield "idx"
                lw = [b_.last_w for b_ in sidxB if b_.last_w is not None]
                assert all(c[0] is lw[0][0] for c in lw)
                sidx.last_w = max(lw, key=lambda c: c[1])
                sidx.readers = {}
                yield "idx_done"
                sa = sidx.t[:, 0:nk]
                op("dve", lambda e, a=sa: e.tensor_reduce(out=smv("mx"), in_=a, axis=AX.X, op=ALU.max), [sidx], [smb("mx")])
                op("dve", lambda e, a=sa: e.tensor_reduce(out=smv("lo"), in_=a, axis=AX.X, op=ALU.min), [sidx], [smb("lo")])
                tt(smv("step"), smv("mx"), smv("lo"), ALU.subtract, [smb("mx"), smb("lo")], [smb("step")])
                dv = sidx.t[:, qb * 128:(qb + 1) * 128]
                tt(dv, dv, causneg.t[:], ALU.add, [sidx, causneg], [sidx])
                cD = int(nk * BIS_DVE_FRAC)
                thr_c = float(ktop) - 0.5 - (nk - cD) / 2.0
                stt(smv("mid"), smv("step"), 0.5, smv("lo"), ALU.mult, ALU.add, [smb("step"), smb("lo")], [smb("mid")])
                op("dve", lambda e, v_=thr_c: e.memset(smv("thrc"), v_), [], [smb("thrc")])
                for k in range(1, nit + 1):
                    f = 2.0 ** (-k)
                    act(_zs(jkA.t[:, 0:1], nk - cD), sidx.t[:, cD:nk], AF.Sign, [sidx, smb("mid")], [jkA, smb("sg")],
                        scale=-1.0, bias=smv("mid"), accum=smv("sg"))
                    ts(_zs(jkD.t[:, 0:1], cD), sidx.t[:, 0:cD], smv("mid"), ALU.is_ge, [sidx, smb("mid")], [jkD, smb("cnt")],
                       s2=0.0, op1=ALU.add, accum=smv("cnt"))
                    ts(smv("thrA"), smv("sg"), 0.5, ALU.mult, [smb("sg")], [smb("thrA")], s2=thr_c, op1=ALU.add)
                    stt(smv("base"), smv("step"), -0.5 * f, smv("mid"), ALU.mult, ALU.add, [smb("step"), smb("mid")],
                        [smb("base")])
                    ts(smv("fl"), smv("cnt"), smv("thrA"), ALU.is_ge, [smb("cnt"), smb("thrA")], [smb("fl")], s2=f, op1=ALU.mult)
                    stt(smv("mid"), smv("fl"), smv("step"), smv("base"), ALU.mult, ALU.add,
                        [smb("fl"), smb("step"), smb("base")], [smb("mid")])
                    yield "bis"
                stt(smv("lo"), smv("step"), -(2.0 ** (-nit - 1)), smv("mid"), ALU.mult, ALU.add, [smb("step"), smb("mid")],
                    [smb("lo")])
                ts(mball.t[:, 0:nk], sa, smv("lo"), ALU.is_lt, [sidx, smb("lo")], [mball])
            else:
                yield "idx_done"
            yield "bis_done"
            ck("l%dt%d_thr" % (l, i))
            if i == 0:
                kblocks = [(0, NMETA, (identb.t[0:NMETA, 0:NMETA], B0, NMETA), False, None)]
            else:
                if qb == 0:
                    kblocks = [(0, NMETA, (shiftm.t[:, :], B1, 128), False, None)]
                else:
                    kblocks = [(0, NMETA, None, False, None)]
                for kb in range(qb + 1):
                    if kb == qb:
                        bias = (identb.t[:, :], B0, 128)
                    elif kb == qb - 1:
                        bias = (identb.t[:, :], B1, 128)
                    else:
                        bias = None
                    kblocks.append((kb + 1, 128, bias, use_thr, kb))
            nblk = len(kblocks)
            def emit_pc(bi_, kt_, KR_, E_):
                for h in range(8):
                    mm(pc_ap(h, R), E_.t[0:KR_, h, 0:R], caug.t[0:KR_, kt_, 0:129], bi_ == 0 and h % 3 == 0, bi_ == nblk - 1,
                       [E_, caugB[kt_]], [bank[5 + h // 3]], sg=True)

            pend = None
            for bi, (kt, KR, bias, masked, kb) in enumerate(kblocks):
                E = Eb[bi % 2]
                if masked:
                    mb = maskb[bi % 2]
                    ts(mb.t[:, :], mball.t[:, kb * 128:(kb + 1) * 128], NEG, ALU.mult, [mball], [mb], eng="pool")
                for half in range(2):
                    bk = bank[3 + half]
                    lgv = _split(bk.t[0:KR, 0:4 * R], 4, R)
                    nacc = 1 + (2 if bias is not None else 0) + (1 if masked else 0)
                    na = [0]

                    def acc(lhsT, rhs, rd):
                        na[0] += 1
                        mm(bk.t[0:KR, 0:4 * R], lhsT, rhs, na[0] == 1, na[0] == nacc, rd, [bk])
                    acc(cT.t[:, kt * 128:kt * 128 + KR], qlT.t[:, 4 * half:4 * half + 4, 0:R], [cTB[kt], qlT])
                    if bias is not None:
                        bl, bt, bk_rows = bias
                        for hl in range(2):
                            acc(bl, bt.t[0:bk_rows, hl, 4 * half:4 * half + 4, 0:R], [bt, identb, shiftm])
                    if masked:
                        acc(mb.t[:, :], _mid_bc(identb.t[:, :], 4), [mb, identb])
                    act(E.t[0:KR, 4 * half:4 * half + 4, 0:R], lgv, AF.Exp, [bk], [E])
                if pend is not None:
                    emit_pc(*pend)
                pend = (bi, kt, KR, E)
                yield "post"
            emit_pc(*pend)
            for g in range(3):
                nh = 3 if g < 2 else 2
                dn = bank[5 + g].t[0:R, 0:nh * 129]
                dnv = bass.AP(dn.tensor, dn.offset + 128, [list(dn.ap[0]), [129, nh]])
                op("dve", lambda e, a=dnv, o_=smv("rden", R, 3 * g, 3 * g + nh): e.reciprocal(out=o_, in_=a),
                   [bank[5 + g]], [smb("rden")])
            for h in range(8):
                if h % 2 == 0:
                    act(olat.t[0:R, h, :], pc_ap(h, R, 128), AF.Copy, [bank[5 + h // 3], smb("rden")], [olat],
                        scale=smv("rden", R, h, h + 1))
                else:
                    ts(olat.t[0:R, h, :], pc_ap(h, R, 128), smv("rden", R, h, h + 1), ALU.mult,
                       [bank[5 + h // 3], smb("rden")], [olat])
            for h in range(8):
                tr(tbv(h * 128, R), olat.t[0:R, h, :], R, [olat], [tbB])
            act(olatT.t[:, :, 0:R], t8[:, :, 0:R], AF.Copy, [tbB], [olatT])
            yield "post"
            for j in range(4):
                mm(b0.t[:, j * 128:j * 128 + R], wuv.t[:, 2 * j, :], olatT.t[:, 2 * j, 0:R], True, False, [wuv, olatT], [b0])
                mm(b0.t[:, j * 128:j * 128 + R], wuv.t[:, 2 * j + 1, :], olatT.t[:, 2 * j + 1, 0:R], False, True,
                   [wuv, olatT], [b0])
            tt(arT.t[:, 0:4, 0:R], _split(b0.t[:, :], 4, 128)[:, :, 0:R], gaT.t[:, :, 0:R], ALU.mult, [b0, gaT], [arT])

            yield "post"
            ck("l%dt%d_attn" % (l, i))
            for half in range(2):
                bk = bank[half]
                for j in range(8):
                    mm(bk.t[0:R, :], arT.t[:, j, 0:R], wout.t[:, j, half * 512:(half + 1) * 512], j == 0, j == 7,
                       [arT, woutB[j]], [bk])
            for half in range(2):
                hv = hres.t[0:R, half * 512:(half + 1) * 512]
                stt(hv, hv, DN_ALPHA, bank[half].t[0:R, :], ALU.mult, ALU.add, [hres, bank[half]], [hres])
            stB = Buf("stats_", stats.t)
            for half in range(2):
                op("dve", lambda e, hf=half: e.bn_stats(out=stats.t[0:R, hf * 6:(hf + 1) * 6],
                                                        in_=hres.t[0:R, hf * 512:(hf + 1) * 512]), [hres], [stB])
            op("dve", lambda e: e.bn_aggr(out=smv("mv", R), in_=stats.t[0:R, :]), [stB], [smb("mv")])
            ts(smv("lnr", R), smv("mv", R, 1, 2), EPS, ALU.add, [smb("mv")], [smb("lnr")])
            act(smv("lnr", R), smv("lnr", R), AF.Ln, [smb("lnr")], [smb("lnr")])
            act(smv("lnr", R), smv("lnr", R), AF.Exp, [smb("lnr")], [smb("lnr")], scale=-0.5)
            ts(smv("lnm", R), smv("mv", R, 0, 1), -1.0, ALU.mult, [smb("mv"), smb("lnr")], [smb("lnm")],
               s2=smv("lnr", R), op1=ALU.mult)
            act(hres.t[0:R, :], hres.t[0:R, :], AF.Identity, [hres, smb("lnr"), smb("lnm")], [hres],
                scale=smv("lnr", R), bias=smv("lnm", R))
            tt(hres.t[0:R, :], hres.t[0:R, :], lnG.t[0:R, :], ALU.mult, [hres, lnG], [hres])
            tt(hres.t[0:R, :], hres.t[0:R, :], lnBt.t[0:R, :], ALU.add, [hres, lnBt], [hres])
            if last_layer:
                P.dma("pool", out_d[qb * 128:(qb + 1) * 128, :], hres.t[0:R, :], reads=[hres], is_output=True)
            else:
                P.dma("pool", h1_d[i * 128:i * 128 + R, :], hres.t[0:R, :], reads=[hres], writes=[h1B[i]])
        def step(g):
            try:
                return next(g)
            except StopIteration:
                return None

        def run_to(g, marker):
            while True:
                m = step(g)
                if m is None or m == marker:
                    return m

        gens = [tile_gen(i) for i in range(NTT)]
        run_to(gens[0], "A_done")
        run_to(gens[0], "bis_done")
        if NTT > 1:
            run_to(gens[1], "A_done")
        for i in range(NTT):
            s1 = gens[i]
            s2 = gens[i + 1] if i + 1 < NTT else None
            s3 = gens[i + 2] if i + 2 < NTT else None
            l1, l2, l3 = True, s2 is not None, s3 is not None
            if l3:
                step(s3)
            while l1 or l2 or l3:
                if l2:
                    m = step(s2)
                    if m is None or m == "bis_done":
                        l2 = False
                if l1:
                    for _ in range(SCHED_POST_STEPS):
                        if step(s1) is None:
                            l1 = False
                            break
                elif l3:
                    m = step(s3)
                    if m is None or m == "A_done":
                        l3 = False
    P.finish()
    return P, tap_out


def prep_shared(inp):
    w_in = np.asarray(inp["w_in"], np.float32)
    o = {"q": (0, 512), "c": (512, 640), "qi": (640, 896), "ki": (896, 960), "wi": (960, 964), "ga": (964, 1476),
         "qh": (1476, 1988), "fh": (1988, 2500), "ih": (2500, 3012), "gh": (3012, 3524)}
    order = ["q", "ki", "ki", "ga", "c", "wi", "qi", "qh", "fh", "ih", "gh"]
    w_perm = np.ascontiguousarray(np.concatenate([w_in[:, :, o[k][0]:o[k][1]] for k in order], axis=2))
    assert w_perm.shape[2] == NCOL
    w_uk = np.asarray(inp["w_uk"], np.float32)
    wuk_l = np.zeros((2, 128, 8, 128), np.float32)
    for h in range(8):
        wuk_l[:, (h % 2) * 64:(h % 2) * 64 + 64, h, :] = w_uk[:, h]
    w_uv = np.asarray(inp["w_uv"], np.float32)
    wuv_l = np.zeros((2, 128, 8, 128), np.float32)
    for h in range(8):
        wuv_l[:, :, h, (h % 2) * 64:(h % 2) * 64 + 64] = w_uv[:, h]
    bc = lambda a, n: np.ascontiguousarray(np.broadcast_to(a[:, None, :], (a.shape[0], n, a.shape[1])))
    ghn = np.tile(np.asarray(inp["hgrn_norm_g"], np.float32), (1, 4))
    rb = np.asarray(inp["rel_bias"], np.float32)
    lbraw = np.asarray(inp["hgrn_lb_raw"], np.float32).reshape(1, 1024)
    d = {
        "meta": np.ascontiguousarray(np.asarray(inp["meta_tokens"], np.float32)),
        "w_in": w_perm,
        "w_out": np.ascontiguousarray(np.asarray(inp["w_out"], np.float32)),
        "w_uk": wuk_l.reshape(2, 128, 1024),
        "w_uv": wuv_l.reshape(2, 128, 1024),
        "gkv": bc(np.asarray(inp["kv_norm_g"], np.float32), 128),
        "ghn": bc(ghn, 128),
        "lng": bc(np.asarray(inp["ln_g"], np.float32), 128),
        "lnb": bc(np.asarray(inp["ln_b"], np.float32), 128),
        "lbraw": np.ascontiguousarray(np.broadcast_to(lbraw, (128, 1024))),
        "rb": np.ascontiguousarray(rb),
        "rb31": np.ascontiguousarray(rb[31].reshape(8, 1)),
    }
    d.update(host_constants())
    return d


NT_FULL = 32
NIT = 16
BIS_DVE_FRAC = 0.5
SCHED_POST_STEPS = 1


def kernel(**inputs):
    shared = prep_shared(inputs)
    x = np.asarray(inputs["x"], np.float32)
    B = x.shape[0]
    nc = bass.Bass("TRN2", target_bir_lowering=False)
    build_program(nc, NT_FULL, layers=(0, 1), ktop=256, nit=NIT, final_layer=1)
    in_maps = []
    for b in range(B):
        m = dict(shared)
        m["x"] = np.ascontiguousarray(x[b])
        in_maps.append(m)
    res = run_bass_kernel_spmd(nc, in_maps, core_ids=list(range(B)))
    out = np.stack([np.asarray(r["out"], np.float32) for r in res.results], axis=0)
    return out
```

```python
from contextlib import ExitStack
import numpy as np
import concourse.bass as bass
import concourse.mybir as mybir
from concourse.bass_utils import run_bass_kernel_spmd

F32 = mybir.dt.float32
BF16 = mybir.dt.bfloat16
U8 = mybir.dt.uint8
AF = mybir.ActivationFunctionType
ALU = mybir.AluOpType
AX = mybir.AxisListType

NDS = 16


class _Sem:
    def __init__(self, h, name):
        self.h = h
        self.name = name


class Buf:
    def __init__(self, name, t=None):
        self.name = name
        self.t = t
        self.last_w = None
        self.readers = {}


class _Eng:
    def __init__(self, name, sem):
        self.name = name
        self.sem = sem
        self.n = 0
        self.seen = {}
        self.q = []
        self.ndma = 0
        self.dsems = []


class Prog:
    ENG = ("pe", "act", "dve", "pool", "sp")

    def __init__(self, nc):
        self.nc = nc
        self.es = ExitStack()
        self.eng = {}
        for n in self.ENG:
            s = _Sem(self.es.enter_context(nc.semaphore("s_" + n)), n)
            self.eng[n] = _Eng(n, s)
        for n in ("sp", "pool", "act"):
            self.eng[n].dsems = [_Sem(self.es.enter_context(nc.semaphore("d_%s_%d" % (n, i))), "d%s%d" % (n, i))
                                 for i in range(NDS)]
        self.out_clocks = []
        self.nwait = 0

    def sb(self, name, shape, dt):
        return Buf(name, self.es.enter_context(self.nc.sbuf_tensor("s_" + name, list(shape), dt)))

    def ps(self, name, shape, dt):
        return Buf(name, self.es.enter_context(self.nc.psum_tensor("p_" + name, list(shape), dt)))

    def view(self, name, t):
        return Buf(name, t)

    def _need(self, E, clock, strict_same):
        sem, val = clock
        if sem is E.sem and not strict_same:
            return
        if E.seen.get(sem, 0) >= val:
            return
        E.seen[sem] = val
        E.q.append(("w", sem, val))
        self.nwait += 1

    def _deps(self, E, reads, writes, is_dma=False):
        strict = is_dma or E.name != "pe"
        for b in reads:
            if b.last_w is not None:
                self._need(E, b.last_w, True)
        for b in writes:
            if b.last_w is not None:
                self._need(E, b.last_w, strict)
            for s, v in b.readers.items():
                self._need(E, (s, v), strict)

    def _mark(self, clock, reads, writes):
        sem, val = clock
        for b in writes:
            b.last_w = clock
            b.readers = {}
        for b in reads:
            if b.readers.get(sem, 0) < val:
                b.readers[sem] = val

    def op(self, eng, fn, reads=(), writes=()):
        E = self.eng[eng]
        self._deps(E, reads, writes)
        E.n += 1
        E.q.append(("o", fn))
        self._mark((E.sem, E.n), reads, writes)

    def dma(self, queue, out, in_, reads=(), writes=(), is_output=False):
        E = self.eng[queue]
        j = E.ndma
        E.ndma += 1
        ds = E.dsems[j % NDS]
        prev = 16 * (j // NDS)
        if prev > 0:
            self._need(E, (ds, prev), True)
        self._deps(E, reads, writes, is_dma=True)
        E.q.append(("d", out, in_, ds))
        clock = (ds, prev + 16)
        self._mark(clock, reads, writes)
        if is_output:
            self.out_clocks.append(clock)
        return clock

    def finish(self, final_eng="sp"):
        E = self.eng[final_eng]
        last = {}
        for s, v in self.out_clocks:
            if last.get(s, (None, 0))[1] < v:
                last[s] = (s, v)
        for s, v in last.values():
            E.q.append(("w", s, v))
        nc = self.nc
        engs = self.eng

        def replay(E, e):
            for it in E.q:
                k = it[0]
                if k == "w":
                    e.wait_ge(it[1].h, it[2])
                elif k == "o":
                    it[1](e).then_inc(E.sem.h, 1)
                else:
                    e.dma_start(out=it[1], in_=it[2]).then_inc(it[3].h, 16)

        with nc.Block() as block:
            @block.tensor
            def _(e):
                replay(engs["pe"], e)

            @block.scalar
            def _(e):
                replay(engs["act"], e)

            @block.vector
            def _(e):
                replay(engs["dve"], e)

            @block.gpsimd
            def _(e):
                replay(engs["pool"], e)

            @block.sync
            def _(e):
                replay(engs["sp"], e)
        self.es.close()

    def alias(self, dst, srcs):
        for s in srcs:
            if s.last_w is not None:
                c = s.last_w
                if dst.readers.get(c[0], 0) < c[1]:
                    dst.readers[c[0]] = c[1]
            for k, v in s.readers.items():
                if dst.readers.get(k, 0) < v:
                    dst.readers[k] = v


D = 1024
NMETA = 16
FQ, FK, FG, TC, TQ, TF, TI, TG = 0, 512, 640, 1152, 1540, 2052, 2564, 3076
NTC = 388
NCOL = 3588
DN_ALPHA = 4.0 ** 0.25
EPS = 1e-6
NEG = -30000.0


def _mid_bc(ap, n):
    a = [list(x) for x in ap.ap]
    return bass.AP(ap.tensor, ap.offset, [a[0], [0, n]] + a[1:])


def host_constants():
    c = {}
    idx = np.arange(128)
    c["identf"] = np.eye(128, dtype=np.float32)
    same = (idx[:, None] // 64) == (idx[None, :] // 64)
    c["m1t"] = (same & (idx[:, None] <= idx[None, :])).astype(np.float32)
    c["m3t"] = (same & (idx[:, None] > idx[None, :])).astype(np.float32)
    sel = np.zeros((128, 2), np.float32)
    sel[:64, 0] = 1.0
    sel[64:, 1] = 1.0
    c["sel"] = sel
    c["causneg"] = np.where(idx[None, :] <= idx[:, None], 0.0, -1e30).astype(np.float32)
    c["cmask"] = (same & (idx[:, None] <= idx[None, :])).astype(np.float32)
    c["j128"] = np.fliplr(np.eye(128, dtype=np.float32)).copy()
    sh = np.zeros((128, 16), np.float32)
    sh[112 + np.arange(16), np.arange(16)] = 1.0
    c["shiftm"] = sh
    d = np.arange(384) - 127
    n = np.maximum(d, 0)
    nf = np.maximum(n, 1).astype(np.float32)
    large = 16 + (np.log(nf / np.float32(16)) / np.float32(np.log(128 / 16)) * np.float32(16)).astype(np.int32)
    large = np.minimum(large, 31)
    bucket = np.where(n < 16, n, large)
    oh = np.zeros((32, 384), np.float32)
    for j in range(384):
        if d[j] >= 0:
            oh[bucket[j], j] = 1.0
    c["ohd"] = oh
    c["negrow"] = np.broadcast_to(np.where(d >= 0, 0.0, NEG).astype(np.float32)[None, :], (8, 384)).copy()
    return c


CONST_SHAPES = {"identf": [128, 128], "m1t": [128, 128], "m3t": [128, 128], "sel": [128, 2],
                "causneg": [128, 128], "cmask": [128, 128], "j128": [128, 128], "shiftm": [128, 16],
                "ohd": [32, 384], "negrow": [8, 384]}


def _resplit(ap, a, b):
    return bass.AP(ap.tensor, ap.offset, [list(ap.ap[0]), [b, a], [1, b]])


def _zs(ap, n):
    return bass.AP(ap.tensor, ap.offset, [list(ap.ap[0]), [0, n]])


def _split(ap, a, b):
    p = list(ap.ap[0])
    st = ap.ap[-1][0]
    return bass.AP(ap.tensor, ap.offset, [p, [b * st, a], [st, b]])


class _StopBuild(Exception):
    pass


def build_program(nc, NT, layers=(0, 1), ktop=256, nit=24, taps=None, final_layer=1, stop_at=None):
    try:
        return _build_program(nc, NT, layers, ktop, nit, taps, final_layer, stop_at)
    except _StopBuild as e:
        P = e.args[0]
        P.finish()
        return P, {}


def _build_program(nc, NT, layers, ktop, nit, taps, final_layer, stop_at):
    NTT = NT + 1
    S = NT * 128
    P = Prog(nc)
    tap_out = {}

    def ck(name):
        if stop_at == name:
            raise _StopBuild(P)

    def dr(name, shape, kind="ExternalInput"):
        return nc.dram_tensor(name, list(shape), F32, kind=kind).ap()

    x_d = dr("x", [S, D])
    meta_d = dr("meta", [NMETA, D])
    out_d = dr("out", [S, D], "ExternalOutput")
    win_d = dr("w_in", [2, D, NCOL])
    wout_d = dr("w_out", [2, D, D])
    wuk_d = dr("w_uk", [2, 128, 1024])
    wuv_d = dr("w_uv", [2, 128, 1024])
    gkv_d = dr("gkv", [2, 128, 128])
    ghn_d = dr("ghn", [2, 128, 512])
    lng_d = dr("lng", [2, 128, D])
    lnb_d = dr("lnb", [2, 128, D])
    lbraw_d = dr("lbraw", [128, 1024])
    rb_d = dr("rb", [32, 8])
    rb31_d = dr("rb31", [8, 1])
    const_d = {k: dr(k, v) for k, v in CONST_SHAPES.items()}
    h1_d = dr("h1s", [NTT * 128, D], "Internal")
    vd_d = dr("vds", [8, 384], "Internal")
    h1B = [Buf("h1_%d" % i) for i in range(NTT)]
    vdB = Buf("vd")

    def tap(name, buf, ap, shape):
        if taps is None or name not in taps:
            return
        d = dr("tap_" + name, shape, "ExternalOutput")
        tap_out[name] = shape
        P.dma("pool", d, ap, reads=[buf], is_output=True)

    sb = P.sb
    win = sb("win", [128, 8, NCOL], BF16)
    winB = [Buf("win%d" % k, win.t) for k in range(8)]
    wout = sb("wout", [128, 8, D], BF16)
    woutB = [Buf("wout%d" % k, wout.t) for k in range(8)]
    wuk = sb("wuk", [128, 8, 128], BF16)
    wuv = sb("wuv", [128, 8, 128], BF16)
    caug = sb("caug", [128, NTT, 130], BF16)
    caugB = [Buf("caug%d" % i, caug.t) for i in range(NTT)]
    cT = sb("cT", [128, NTT * 128], BF16)
    cTB = [Buf("cT%d" % i, cT.t) for i in range(NTT)]
    kiT = sb("kiT", [128, NTT * 128], BF16)
    kiTB = [Buf("kiT%d" % i, kiT.t) for i in range(NTT)]
    B0 = sb("B0", [128, 2, 8, 128], BF16)
    B1 = sb("B1", [128, 2, 8, 128], BF16)
    shiftm = sb("shiftm", [128, 16], BF16)
    gkvB = sb("gkvB", [128, 128], F32)
    ghnB = sb("ghnB", [128, 512], F32)
    lnG = sb("lnG", [128, D], F32)
    lnBt = sb("lnB", [128, D], F32)
    lbB = sb("lbB", [128, 512], F32)
    omlB = sb("omlB", [128, 512], F32)
    identb = sb("identb", [128, 128], BF16)
    hresP = [sb("hres%d" % k, [128, D], F32) for k in range(2)]
    hres = hresP[0]
    hb = sb("hb", [128, D], BF16)
    hT = sb("hT", [128, 8, 128], BF16)
    qT = sb("qT", [128, 4, 128], BF16)
    qlTP = [sb("qlT%d" % k, [128, 8, 128], BF16) for k in range(2)]
    qlT = qlTP[0]
    qis = sb("qis", [128, 256], BF16)
    qiT = sb("qiT", [128, 4, 128], BF16)
    gaTP = [sb("gaT%d" % k, [128, 4, 128], BF16) for k in range(2)]
    gaT = gaTP[0]
    craw = sb("craw", [128, NTC], F32)
    sm = sb("sm", [128, 64], F32)
    smB = {}

    def small(name, c0, n):
        smB[name] = (Buf("sm_" + name, sm.t), c0, n)

    small("wabs", 0, 4); small("wsgn", 4, 4); small("cs", 8, 1); small("crs", 9, 1)
    small("lo", 10, 1); small("step", 11, 1); small("mid", 12, 1); small("cnt", 13, 1); small("fl", 14, 1)
    small("mx", 15, 1); small("rden", 16, 8); small("ssq4", 24, 4); small("rs4", 28, 4)
    small("sg", 36, 1); small("thrA", 37, 1); small("thrc", 38, 1); small("base", 39, 1); small("mv", 32, 2); small("lnr", 34, 1); small("lnm", 35, 1); small("dS", 40, 8)

    def smv(name, R=128, a=None, b=None):
        bf, c0, n = smB[name]
        a = 0 if a is None else a
        b = n if b is None else b
        return sm.t[0:R, c0 + a:c0 + b]

    def smb(name):
        return smB[name][0]

    stats = sb("stats", [128, 12], F32)
    sidx = sb("sidx", [128, 4096], F32)
    stgB = [Buf("stg0", sidx.t), Buf("stg1", sidx.t)]
    itmp = sb("itmp", [128, 4, 128], F32)
    itmpP = [itmp, sb("itmp2", [128, 4, 128], F32)]
    sidxB = [Buf("sidxA"), Buf("sidxB")]
    mball = sb("mball", [128, 4096], U8)
    jkD = sb("jkD", [128, 8], U8)
    jkA = sb("jkA", [128, 8], mybir.dt.int8)
    _iv = itmp.t[0:8, 0:3, :]
    vs = Buf("vs", None)
    vs_ap = bass.AP(_iv.tensor, _iv.offset, [list(_iv.ap[0]), [1, 384]])
    maskb = [sb("maskb%d" % i, [128, 128], BF16) for i in range(2)]
    E2 = sb("E2", [128, 2, 8, 128], BF16)
    Eb = [Buf("E%d" % k, E2.t[:, k]) for k in range(2)]
    _e0 = E2.t[:, 0, 0, :]
    olat = sb("olat", [128, 8, 128], BF16)
    olatT = sb("olatT", [128, 8, 128], BF16)
    arTP = [sb("arT%d" % k, [128, 8, 128], BF16) for k in range(2)]
    arT = arTP[0]
    TT = sb("TT", [128, 5, 512], F32)
    Tb = [Buf("T%d" % k, TT.t) for k in range(5)]

    def Tv(k, R=128):
        return TT.t[0:R, k, :]
    Vb = sb("Vb", [128, 512], BF16)
    QD = sb("QD", [128, 512], BF16)
    KD = sb("KD", [128, 512], BF16)
    KL = sb("KL", [128, 512], BF16)
    KLB = sb("KLB", [128, 512], BF16)
    QDTA = sb("QDTA", [128, 4, 128], BF16)
    QDTB = sb("QDTB", [128, 4, 128], BF16)
    KDT = sb("KDT", [128, 4, 128], BF16)
    SC = sb("SC", [128, 4, 128], BF16)
    Rb = QD
    Sf = sb("Sf", [128, 4, 128], F32)
    SbE = sb("SbE", [128, 4, 128], BF16)
    SbM = sb("SbM", [128, 4, 128], BF16)
    rbs = sb("rbs", [32, 8], F32)
    rb31 = sb("rb31", [8, 1], F32)

    _e2f = bass.AP(_e0.tensor, _e0.offset, [list(_e0.ap[0]), [1, 2048]]).bitcast(F32)
    cviews = {"ohd": _e2f[0:32, 0:384], "negrow": _e2f[0:8, 384:768], "j128": _e2f[:, 768:896],
              "shiftm": _e2f[:, 896:912], "identf": itmp.t[:, 3, :]}
    cst = {}
    for k, shp in CONST_SHAPES.items():
        if k in cviews:
            cst[k] = Buf("c_" + k, cviews[k])
        else:
            cst[k] = sb("c_" + k, shp, F32)
        P.dma("sp", cst[k].t[:], const_d[k], writes=[cst[k]])
    identf, m1t, m3t, sel, causneg, cmask, j128 = (cst[k] for k in
                                                   ("identf", "m1t", "m3t", "sel", "causneg", "cmask", "j128"))
    bank = [P.ps("bk%d" % i, [128, 512], F32) if i != 2 else P.ps("bk2", [128, 1024], BF16) for i in range(8)]
    tbB = bank[2]
    tb = bank[2].t[:, :]

    def tbv(c0, n, R=128):
        return tb[0:R, c0:c0 + n]

    op = P.op

    def act(out, in_, func, reads, writes, scale=None, bias=None, accum=None, eng="act"):
        kw = {}
        if scale is not None:
            kw["scale"] = scale
        if bias is not None:
            kw["bias"] = bias
        if accum is not None:
            kw["accum_out"] = accum
        op(eng, lambda e: e.activation(out=out, in_=in_, func=func, **kw), reads, writes)

    def ts(out, in0, s1, op0, reads, writes, s2=None, op1=None, accum=None, eng="dve"):
        kw = {}
        if op1 is not None:
            kw["op1"] = op1
        if accum is not None:
            kw["accum_out"] = accum
        op(eng, lambda e: e.tensor_scalar(out=out, in0=in0, scalar1=s1, scalar2=s2, op0=op0, **kw), reads, writes)

    def tt(out, in0, in1, o, reads, writes, eng="dve"):
        op(eng, lambda e: e.tensor_tensor(out=out, in0=in0, in1=in1, op=o), reads, writes)

    def stt(out, in0, scalar, in1, op0, op1, reads, writes):
        op("dve", lambda e: e.scalar_tensor_tensor(out=out, in0=in0, scalar=scalar, in1=in1, op0=op0, op1=op1),
           reads, writes)

    def cp(out, in_, reads, writes, eng="dve"):
        op(eng, lambda e: e.tensor_copy(out=out, in_=in_), reads, writes)

    def mm(out, lhsT, rhs, start, stop, reads, writes, sg=False):
        op("pe", lambda e: e.matmul(out, lhsT=lhsT, rhs=rhs, start=start, stop=stop, skip_group_check=sg), reads, writes)

    def tr(out, in_, R, reads, writes):
        op("pe", lambda e: e.transpose(out=out, in_=in_, identity=identb.t[0:R, 0:R]), list(reads) + [identb], writes)

    def sigm(dst, src, src_bufs, dbuf):
        act(dst, src, AF.Exp, src_bufs, [dbuf], scale=-1.0)
        act(dst, dst, AF.Ln, [dbuf], [dbuf], bias=1.0)
        act(dst, dst, AF.Exp, [dbuf], [dbuf], scale=-1.0)

    def rsqrt_small(dst, src, R, mul, add):
        dv = smv(dst[0], R, dst[1], dst[2])
        sv = smv(src[0], R, src[1], src[2])
        ts(dv, sv, mul, ALU.mult, [smb(src[0])], [smb(dst[0])], s2=add, op1=ALU.add)
        act(dv, dv, AF.Ln, [smb(dst[0])], [smb(dst[0])])
        act(dv, dv, AF.Exp, [smb(dst[0])], [smb(dst[0])], scale=-0.5)

    cp(identb.t[:], identf.t[:], [identf], [identb])
    op("dve", lambda e: e.memset(caug.t[:, :, 128:130], 1.0), [], caugB)
    op("dve", lambda e: e.memset(QDTA.t[:], 0.0), [], [QDTA])
    op("dve", lambda e: e.memset(QDTB.t[:], 0.0), [], [QDTB])
    op("dve", lambda e: e.memset(qiT.t[:], 0.0), [], [qiT])
    op("dve", lambda e: e.memset(KLB.t[:], 0.0), [], [KLB])
    P.dma("sp", rbs.t[:], rb_d, writes=[rbs])
    P.dma("sp", rb31.t[:], rb31_d, writes=[rb31])
    ohd = cst["ohd"]
    mm(bank[0].t[0:8, 0:384], rbs.t[:], ohd.t[:], True, True, [rbs, ohd], [bank[0]])
    ts(vs_ap, bank[0].t[0:8, 0:384], rb31.t[:, 0:1], ALU.subtract, [bank[0], rb31], [vs])
    tt(vs_ap, vs_ap, cst["negrow"].t[:], ALU.add, [vs, cst["negrow"]], [vs])
    P.dma("sp", vd_d, vs_ap, reads=[vs], writes=[vdB])
    P.alias(itmp, [vs])
    P.dma("sp", _split(hres.t[:, :], 8, 128), bass.AP(vd_d.tensor, 0, [[1, 128], [384, 8], [1, 128]]),
          reads=[vdB], writes=[hres])
    P.dma("sp", _resplit(TT.t[:, 0:2, :], 8, 128),
          bass.AP(vd_d.tensor, 128, [[1, 128], [384, 8], [1, 128]]), reads=[vdB], writes=[Tb[0], Tb[1]])
    for half in range(2):
        mm(bank[half].t[:, :], j128.t[:], hres.t[:, half * 512:(half + 1) * 512], True, True, [j128, hres], [bank[half]])
        act(B0.t[:, 0, half * 4:(half + 1) * 4, :], _split(bank[half].t[:, :], 4, 128), AF.Copy, [bank[half]], [B0])
        tt(B0.t[:, 1, half * 4:(half + 1) * 4, :], _split(bank[half].t[:, :], 4, 128), B0.t[:, 0, half * 4:(half + 1) * 4, :],
           ALU.subtract, [bank[half], B0], [B0])
    for half in range(2):
        mm(bank[half].t[:, :], j128.t[:], TT.t[:, half, :], True, True, [j128, Tb[half]], [bank[half]])
        act(B1.t[:, 0, half * 4:(half + 1) * 4, :], _split(bank[half].t[:, :], 4, 128), AF.Copy, [bank[half]], [B1])
        tt(B1.t[:, 1, half * 4:(half + 1) * 4, :], _split(bank[half].t[:, :], 4, 128), B1.t[:, 0, half * 4:(half + 1) * 4, :],
           ALU.subtract, [bank[half], B1], [B1])
    cp(shiftm.t[:], cst["shiftm"].t[:], [cst["shiftm"]], [shiftm])
    P.dma("sp", TT.t[:, 2:4, :], _split(lbraw_d, 2, 512), writes=[Tb[2], Tb[3]])
    tt(Tv(4), Tv(2), Tv(3), ALU.subtract, [Tb[2], Tb[3]], [Tb[4]])
    act(Tv(4), Tv(4), AF.Exp, [Tb[4]], [Tb[4]])
    ts(Tv(4), Tv(4), 1.0, ALU.add, [Tb[4]], [Tb[4]])
    op("dve", lambda e: e.reciprocal(out=lbB.t[:], in_=Tv(4)), [Tb[4]], [lbB])
    ts(omlB.t[:], lbB.t[:], -1.0, ALU.mult, [lbB], [omlB], s2=1.0, op1=ALU.add)

    for b_ in Eb:
        P.alias(b_, [cst["ohd"], cst["negrow"], cst["j128"], cst["shiftm"]])
    P.alias(itmp, [identf, vs])
    ck("prologue")
    SCALE_Q = 0.125
    SCALE_I = 1.0 / 16.0
    SCALE_H = 128.0 ** -0.5

    castn = [0]

    def cast(out, in_, reads, writes):
        e = ("pool", "dve", "act")[castn[0] % 3]
        castn[0] += 1
        if e == "act":
            act(out, in_, AF.Copy, reads, writes)
        else:
            cp(out, in_, reads, writes, eng=e)

    def pc_ap(h, R, n=129):
        return bank[5 + h // 3].t[0:R, (h % 3) * 129:(h % 3) * 129 + n]

    for l in layers:
        for b_ in stgB:
            P.alias(b_, [sidx])
        HW = NCOL // 2
        n = 0
        for kc in range(8):
            for half in range(2):
                st = stgB[n % 2]
                so = (n % 2) * 2048
                n += 1
                P.dma("sp", sidx.t[:, so:so + HW], win_d[l, kc * 128:(kc + 1) * 128, half * HW:(half + 1) * HW], writes=[st])
                cast(win.t[:, kc, half * HW:(half + 1) * HW], sidx.t[:, so:so + HW], [st], [winB[kc]])
        for j in range(8):
            st = stgB[n % 2]
            so = (n % 2) * 2048
            n += 1
            P.dma("sp", sidx.t[:, so:so + D], wout_d[l, j * 128:(j + 1) * 128, :], writes=[st])
            cast(wout.t[:, j, :], sidx.t[:, so:so + D], [st], [woutB[j]])
        st = stgB[n % 2]; so = (n % 2) * 2048; n += 1
        P.dma("sp", sidx.t[:, so:so + 1024], wuk_d[l], writes=[st])
        cast(wuk.t[:, :, :], _split(sidx.t[:, so:so + 1024], 8, 128), [st], [wuk])
        st = stgB[n % 2]; so = (n % 2) * 2048; n += 1
        P.dma("sp", sidx.t[:, so:so + 1024], wuv_d[l], writes=[st])
        cast(wuv.t[:, :, :], _split(sidx.t[:, so:so + 1024], 8, 128), [st], [wuv])
        P.alias(sidx, stgB)
        P.dma("sp", gkvB.t[:], gkv_d[l], writes=[gkvB])
        P.dma("sp", ghnB.t[:], ghn_d[l], writes=[ghnB])
        P.dma("sp", lnG.t[:], lng_d[l], writes=[lnG])
        P.dma("sp", lnBt.t[:], lnb_d[l], writes=[lnBt])
        op("dve", lambda e: e.memset(Sf.t[:], 0.0), [], [Sf])
        op("dve", lambda e: e.memset(SbE.t[:], 0.0), [], [SbE])
        last_layer = (l == final_layer)
        ck("weights%d" % l)

        def tile_gen(i, l=l, last_layer=last_layer):
            R = NMETA if i == 0 else 128
            RA = min(R, 64)
            qb = i - 1
            need_out = not (last_layer and i == 0)
            hres, qlT, gaT, arT = hresP[i % 2], qlTP[i % 2], gaTP[i % 2], arTP[i % 2]
            if l == 0:
                src, srcB = (meta_d if i == 0 else x_d[qb * 128:(qb + 1) * 128, :]), []
            else:
                src, srcB = h1_d[i * 128:i * 128 + R, :], [h1B[i]]
            P.dma("pool", hb.t[0:R, :], src, reads=srcB, writes=[hb])
            yield "A0"
            for kc in range(8):
                tr(tbv(kc * 128, R), hb.t[0:R, kc * 128:(kc + 1) * 128], R, [hb], [tbB])
            act(hT.t[:, :, 0:R], _split(tb[:, :], 8, 128)[:, :, 0:R], AF.Copy, [tbB], [hT])

            ck("l%dt%d_load" % (l, i))
            def fm_group(bk, col0, nchunk):
                for j in range(nchunk):
                    for kc in range(8):
                        mm(bk.t[:, j * 128:j * 128 + R], win.t[:, kc, col0 + j * 128:col0 + (j + 1) * 128],
                           hT.t[:, kc, 0:R], kc == 0, kc == 7, [winB[kc], hT], [bk])

            def tm_group(bk, col0, ncol):
                for kc in range(8):
                    mm(bk.t[0:R, 0:ncol], hT.t[:, kc, 0:R], win.t[:, kc, col0:col0 + ncol], kc == 0, kc == 7,
                       [winB[kc], hT], [bk])

            yield "A"
            b0, b1 = bank[0], bank[1]
            bq = bank[0]
            fm_group(bq, FQ, 4)
            act(qT.t[:, :, 0:R], _split(bq.t[:, :], 4, 128)[:, :, 0:R], AF.Copy, [bq], [qT], scale=SCALE_Q)
            yield "A"
            bq = bank[1]
            fm_group(bq, FK, 1)
            act(kiT.t[:, i * 128:i * 128 + R], bq.t[:, 0:R], AF.Copy, [bq], [kiTB[i]])
            yield "A"
            bq = bank[3]
            fm_group(bq, FG, 4)
            g4v = _split(bq.t[:, :], 4, 128)[:, :, 0:R]
            t4v = _split(TT.t[:, 4, :], 4, 128)[:, :, 0:R]
            sigm(t4v, g4v, [bq], Tb[4])
            tt(gaT.t[:, :, 0:R], g4v, t4v, ALU.mult, [bq, Tb[4]], [gaT])
            yield "A"
            bq = bank[4]
            tm_group(bq, TC, NTC)
            act(craw.t[0:R, :], bq.t[0:R, 0:NTC], AF.Copy, [bq], [craw])
            yield "A"
            bq = bank[5]
            tm_group(bq, TQ, 512)
            sigm(Tv(0, R), bq.t[0:R, :], [bq], Tb[0])
            stt(Tv(0, R), bq.t[0:R, :], SCALE_H, Tv(0, R), ALU.mult, ALU.mult, [bq, Tb[0]], [Tb[0]])
            yield "A"
            b1 = bank[6]
            tm_group(b1, TF, 512)
            act(Tv(1, R), b1.t[0:R, :], AF.Exp, [b1], [Tb[1]], scale=-1.0)
            act(Tv(2, R), Tv(1, R), AF.Ln, [Tb[1]], [Tb[2]], bias=1.0)
            act(Tv(1, R), Tv(2, R), AF.Exp, [Tb[2]], [Tb[1]], scale=-1.0)
            gs = -1.0
            if l > 0:
                gs = 1.0
                tt(Tv(1, R), Tv(1, R), omlB.t[0:R, :], ALU.mult, [Tb[1], omlB], [Tb[1]])
                tt(Tv(1, R), Tv(1, R), lbB.t[0:R, :], ALU.add, [Tb[1], lbB], [Tb[1]])
                act(Tv(2, R), Tv(1, R), AF.Ln, [Tb[1]], [Tb[2]])
            ts(Tv(1, R), Tv(1, R), -1.0, ALU.mult, [Tb[1]], [Tb[1]], s2=1.0, op1=ALU.add)
            yield "A"
            bq = bank[7]
            tm_group(bq, TI, 512)
            act(Vb.t[0:R, :], bq.t[0:R, :], AF.Copy, [bq], [Vb])
            yield "A"
            b1 = bank[0]
            tm_group(b1, TG, 512)
            sigm(Tv(3, R), b1.t[0:R, :], [b1], Tb[3])
            tt(Tv(3, R), b1.t[0:R, :], Tv(3, R), ALU.mult, [b1, Tb[3]], [Tb[3]])
            tt(Tv(3, R), Tv(3, R), ghnB.t[0:R, :], ALU.mult, [Tb[3], ghnB], [Tb[3]])

            b0, b1 = bank[0], bank[1]
            P.dma("sp", hres.t[0:R, :], src, reads=srcB, writes=[hres])
            yield "A"
            ck("l%dt%d_inproj" % (l, i))
            act(Tv(4, R)[:, 0:128], craw.t[0:R, 0:128], AF.Square, [craw], [Tb[4], smb("cs")], accum=smv("cs", R))
            rsqrt_small(("crs", 0, 1), ("cs", 0, 1), R, 1.0 / 128.0, EPS)
            stt(caug.t[0:R, i, 0:128], craw.t[0:R, 0:128], smv("crs", R), gkvB.t[0:R, :], ALU.mult, ALU.mult,
                [craw, smb("crs"), gkvB], [caugB[i]])
            tr(tbv(0, R), caug.t[0:R, i, 0:128], R, [caugB[i]], [tbB])
            ck("l%dt%d_c0" % (l, i))
            cp(cT.t[:, i * 128:i * 128 + R], tbv(0, R), [tbB], [cTB[i]])
            ck("l%dt%d_c" % (l, i))
            if i >= 1:
                wv = craw.t[0:R, 128:132]
                ts(smv("wsgn", R), wv, 0.0, ALU.is_ge, [craw], [smb("wsgn")], s2=2.0, op1=ALU.mult)
                ts(smv("wsgn", R), smv("wsgn", R), -1.0, ALU.add, [smb("wsgn")], [smb("wsgn")])
                tt(smv("wabs", R), wv, smv("wsgn", R), ALU.mult, [craw, smb("wsgn")], [smb("wabs")])
                ts(smv("wabs", R), smv("wabs", R), SCALE_I, ALU.mult, [smb("wabs")], [smb("wabs")])
                wb = smv("wabs", R)
                wbc = bass.AP(wb.tensor, wb.offset, [list(wb.ap[0]), [1, 4], [0, 64]])
                tt(_split(qis.t[0:R, :], 4, 64), _split(craw.t[0:R, 132:388], 4, 64), wbc, ALU.mult,
                   [craw, smb("wabs")], [qis])
                for j in range(2):
                    tr(tbv(j * 128, R), qis.t[0:R, j * 128:(j + 1) * 128], R, [qis], [tbB])
                for h in range(4):
                    pb = (h % 2) * 64
                    cp(qiT.t[pb:pb + 64, h, 0:R], tb[pb:pb + 64, (h // 2) * 128:(h // 2) * 128 + R], [tbB], [qiT])
            for h in range(8):
                bk = bank[3 + h // 4]
                mm(bk.t[:, (h % 4) * 128:(h % 4) * 128 + R], wuk.t[:, h, :], qT.t[:, h // 2, 0:R],
                   True, True, [wuk, qT], [bk])
            ck("l%dt%d_ql" % (l, i))
            act(qlT.t[:, 0:4, 0:R], _split(bank[3].t[:, :], 4, 128)[:, :, 0:R], AF.Copy, [bank[3]], [qlT])
            cp(qlT.t[:, 4:8, 0:R], _split(bank[4].t[:, :], 4, 128)[:, :, 0:R], [bank[4]], [qlT])

            yield "A"
            ck("l%dt%d_prep" % (l, i))
            mm(b0.t[0:R, :], m1t.t[0:R, 0:R], Tv(2, R), True, True, [m1t, Tb[2]], [b0])
            mm(b1.t[0:R, :], m3t.t[0:R, 0:R], Tv(2, R), True, True, [m3t, Tb[2]], [b1])
            for h in range(4):
                mm(bank[7].t[:, 2 * h:2 * h + 2], TT.t[0:R, 2, h * 128:(h + 1) * 128], sel.t[0:R, :], True, True,
                   [Tb[2], sel], [bank[7]])
            act(smv("dS"), bank[7].t[:, 0:8], AF.Exp, [bank[7]], [smb("dS")], scale=gs)
            act(Tv(4, R), b0.t[0:R, :], AF.Exp, [b0], [Tb[4]], scale=gs)
            tt(QD.t[0:R, :], Tv(0, R), Tv(4, R), ALU.mult, [Tb[0], Tb[4]], [QD])
            act(Tv(4, R), b0.t[0:R, :], AF.Exp, [b0], [Tb[4]], scale=-gs)
            tt(KD.t[0:R, :], Tv(1, R), Tv(4, R), ALU.mult, [Tb[1], Tb[4]], [KD])
            act(Tv(4, R), b1.t[0:R, :], AF.Exp, [b1], [Tb[4]], scale=gs)
            tt(KL.t[0:RA, :], Tv(1, RA), Tv(4, RA), ALU.mult, [Tb[1], Tb[4]], [KL])
            if R == 128:
                tt(KLB.t[64:128, :], TT.t[64:128, 1, :], TT.t[64:128, 4, :], ALU.mult, [Tb[1], Tb[4]], [KLB])
            yield "A"
            for h in range(4):
                tr(tbv(h * 128, R), QD.t[0:R, h * 128:(h + 1) * 128], R, [QD], [tbB])
            for h in range(4):
                tr(tbv(512 + h * 128, R), KD.t[0:R, h * 128:(h + 1) * 128], R, [KD], [tbB])
            t8 = _split(tb[:, :], 8, 128)
            act(QDTA.t[:, :, 0:RA], t8[:, 0:4, 0:RA], AF.Copy, [tbB], [QDTA])
            if R == 128:
                cp(QDTB.t[:, :, 64:128], t8[:, 0:4, 64:128], [tbB], [QDTB])
            act(KDT.t[:, :, 0:R], t8[:, 4:8, 0:R], AF.Copy, [tbB], [KDT])
            yield "A"
            b3, b4 = bank[3], bank[4]
            for h in range(4):
                mm(b3.t[0:R, h * 128:h * 128 + RA], KDT.t[:, h, 0:R], QDTA.t[:, h, 0:RA], True, True, [KDT, QDTA], [b3])
                if R == 128:
                    mm(b3.t[0:R, h * 128 + 64:h * 128 + 128], KDT.t[:, h, 0:R], QDTB.t[:, h, 64:128], True, True,
                       [KDT, QDTB], [b3])
            tt(SC.t[0:R, :, 0:R], _split(b3.t[0:R, :], 4, 128)[:, :, 0:R], _mid_bc(cmask.t[0:R, 0:R], 4), ALU.mult,
               [b3, cmask], [SC])
            for h in range(4):
                hc = slice(h * 128, (h + 1) * 128)
                mm(bank[5].t[:, hc], KL.t[0:RA, hc], Vb.t[0:RA, hc], True, True, [KL, Vb], [bank[5]])
                if R == 128:
                    mm(bank[6].t[:, hc], KLB.t[:, hc], Vb.t[:, hc], True, True, [KLB, Vb], [bank[6]])
            yield "A"
            for h in range(4):
                hc = slice(h * 128, (h + 1) * 128)
                mm(b4.t[0:R, hc], SC.t[0:R, h, 0:R], Vb.t[0:R, hc], h == 0, False, [SC, Vb], [b4], sg=True)
                mm(b4.t[0:R, hc], QDTA.t[:, h, 0:R], SbE.t[:, h, :], False, R < 128, [QDTA, SbE], [b4], sg=True)
            for h in range(4):
                hc = slice(h * 128, (h + 1) * 128)
                stt(Sf.t[:, h, :], Sf.t[:, h, :], smv("dS", 128, 2 * h, 2 * h + 1), bank[5].t[:, hc], ALU.mult, ALU.add,
                    [Sf, smb("dS"), bank[5]], [Sf])
            if R == 128:
                cp(SbM.t[:], Sf.t[:], [Sf], [SbM])
                for h in range(4):
                    hc = slice(h * 128, (h + 1) * 128)
                    mm(b4.t[0:R, hc], QDTB.t[:, h, :], SbM.t[:, h, :], False, True, [QDTB, SbM], [b4], sg=True)
                for h in range(4):
                    hc = slice(h * 128, (h + 1) * 128)
                    stt(Sf.t[:, h, :], Sf.t[:, h, :], smv("dS", 128, 2 * h + 1, 2 * h + 2), bank[6].t[:, hc], ALU.mult,
                        ALU.add, [Sf, smb("dS"), bank[6]], [Sf])
            cp(SbE.t[:], Sf.t[:], [Sf], [SbE])
            yield "A"
            if need_out:
                for h in range(4):
                    hc = slice(h * 128, (h + 1) * 128)
                    act(Tv(4, R)[:, hc], b4.t[0:R, hc], AF.Square, [b4], [Tb[4], smb("ssq4")],
                        accum=smv("ssq4", R, h, h + 1))
                rsqrt_small(("rs4", 0, 4), ("ssq4", 0, 4), R, 1.0 / 128.0, EPS)
                for h in range(4):
                    hc = slice(h * 128, (h + 1) * 128)
                    stt(Rb.t[0:R, hc], b4.t[0:R, hc], smv("rs4", R, h, h + 1), Tv(3, R)[:, hc], ALU.mult, ALU.mult,
                        [b4, smb("rs4"), Tb[3]], [Rb])
                for h in range(4):
                    tr(tbv(h * 128, R), Rb.t[0:R, h * 128:(h + 1) * 128], R, [Rb], [tbB])
                act(arT.t[:, 4:8, 0:R], t8[:, 0:4, 0:R], AF.Copy, [tbB], [arT])
            if taps and i == taps.get("_tile", 1) and l == taps.get("_layer", 0):
                tap("hgrn_o", b4, b4.t[0:R, :], [R, 512])
                tap("caug", caugB[i], caug.t[0:R, i, :], [R, 130])
                tap("qlT", qlT, qlT.t[:, :, :], [128, 8, 128])
                tap("kiT", kiTB[i], kiT.t[:, i * 128:(i + 1) * 128], [128, 128])
                tap("qiT", qiT, qiT.t[:, :, :], [128, 4, 128])
                tap("craw", craw, craw.t[:, :], [128, NTC])
                tap("gaT", gaT, gaT.t[:, :, :], [128, 4, 128])

            ck("l%dt%d_hgrn" % (l, i))
            yield "A_done"
            if not need_out:
                return
            use_thr = (i >= 1) and ((qb + 1) * 128 > ktop)
            if use_thr:
                nk = (qb + 1) * 128
                for b_ in sidxB:
                    b_.last_w = None
                    b_.readers = {}
                    P.alias(b_, [sidx])
                for kb0 in range(0, qb + 1, 2):
                    kbs = [kb for kb in (kb0, kb0 + 1) if kb <= qb]
                    for kb in kbs:
                        bkI = bank[kb % 2]
                        for h in range(4):
                            mm(bkI.t[:, h * 128:(h + 1) * 128], qiT.t[:, h, :],
                               kiT.t[:, (kb + 1) * 128:(kb + 2) * 128], True, True, [qiT, kiTB[kb + 1]], [bkI])
                    for kb in kbs:
                        bkI = bank[kb % 2]
                        act(itmpP[kb % 2].t[:, :, :], _split(bkI.t[:, :], 4, 128), AF.Relu, [bkI], [itmpP[kb % 2]])
                    for h in range(4):
                        for kb in kbs:
                            it_ = itmpP[kb % 2]
                            sv = sidx.t[:, kb * 128:(kb + 1) * 128]
                            sxb = sidxB[kb % 2]
                            if h == 0:
                                ts(sv, it_.t[:, 0, :], smv("wsgn", 128, 0, 1), ALU.mult, [it_, smb("wsgn")], [sxb])
                            else:
                                stt(sv, it_.t[:, h, :], smv("wsgn", 128, h, h + 1), sv, ALU.mult, ALU.add,
                                    [it_, smb("wsgn"), sxb], [sxb])
                    yield "idx"
                lw = [b_.last_w for b_ in sidxB if b_.last_w is not None]
                assert all(c[0] is lw[0][0] for c in lw)
                sidx.last_w = max(lw, key=lambda c: c[1])
                sidx.readers = {}
                yield "idx_done"
                sa = sidx.t[:, 0:nk]
                op("dve", lambda e, a=sa: e.tensor_reduce(out=smv("mx"), in_=a, axis=AX.X, op=ALU.max), [sidx], [smb("mx")])
                op("dve", lambda e, a=sa: e.tensor_reduce(out=smv("lo"), in_=a, axis=AX.X, op=ALU.min), [sidx], [smb("lo")])
                tt(smv("step"), smv("mx"), smv("lo"), ALU.subtract, [smb("mx"), smb("lo")], [smb("step")])
                dv = sidx.t[:, qb * 128:(qb + 1) * 128]
                tt(dv, dv, causneg.t[:], ALU.add, [sidx, causneg], [sidx])
                cD = int(nk * BIS_DVE_FRAC)
                thr_c = float(ktop) - 0.5 - (nk - cD) / 2.0
                stt(smv("mid"), smv("step"), 0.5, smv("lo"), ALU.mult, ALU.add, [smb("step"), smb("lo")], [smb("mid")])
                op("dve", lambda e, v_=thr_c: e.memset(smv("thrc"), v_), [], [smb("thrc")])
                for k in range(1, nit + 1):
                    f = 2.0 ** (-k)
                    act(_zs(jkA.t[:, 0:1], nk - cD), sidx.t[:, cD:nk], AF.Sign, [sidx, smb("mid")], [jkA, smb("sg")],
                        scale=-1.0, bias=smv("mid"), accum=smv("sg"))
                    ts(_zs(jkD.t[:, 0:1], cD), sidx.t[:, 0:cD], smv("mid"), ALU.is_ge, [sidx, smb("mid")], [jkD, smb("cnt")],
                       s2=0.0, op1=ALU.add, accum=smv("cnt"))
                    ts(smv("thrA"), smv("sg"), 0.5, ALU.mult, [smb("sg")], [smb("thrA")], s2=thr_c, op1=ALU.add)
                    stt(smv("base"), smv("step"), -0.5 * f, smv("mid"), ALU.mult, ALU.add, [smb("step"), smb("mid")],
                        [smb("base")])
                    ts(smv("fl"), smv("cnt"), smv("thrA"), ALU.is_ge, [smb("cnt"), smb("thrA")], [smb("fl")], s2=f, op1=ALU.mult)
                    stt(smv("mid"), smv("fl"), smv("step"), smv("base"), ALU.mult, ALU.add,
                        [smb("fl"), smb("step"), smb("base")], [smb("mid")])
                    yield "bis"
                stt(smv("lo"), smv("step"), -(2.0 ** (-nit - 1)), smv("mid"), ALU.mult, ALU.add, [smb("step"), smb("mid")],
                    [smb("lo")])
                ts(mball.t[:, 0:nk], sa, smv("lo"), ALU.is_lt, [sidx, smb("lo")], [mball])
            else:
                yield "idx_done"
            yield "bis_done"
            ck("l%dt%d_thr" % (l, i))
            if i == 0:
                kblocks = [(0, NMETA, (identb.t[0:NMETA, 0:NMETA], B0, NMETA), False, None)]
            else:
                if qb == 0:
                    kblocks = [(0, NMETA, (shiftm.t[:, :], B1, 128), False, None)]
                else:
                    kblocks = [(0, NMETA, None, False, None)]
                for kb in range(qb + 1):
                    if kb == qb:
                        bias = (identb.t[:, :], B0, 128)
                    elif kb == qb - 1:
                        bias = (identb.t[:, :], B1, 128)
                    else:
                        bias = None
                    kblocks.append((kb + 1, 128, bias, use_thr, kb))
            nblk = len(kblocks)
            def emit_pc(bi_, kt_, KR_, E_):
                for h in range(8):
                    mm(pc_ap(h, R), E_.t[0:KR_, h, 0:R], caug.t[0:KR_, kt_, 0:129], bi_ == 0 and h % 3 == 0, bi_ == nblk - 1,
                       [E_, caugB[kt_]], [bank[5 + h // 3]], sg=True)

            pend = None
            for bi, (kt, KR, bias, masked, kb) in enumerate(kblocks):
                E = Eb[bi % 2]
                if masked:
                    mb = maskb[bi % 2]
                    ts(mb.t[:, :], mball.t[:, kb * 128:(kb + 1) * 128], NEG, ALU.mult, [mball], [mb], eng="pool")
                for half in range(2):
                    bk = bank[3 + half]
                    lgv = _split(bk.t[0:KR, 0:4 * R], 4, R)
                    nacc = 1 + (2 if bias is not None else 0) + (1 if masked else 0)
                    na = [0]

                    def acc(lhsT, rhs, rd):
                        na[0] += 1
                        mm(bk.t[0:KR, 0:4 * R], lhsT, rhs, na[0] == 1, na[0] == nacc, rd, [bk])
                    acc(cT.t[:, kt * 128:kt * 128 + KR], qlT.t[:, 4 * half:4 * half + 4, 0:R], [cTB[kt], qlT])
                    if bias is not None:
                        bl, bt, bk_rows = bias
                        for hl in range(2):
                            acc(bl, bt.t[0:bk_rows, hl, 4 * half:4 * half + 4, 0:R], [bt, identb, shiftm])
                    if masked:
                        acc(mb.t[:, :], _mid_bc(identb.t[:, :], 4), [mb, identb])
                    act(E.t[0:KR, 4 * half:4 * half + 4, 0:R], lgv, AF.Exp, [bk], [E])
                if pend is not None:
                    emit_pc(*pend)
                pend = (bi, kt, KR, E)
                yield "post"
            emit_pc(*pend)
            for g in range(3):
                nh = 3 if g < 2 else 2
                dn = bank[5 + g].t[0:R, 0:nh * 129]
                dnv = bass.AP(dn.tensor, dn.offset + 128, [list(dn.ap[0]), [129, nh]])
                op("dve", lambda e, a=dnv, o_=smv("rden", R, 3 * g, 3 * g + nh): e.reciprocal(out=o_, in_=a),
                   [bank[5 + g]], [smb("rden")])
            for h in range(8):
                if h % 2 == 0:
                    act(olat.t[0:R, h, :], pc_ap(h, R, 128), AF.Copy, [bank[5 + h // 3], smb("rden")], [olat],
                        scale=smv("rden", R, h, h + 1))
                else:
                    ts(olat.t[0:R, h, :], pc_ap(h, R, 128), smv("rden", R, h, h + 1), ALU.mult,
                       [bank[5 + h // 3], smb("rden")], [olat])
            for h in range(8):
                tr(tbv(h * 128, R), olat.t[0:R, h, :], R, [olat], [tbB])
            act(olatT.t[:, :, 0:R], t8[:, :, 0:R], AF.Copy, [tbB], [olatT])
            yield "post"
            for j in range(4):
                mm(b0.t[:, j * 128:j * 128 + R], wuv.t[:, 2 * j, :], olatT.t[:, 2 * j, 0:R], True, False, [wuv, olatT], [b0])
                mm(b0.t[:, j * 128:j * 128 + R], wuv.t[:, 2 * j + 1, :], olatT.t[:, 2 * j + 1, 0:R], False, True,
                   [wuv, olatT], [b0])
            tt(arT.t[:, 0:4, 0:R], _split(b0.t[:, :], 4, 128)[:, :, 0:R], gaT.t[:, :, 0:R], ALU.mult, [b0, gaT], [arT])

            yield "post"
            ck("l%dt%d_attn" % (l, i))
            for half in range(2):
                bk = bank[half]
                for j in range(8):
                    mm(bk.t[0:R, :], arT.t[:, j, 0:R], wout.t[:, j, half * 512:(half + 1) * 512], j == 0, j == 7,
                       [arT, woutB[j]], [bk])
            for half in range(2):
                hv = hres.t[0:R, half * 512:(half + 1) * 512]
                stt(hv, hv, DN_ALPHA, bank[half].t[0:R, :], ALU.mult, ALU.add, [hres, bank[half]], [hres])
            stB = Buf("stats_", stats.t)
            for half in range(2):
                op("dve", lambda e, hf=half: e.bn_stats(out=stats.t[0:R, hf * 6:(hf + 1) * 6],
                                                        in_=hres.t[0:R, hf * 512:(hf + 1) * 512]), [hres], [stB])
            op("dve", lambda e: e.bn_aggr(out=smv("mv", R), in_=stats.t[0:R, :]), [stB], [smb("mv")])
            ts(smv("lnr", R), smv("mv", R, 1, 2), EPS, ALU.add, [smb("mv")], [smb("lnr")])
            act(smv("lnr", R), smv("lnr", R), AF.Ln, [smb("lnr")], [smb("lnr")])
            act(smv("lnr", R), smv("lnr", R), AF.Exp, [smb("lnr")], [smb("lnr")], scale=-0.5)
            stt(hres.t[0:R, :], hres.t[0:R, :], smv("mv", R, 0, 1), lnG.t[0:R, :], ALU.subtract, ALU.mult,
                [hres, smb("mv"), lnG], [hres])
            stt(hres.t[0:R, :], hres.t[0:R, :], smv("lnr", R), lnBt.t[0:R, :], ALU.mult, ALU.add,
                [hres, smb("lnr"), lnBt], [hres])
            if last_layer:
                P.dma("pool", out_d[qb * 128:(qb + 1) * 128, :], hres.t[0:R, :], reads=[hres], is_output=True)
            else:
                P.dma("pool", h1_d[i * 128:i * 128 + R, :], hres.t[0:R, :], reads=[hres], writes=[h1B[i]])
        def step(g):
            try:
                return next(g)
            except StopIteration:
                return None

        def run_to(g, marker):
            while True:
                m = step(g)
                if m is None or m == marker:
                    return m

        gens = [tile_gen(i) for i in range(NTT)]
        run_to(gens[0], "A_done")
        run_to(gens[0], "bis_done")
        if NTT > 1:
            run_to(gens[1], "A_done")
        for i in range(NTT):
            s1 = gens[i]
            s2 = gens[i + 1] if i + 1 < NTT else None
            s3 = gens[i + 2] if i + 2 < NTT else None
            l1, l2, l3 = True, s2 is not None, s3 is not None
            if l3:
                step(s3)
            while l1 or l2 or l3:
                if l2:
                    m = step(s2)
                    if m is None or m == "bis_done":
                        l2 = False
                if l1:
                    for _ in range(SCHED_POST_STEPS):
                        if step(s1) is None:
                            l1 = False
                            break
                elif l3:
                    m = step(s3)
                    if m is None or m == "A_done":
                        l3 = False
    P.finish()
    return P, tap_out


def prep_shared(inp):
    w_in = np.asarray(inp["w_in"], np.float32)
    o = {"q": (0, 512), "c": (512, 640), "qi": (640, 896), "ki": (896, 960), "wi": (960, 964), "ga": (964, 1476),
         "qh": (1476, 1988), "fh": (1988, 2500), "ih": (2500, 3012), "gh": (3012, 3524)}
    order = ["q", "ki", "ki", "ga", "c", "wi", "qi", "qh", "fh", "ih", "gh"]
    w_perm = np.ascontiguousarray(np.concatenate([w_in[:, :, o[k][0]:o[k][1]] for k in order], axis=2))
    assert w_perm.shape[2] == NCOL
    w_uk = np.asarray(inp["w_uk"], np.float32)
    wuk_l = np.zeros((2, 128, 8, 128), np.float32)
    for h in range(8):
        wuk_l[:, (h % 2) * 64:(h % 2) * 64 + 64, h, :] = w_uk[:, h]
    w_uv = np.asarray(inp["w_uv"], np.float32)
    wuv_l = np.zeros((2, 128, 8, 128), np.float32)
    for h in range(8):
        wuv_l[:, :, h, (h % 2) * 64:(h % 2) * 64 + 64] = w_uv[:, h]
    bc = lambda a, n: np.ascontiguousarray(np.broadcast_to(a[:, None, :], (a.shape[0], n, a.shape[1])))
    ghn = np.tile(np.asarray(inp["hgrn_norm_g"], np.float32), (1, 4))
    rb = np.asarray(inp["rel_bias"], np.float32)
    lbraw = np.asarray(inp["hgrn_lb_raw"], np.float32).reshape(1, 1024)
    d = {
        "meta": np.ascontiguousarray(np.asarray(inp["meta_tokens"], np.float32)),
        "w_in": w_perm,
        "w_out": np.ascontiguousarray(np.asarray(inp["w_out"], np.float32)),
        "w_uk": wuk_l.reshape(2, 128, 1024),
        "w_uv": wuv_l.reshape(2, 128, 1024),
        "gkv": bc(np.asarray(inp["kv_norm_g"], np.float32), 128),
        "ghn": bc(ghn, 128),
        "lng": bc(np.asarray(inp["ln_g"], np.float32), 128),
        "lnb": bc(np.asarray(inp["ln_b"], np.float32), 128),
        "lbraw": np.ascontiguousarray(np.broadcast_to(lbraw, (128, 1024))),
        "rb": np.ascontiguousarray(rb),
        "rb31": np.ascontiguousarray(rb[31].reshape(8, 1)),
    }
    d.update(host_constants())
    return d


NT_FULL = 32
NIT = 16
BIS_DVE_FRAC = 0.55
SCHED_POST_STEPS = 1


def kernel(**inputs):
    shared = prep_shared(inputs)
    x = np.asarray(inputs["x"], np.float32)
    B = x.shape[0]
    nc = bass.Bass("TRN2", target_bir_lowering=False)
    build_program(nc, NT_FULL, layers=(0, 1), ktop=256, nit=NIT, final_layer=1)
    in_maps = []
    for b in range(B):
        m = dict(shared)
        m["x"] = np.ascontiguousarray(x[b])
        in_maps.append(m)
    res = run_bass_kernel_spmd(nc, in_maps, core_ids=list(range(B)))
    out = np.stack([np.asarray(r["out"], np.float32) for r in res.results], axis=0)
    return out
```

```python
from contextlib import ExitStack
import numpy as np
import concourse.bass as bass
import concourse.mybir as mybir
from concourse.bass_utils import run_bass_kernel_spmd

F32 = mybir.dt.float32
BF16 = mybir.dt.bfloat16
U8 = mybir.dt.uint8
AF = mybir.ActivationFunctionType
ALU = mybir.AluOpType
AX = mybir.AxisListType

NDS = 16


class _Sem:
    def __init__(self, h, name):
        self.h = h
        self.name = name


class Buf:
    def __init__(self, name, t=None):
        self.name = name
        self.t = t
        self.last_w = None
        self.readers = {}


class _Eng:
    def __init__(self, name, sem):
        self.name = name
        self.sem = sem
        self.n = 0
        self.seen = {}
        self.q = []
        self.ndma = 0
        self.dsems = []


class Prog:
    ENG = ("pe", "act", "dve", "pool", "sp")

    def __init__(self, nc):
        self.nc = nc
        self.es = ExitStack()
        self.eng = {}
        for n in self.ENG:
            s = _Sem(self.es.enter_context(nc.semaphore("s_" + n)), n)
            self.eng[n] = _Eng(n, s)
        for n in ("sp", "pool", "act"):
            self.eng[n].dsems = [_Sem(self.es.enter_context(nc.semaphore("d_%s_%d" % (n, i))), "d%s%d" % (n, i))
                                 for i in range(NDS)]
        self.out_clocks = []
        self.nwait = 0

    def sb(self, name, shape, dt):
        return Buf(name, self.es.enter_context(self.nc.sbuf_tensor("s_" + name, list(shape), dt)))

    def ps(self, name, shape, dt):
        return Buf(name, self.es.enter_context(self.nc.psum_tensor("p_" + name, list(shape), dt)))

    def view(self, name, t):
        return Buf(name, t)

    def _need(self, E, clock, strict_same):
        sem, val = clock
        if sem is E.sem and not strict_same:
            return
        if E.seen.get(sem, 0) >= val:
            return
        E.seen[sem] = val
        E.q.append(("w", sem, val))
        self.nwait += 1

    def _deps(self, E, reads, writes, is_dma=False):
        strict = is_dma or E.name != "pe"
        for b in reads:
            if b.last_w is not None:
                self._need(E, b.last_w, True)
        for b in writes:
            if b.last_w is not None:
                self._need(E, b.last_w, strict)
            for s, v in b.readers.items():
                self._need(E, (s, v), strict)

    def _mark(self, clock, reads, writes):
        sem, val = clock
        for b in writes:
            b.last_w = clock
            b.readers = {}
        for b in reads:
            if b.readers.get(sem, 0) < val:
                b.readers[sem] = val

    def op(self, eng, fn, reads=(), writes=()):
        E = self.eng[eng]
        self._deps(E, reads, writes)
        E.n += 1
        E.q.append(("o", fn))
        self._mark((E.sem, E.n), reads, writes)

    def dma(self, queue, out, in_, reads=(), writes=(), is_output=False):
        E = self.eng[queue]
        j = E.ndma
        E.ndma += 1
        ds = E.dsems[j % NDS]
        prev = 16 * (j // NDS)
        if prev > 0:
            self._need(E, (ds, prev), True)
        self._deps(E, reads, writes, is_dma=True)
        E.q.append(("d", out, in_, ds))
        clock = (ds, prev + 16)
        self._mark(clock, reads, writes)
        if is_output:
            self.out_clocks.append(clock)
        return clock

    def finish(self, final_eng="sp"):
        E = self.eng[final_eng]
        last = {}
        for s, v in self.out_clocks:
            if last.get(s, (None, 0))[1] < v:
                last[s] = (s, v)
        for s, v in last.values():
            E.q.append(("w", s, v))
        nc = self.nc
        engs = self.eng

        def replay(E, e):
            for it in E.q:
                k = it[0]
                if k == "w":
                    e.wait_ge(it[1].h, it[2])
                elif k == "o":
                    it[1](e).then_inc(E.sem.h, 1)
                else:
                    e.dma_start(out=it[1], in_=it[2]).then_inc(it[3].h, 16)

        with nc.Block() as block:
            @block.tensor
            def _(e):
                replay(engs["pe"], e)

            @block.scalar
            def _(e):
                replay(engs["act"], e)

            @block.vector
            def _(e):
                replay(engs["dve"], e)

            @block.gpsimd
            def _(e):
                replay(engs["pool"], e)

            @block.sync
            def _(e):
                replay(engs["sp"], e)
        self.es.close()

    def alias(self, dst, srcs):
        for s in srcs:
            if s.last_w is not None:
                c = s.last_w
                if dst.readers.get(c[0], 0) < c[1]:
                    dst.readers[c[0]] = c[1]
            for k, v in s.readers.items():
                if dst.readers.get(k, 0) < v:
                    dst.readers[k] = v


D = 1024
NMETA = 16
FQ, FK, FG, TC, TQ, TF, TI, TG = 0, 512, 640, 1152, 1540, 2052, 2564, 3076
NTC = 388
NCOL = 3588
DN_ALPHA = 4.0 ** 0.25
EPS = 1e-6
NEG = -30000.0


def _mid_bc(ap, n):
    a = [list(x) for x in ap.ap]
    return bass.AP(ap.tensor, ap.offset, [a[0], [0, n]] + a[1:])


def host_constants():
    c = {}
    idx = np.arange(128)
    c["identf"] = np.eye(128, dtype=np.float32)
    same = (idx[:, None] // 64) == (idx[None, :] // 64)
    c["m1t"] = (same & (idx[:, None] <= idx[None, :])).astype(np.float32)
    c["m3t"] = (same & (idx[:, None] > idx[None, :])).astype(np.float32)
    sel = np.zeros((128, 2), np.float32)
    sel[:64, 0] = 1.0
    sel[64:, 1] = 1.0
    c["sel"] = sel
    c["causneg"] = np.where(idx[None, :] <= idx[:, None], 0.0, -1e30).astype(np.float32)
    c["cmask"] = (same & (idx[:, None] <= idx[None, :])).astype(np.float32)
    c["j128"] = np.fliplr(np.eye(128, dtype=np.float32)).copy()
    sh = np.zeros((128, 16), np.float32)
    sh[112 + np.arange(16), np.arange(16)] = 1.0
    c["shiftm"] = sh
    d = np.arange(384) - 127
    n = np.maximum(d, 0)
    nf = np.maximum(n, 1).astype(np.float32)
    large = 16 + (np.log(nf / np.float32(16)) / np.float32(np.log(128 / 16)) * np.float32(16)).astype(np.int32)
    large = np.minimum(large, 31)
    bucket = np.where(n < 16, n, large)
    oh = np.zeros((32, 384), np.float32)
    for j in range(384):
        if d[j] >= 0:
            oh[bucket[j], j] = 1.0
    c["ohd"] = oh
    c["negrow"] = np.broadcast_to(np.where(d >= 0, 0.0, NEG).astype(np.float32)[None, :], (8, 384)).copy()
    return c


CONST_SHAPES = {"identf": [128, 128], "m1t": [128, 128], "m3t": [128, 128], "sel": [128, 2],
                "causneg": [128, 128], "cmask": [128, 128], "j128": [128, 128], "shiftm": [128, 16],
                "ohd": [32, 384], "negrow": [8, 384]}


def _resplit(ap, a, b):
    return bass.AP(ap.tensor, ap.offset, [list(ap.ap[0]), [b, a], [1, b]])


def _zs(ap, n):
    return bass.AP(ap.tensor, ap.offset, [list(ap.ap[0]), [0, n]])


def _split(ap, a, b):
    p = list(ap.ap[0])
    st = ap.ap[-1][0]
    return bass.AP(ap.tensor, ap.offset, [p, [b * st, a], [st, b]])


class _StopBuild(Exception):
    pass


def build_program(nc, NT, layers=(0, 1), ktop=256, nit=24, taps=None, final_layer=1, stop_at=None):
    try:
        return _build_program(nc, NT, layers, ktop, nit, taps, final_layer, stop_at)
    except _StopBuild as e:
        P = e.args[0]
        P.finish()
        return P, {}


def _build_program(nc, NT, layers, ktop, nit, taps, final_layer, stop_at):
    NTT = NT + 1
    S = NT * 128
    P = Prog(nc)
    tap_out = {}

    def ck(name):
        if stop_at == name:
            raise _StopBuild(P)

    def dr(name, shape, kind="ExternalInput"):
        return nc.dram_tensor(name, list(shape), F32, kind=kind).ap()

    x_d = dr("x", [S, D])
    meta_d = dr("meta", [NMETA, D])
    out_d = dr("out", [S, D], "ExternalOutput")
    win_d = dr("w_in", [2, D, NCOL])
    wout_d = dr("w_out", [2, D, D])
    wuk_d = dr("w_uk", [2, 128, 1024])
    wuv_d = dr("w_uv", [2, 128, 1024])
    gkv_d = dr("gkv", [2, 128, 128])
    ghn_d = dr("ghn", [2, 128, 512])
    lng_d = dr("lng", [2, 128, D])
    lnb_d = dr("lnb", [2, 128, D])
    lbraw_d = dr("lbraw", [128, 1024])
    rb_d = dr("rb", [32, 8])
    rb31_d = dr("rb31", [8, 1])
    const_d = {k: dr(k, v) for k, v in CONST_SHAPES.items()}
    h1_d = dr("h1s", [NTT * 128, D], "Internal")
    vd_d = dr("vds", [8, 384], "Internal")
    h1B = [Buf("h1_%d" % i) for i in range(NTT)]
    vdB = Buf("vd")

    def tap(name, buf, ap, shape):
        if taps is None or name not in taps:
            return
        d = dr("tap_" + name, shape, "ExternalOutput")
        tap_out[name] = shape
        P.dma("pool", d, ap, reads=[buf], is_output=True)

    sb = P.sb
    win = sb("win", [128, 8, NCOL], BF16)
    winB = [Buf("win%d" % k, win.t) for k in range(8)]
    wout = sb("wout", [128, 8, D], BF16)
    woutB = [Buf("wout%d" % k, wout.t) for k in range(8)]
    wuk = sb("wuk", [128, 8, 128], BF16)
    wuv = sb("wuv", [128, 8, 128], BF16)
    caug = sb("caug", [128, NTT, 130], BF16)
    caugB = [Buf("caug%d" % i, caug.t) for i in range(NTT)]
    cT = sb("cT", [128, NTT * 128], BF16)
    cTB = [Buf("cT%d" % i, cT.t) for i in range(NTT)]
    kiT = sb("kiT", [128, NTT * 128], BF16)
    kiTB = [Buf("kiT%d" % i, kiT.t) for i in range(NTT)]
    B0 = sb("B0", [128, 2, 8, 128], BF16)
    B1 = sb("B1", [128, 2, 8, 128], BF16)
    shiftm = sb("shiftm", [128, 16], BF16)
    gkvB = sb("gkvB", [128, 128], F32)
    ghnB = sb("ghnB", [128, 512], F32)
    lnG = sb("lnG", [128, D], F32)
    lnBt = sb("lnB", [128, D], F32)
    lbB = sb("lbB", [128, 512], F32)
    omlB = sb("omlB", [128, 512], F32)
    identb = sb("identb", [128, 128], BF16)
    hresP = [sb("hres%d" % k, [128, D], F32) for k in range(2)]
    hres = hresP[0]
    hb = sb("hb", [128, D], BF16)
    hT = sb("hT", [128, 8, 128], BF16)
    qT = sb("qT", [128, 4, 128], BF16)
    qlTP = [sb("qlT%d" % k, [128, 8, 128], BF16) for k in range(2)]
    qlT = qlTP[0]
    qis = sb("qis", [128, 256], BF16)
    qiT = sb("qiT", [128, 4, 128], BF16)
    gaTP = [sb("gaT%d" % k, [128, 4, 128], BF16) for k in range(2)]
    gaT = gaTP[0]
    craw = sb("craw", [128, NTC], F32)
    sm = sb("sm", [128, 64], F32)
    smB = {}

    def small(name, c0, n):
        smB[name] = (Buf("sm_" + name, sm.t), c0, n)

    small("wabs", 0, 4); small("wsgn", 4, 4); small("cs", 8, 1); small("crs", 9, 1)
    small("lo", 10, 1); small("step", 11, 1); small("mid", 12, 1); small("cnt", 13, 1); small("fl", 14, 1)
    small("mx", 15, 1); small("rden", 16, 8); small("ssq4", 24, 4); small("rs4", 28, 4)
    small("sg", 36, 1); small("thrA", 37, 1); small("thrc", 38, 1); small("base", 39, 1); small("mv", 32, 2); small("lnr", 34, 1); small("lnm", 35, 1); small("dS", 40, 8)

    def smv(name, R=128, a=None, b=None):
        bf, c0, n = smB[name]
        a = 0 if a is None else a
        b = n if b is None else b
        return sm.t[0:R, c0 + a:c0 + b]

    def smb(name):
        return smB[name][0]

    stats = sb("stats", [128, 12], F32)
    sidx = sb("sidx", [128, 4096], F32)
    stgB = [Buf("stg0", sidx.t), Buf("stg1", sidx.t)]
    itmp = sb("itmp", [128, 4, 128], F32)
    itmpP = [itmp, sb("itmp2", [128, 4, 128], F32)]
    sidxB = [Buf("sidxA"), Buf("sidxB")]
    mball = sb("mball", [128, 4096], U8)
    jkD = sb("jkD", [128, 8], U8)
    jkA = sb("jkA", [128, 8], mybir.dt.int8)
    _iv = itmp.t[0:8, 0:3, :]
    vs = Buf("vs", None)
    vs_ap = bass.AP(_iv.tensor, _iv.offset, [list(_iv.ap[0]), [1, 384]])
    maskb = [sb("maskb%d" % i, [128, 128], BF16) for i in range(2)]
    E2 = sb("E2", [128, 2, 8, 128], BF16)
    Eb = [Buf("E%d" % k, E2.t[:, k]) for k in range(2)]
    _e0 = E2.t[:, 0, 0, :]
    olat = sb("olat", [128, 8, 128], BF16)
    olatT = sb("olatT", [128, 8, 128], BF16)
    arTP = [sb("arT%d" % k, [128, 8, 128], BF16) for k in range(2)]
    arT = arTP[0]
    TT = sb("TT", [128, 5, 512], F32)
    Tb = [Buf("T%d" % k, TT.t) for k in range(5)]

    def Tv(k, R=128):
        return TT.t[0:R, k, :]
    Vb = sb("Vb", [128, 512], BF16)
    QD = sb("QD", [128, 512], BF16)
    KD = sb("KD", [128, 512], BF16)
    KL = sb("KL", [128, 512], BF16)
    KLB = sb("KLB", [128, 512], BF16)
    QDTA = sb("QDTA", [128, 4, 128], BF16)
    QDTB = sb("QDTB", [128, 4, 128], BF16)
    KDT = sb("KDT", [128, 4, 128], BF16)
    SC = sb("SC", [128, 4, 128], BF16)
    Rb = QD
    Sf = sb("Sf", [128, 4, 128], F32)
    SbE = sb("SbE", [128, 4, 128], BF16)
    SbM = sb("SbM", [128, 4, 128], BF16)
    rbs = sb("rbs", [32, 8], F32)
    rb31 = sb("rb31", [8, 1], F32)

    _e2f = bass.AP(_e0.tensor, _e0.offset, [list(_e0.ap[0]), [1, 2048]]).bitcast(F32)
    cviews = {"ohd": _e2f[0:32, 0:384], "negrow": _e2f[0:8, 384:768], "j128": _e2f[:, 768:896],
              "shiftm": _e2f[:, 896:912], "identf": itmp.t[:, 3, :]}
    cst = {}
    for k, shp in CONST_SHAPES.items():
        if k in cviews:
            cst[k] = Buf("c_" + k, cviews[k])
        else:
            cst[k] = sb("c_" + k, shp, F32)
        P.dma("sp", cst[k].t[:], const_d[k], writes=[cst[k]])
    identf, m1t, m3t, sel, causneg, cmask, j128 = (cst[k] for k in
                                                   ("identf", "m1t", "m3t", "sel", "causneg", "cmask", "j128"))
    bank = [P.ps("bk%d" % i, [128, 512], F32) if i != 2 else P.ps("bk2", [128, 1024], BF16) for i in range(8)]
    tbB = bank[2]
    tb = bank[2].t[:, :]

    def tbv(c0, n, R=128):
        return tb[0:R, c0:c0 + n]

    op = P.op

    def act(out, in_, func, reads, writes, scale=None, bias=None, accum=None, eng="act"):
        kw = {}
        if scale is not None:
            kw["scale"] = scale
        if bias is not None:
            kw["bias"] = bias
        if accum is not None:
            kw["accum_out"] = accum
        op(eng, lambda e: e.activation(out=out, in_=in_, func=func, **kw), reads, writes)

    def ts(out, in0, s1, op0, reads, writes, s2=None, op1=None, accum=None, eng="dve"):
        kw = {}
        if op1 is not None:
            kw["op1"] = op1
        if accum is not None:
            kw["accum_out"] = accum
        op(eng, lambda e: e.tensor_scalar(out=out, in0=in0, scalar1=s1, scalar2=s2, op0=op0, **kw), reads, writes)

    def tt(out, in0, in1, o, reads, writes, eng="dve"):
        op(eng, lambda e: e.tensor_tensor(out=out, in0=in0, in1=in1, op=o), reads, writes)

    def stt(out, in0, scalar, in1, op0, op1, reads, writes):
        op("dve", lambda e: e.scalar_tensor_tensor(out=out, in0=in0, scalar=scalar, in1=in1, op0=op0, op1=op1),
           reads, writes)

    def cp(out, in_, reads, writes, eng="dve"):
        op(eng, lambda e: e.tensor_copy(out=out, in_=in_), reads, writes)

    def mm(out, lhsT, rhs, start, stop, reads, writes, sg=False):
        op("pe", lambda e: e.matmul(out, lhsT=lhsT, rhs=rhs, start=start, stop=stop, skip_group_check=sg), reads, writes)

    def tr(out, in_, R, reads, writes):
        op("pe", lambda e: e.transpose(out=out, in_=in_, identity=identb.t[0:R, 0:R]), list(reads) + [identb], writes)

    def sigm(dst, src, src_bufs, dbuf):
        act(dst, src, AF.Exp, src_bufs, [dbuf], scale=-1.0)
        act(dst, dst, AF.Ln, [dbuf], [dbuf], bias=1.0)
        act(dst, dst, AF.Exp, [dbuf], [dbuf], scale=-1.0)

    def rsqrt_small(dst, src, R, mul, add):
        dv = smv(dst[0], R, dst[1], dst[2])
        sv = smv(src[0], R, src[1], src[2])
        ts(dv, sv, mul, ALU.mult, [smb(src[0])], [smb(dst[0])], s2=add, op1=ALU.add)
        act(dv, dv, AF.Ln, [smb(dst[0])], [smb(dst[0])])
        act(dv, dv, AF.Exp, [smb(dst[0])], [smb(dst[0])], scale=-0.5)

    cp(identb.t[:], identf.t[:], [identf], [identb])
    op("dve", lambda e: e.memset(caug.t[:, :, 128:130], 1.0), [], caugB)
    op("dve", lambda e: e.memset(QDTA.t[:], 0.0), [], [QDTA])
    op("dve", lambda e: e.memset(QDTB.t[:], 0.0), [], [QDTB])
    op("dve", lambda e: e.memset(qiT.t[:], 0.0), [], [qiT])
    op("dve", lambda e: e.memset(KLB.t[:], 0.0), [], [KLB])
    P.dma("sp", rbs.t[:], rb_d, writes=[rbs])
    P.dma("sp", rb31.t[:], rb31_d, writes=[rb31])
    ohd = cst["ohd"]
    mm(bank[0].t[0:8, 0:384], rbs.t[:], ohd.t[:], True, True, [rbs, ohd], [bank[0]])
    ts(vs_ap, bank[0].t[0:8, 0:384], rb31.t[:, 0:1], ALU.subtract, [bank[0], rb31], [vs])
    tt(vs_ap, vs_ap, cst["negrow"].t[:], ALU.add, [vs, cst["negrow"]], [vs])
    P.dma("sp", vd_d, vs_ap, reads=[vs], writes=[vdB])
    P.alias(itmp, [vs])
    P.dma("sp", _split(hres.t[:, :], 8, 128), bass.AP(vd_d.tensor, 0, [[1, 128], [384, 8], [1, 128]]),
          reads=[vdB], writes=[hres])
    P.dma("sp", _resplit(TT.t[:, 0:2, :], 8, 128),
          bass.AP(vd_d.tensor, 128, [[1, 128], [384, 8], [1, 128]]), reads=[vdB], writes=[Tb[0], Tb[1]])
    for half in range(2):
        mm(bank[half].t[:, :], j128.t[:], hres.t[:, half * 512:(half + 1) * 512], True, True, [j128, hres], [bank[half]])
        act(B0.t[:, 0, half * 4:(half + 1) * 4, :], _split(bank[half].t[:, :], 4, 128), AF.Copy, [bank[half]], [B0])
        tt(B0.t[:, 1, half * 4:(half + 1) * 4, :], _split(bank[half].t[:, :], 4, 128), B0.t[:, 0, half * 4:(half + 1) * 4, :],
           ALU.subtract, [bank[half], B0], [B0])
    for half in range(2):
        mm(bank[half].t[:, :], j128.t[:], TT.t[:, half, :], True, True, [j128, Tb[half]], [bank[half]])
        act(B1.t[:, 0, half * 4:(half + 1) * 4, :], _split(bank[half].t[:, :], 4, 128), AF.Copy, [bank[half]], [B1])
        tt(B1.t[:, 1, half * 4:(half + 1) * 4, :], _split(bank[half].t[:, :], 4, 128), B1.t[:, 0, half * 4:(half + 1) * 4, :],
           ALU.subtract, [bank[half], B1], [B1])
    cp(shiftm.t[:], cst["shiftm"].t[:], [cst["shiftm"]], [shiftm])
    P.dma("sp", TT.t[:, 2:4, :], _split(lbraw_d, 2, 512), writes=[Tb[2], Tb[3]])
    tt(Tv(4), Tv(2), Tv(3), ALU.subtract, [Tb[2], Tb[3]], [Tb[4]])
    act(Tv(4), Tv(4), AF.Exp, [Tb[4]], [Tb[4]])
    ts(Tv(4), Tv(4), 1.0, ALU.add, [Tb[4]], [Tb[4]])
    op("dve", lambda e: e.reciprocal(out=lbB.t[:], in_=Tv(4)), [Tb[4]], [lbB])
    ts(omlB.t[:], lbB.t[:], -1.0, ALU.mult, [lbB], [omlB], s2=1.0, op1=ALU.add)

    for b_ in Eb:
        P.alias(b_, [cst["ohd"], cst["negrow"], cst["j128"], cst["shiftm"]])
    P.alias(itmp, [identf, vs])
    ck("prologue")
    SCALE_Q = 0.125
    SCALE_I = 1.0 / 16.0
    SCALE_H = 128.0 ** -0.5

    castn = [0]

    def cast(out, in_, reads, writes):
        e = ("pool", "dve", "act")[castn[0] % 3]
        castn[0] += 1
        if e == "act":
            act(out, in_, AF.Copy, reads, writes)
        else:
            cp(out, in_, reads, writes, eng=e)

    def pc_ap(h, R, n=129):
        return bank[5 + h // 3].t[0:R, (h % 3) * 129:(h % 3) * 129 + n]

    for l in layers:
        for b_ in stgB:
            P.alias(b_, [sidx])
        HW = NCOL // 2
        n = 0
        for kc in range(8):
            for half in range(2):
                st = stgB[n % 2]
                so = (n % 2) * 2048
                n += 1
                P.dma("sp", sidx.t[:, so:so + HW], win_d[l, kc * 128:(kc + 1) * 128, half * HW:(half + 1) * HW], writes=[st])
                cast(win.t[:, kc, half * HW:(half + 1) * HW], sidx.t[:, so:so + HW], [st], [winB[kc]])
        for j in range(8):
            st = stgB[n % 2]
            so = (n % 2) * 2048
            n += 1
            P.dma("sp", sidx.t[:, so:so + D], wout_d[l, j * 128:(j + 1) * 128, :], writes=[st])
            cast(wout.t[:, j, :], sidx.t[:, so:so + D], [st], [woutB[j]])
        st = stgB[n % 2]; so = (n % 2) * 2048; n += 1
        P.dma("sp", sidx.t[:, so:so + 1024], wuk_d[l], writes=[st])
        cast(wuk.t[:, :, :], _split(sidx.t[:, so:so + 1024], 8, 128), [st], [wuk])
        st = stgB[n % 2]; so = (n % 2) * 2048; n += 1
        P.dma("sp", sidx.t[:, so:so + 1024], wuv_d[l], writes=[st])
        cast(wuv.t[:, :, :], _split(sidx.t[:, so:so + 1024], 8, 128), [st], [wuv])
        P.alias(sidx, stgB)
        P.dma("sp", gkvB.t[:], gkv_d[l], writes=[gkvB])
        P.dma("sp", ghnB.t[:], ghn_d[l], writes=[ghnB])
        P.dma("sp", lnG.t[:], lng_d[l], writes=[lnG])
        P.dma("sp", lnBt.t[:], lnb_d[l], writes=[lnBt])
        op("dve", lambda e: e.memset(Sf.t[:], 0.0), [], [Sf])
        op("dve", lambda e: e.memset(SbE.t[:], 0.0), [], [SbE])
        last_layer = (l == final_layer)
        ck("weights%d" % l)

        def tile_gen(i, l=l, last_layer=last_layer):
            R = NMETA if i == 0 else 128
            RA = min(R, 64)
            qb = i - 1
            need_out = not (last_layer and i == 0)
            hres, qlT, gaT, arT = hresP[i % 2], qlTP[i % 2], gaTP[i % 2], arTP[i % 2]
            if l == 0:
                src, srcB = (meta_d if i == 0 else x_d[qb * 128:(qb + 1) * 128, :]), []
            else:
                src, srcB = h1_d[i * 128:i * 128 + R, :], [h1B[i]]
            P.dma("pool", hb.t[0:R, :], src, reads=srcB, writes=[hb])
            yield "A0"
            for kc in range(8):
                tr(tbv(kc * 128, R), hb.t[0:R, kc * 128:(kc + 1) * 128], R, [hb], [tbB])
            act(hT.t[:, :, 0:R], _split(tb[:, :], 8, 128)[:, :, 0:R], AF.Copy, [tbB], [hT])

            ck("l%dt%d_load" % (l, i))
            def fm_group(bk, col0, nchunk):
                for j in range(nchunk):
                    for kc in range(8):
                        mm(bk.t[:, j * 128:j * 128 + R], win.t[:, kc, col0 + j * 128:col0 + (j + 1) * 128],
                           hT.t[:, kc, 0:R], kc == 0, kc == 7, [winB[kc], hT], [bk])

            def tm_group(bk, col0, ncol):
                for kc in range(8):
                    mm(bk.t[0:R, 0:ncol], hT.t[:, kc, 0:R], win.t[:, kc, col0:col0 + ncol], kc == 0, kc == 7,
                       [winB[kc], hT], [bk])

            yield "A"
            b0, b1 = bank[0], bank[1]
            bq = bank[0]
            fm_group(bq, FQ, 4)
            act(qT.t[:, :, 0:R], _split(bq.t[:, :], 4, 128)[:, :, 0:R], AF.Copy, [bq], [qT], scale=SCALE_Q)
            yield "A"
            bq = bank[1]
            fm_group(bq, FK, 1)
            act(kiT.t[:, i * 128:i * 128 + R], bq.t[:, 0:R], AF.Copy, [bq], [kiTB[i]])
            yield "A"
            bq = bank[3]
            fm_group(bq, FG, 4)
            g4v = _split(bq.t[:, :], 4, 128)[:, :, 0:R]
            t4v = _split(TT.t[:, 4, :], 4, 128)[:, :, 0:R]
            sigm(t4v, g4v, [bq], Tb[4])
            tt(gaT.t[:, :, 0:R], g4v, t4v, ALU.mult, [bq, Tb[4]], [gaT])
            yield "A"
            bq = bank[4]
            tm_group(bq, TC, NTC)
            act(craw.t[0:R, :], bq.t[0:R, 0:NTC], AF.Copy, [bq], [craw])
            yield "A"
            bq = bank[5]
            tm_group(bq, TQ, 512)
            sigm(Tv(0, R), bq.t[0:R, :], [bq], Tb[0])
            stt(Tv(0, R), bq.t[0:R, :], SCALE_H, Tv(0, R), ALU.mult, ALU.mult, [bq, Tb[0]], [Tb[0]])
            yield "A"
            b1 = bank[6]
            tm_group(b1, TF, 512)
            act(Tv(1, R), b1.t[0:R, :], AF.Exp, [b1], [Tb[1]], scale=-1.0)
            act(Tv(2, R), Tv(1, R), AF.Ln, [Tb[1]], [Tb[2]], bias=1.0)
            act(Tv(1, R), Tv(2, R), AF.Exp, [Tb[2]], [Tb[1]], scale=-1.0)
            gs = -1.0
            if l > 0:
                gs = 1.0
                tt(Tv(1, R), Tv(1, R), omlB.t[0:R, :], ALU.mult, [Tb[1], omlB], [Tb[1]])
                tt(Tv(1, R), Tv(1, R), lbB.t[0:R, :], ALU.add, [Tb[1], lbB], [Tb[1]])
                act(Tv(2, R), Tv(1, R), AF.Ln, [Tb[1]], [Tb[2]])
            ts(Tv(1, R), Tv(1, R), -1.0, ALU.mult, [Tb[1]], [Tb[1]], s2=1.0, op1=ALU.add)
            yield "A"
            bq = bank[7]
            tm_group(bq, TI, 512)
            act(Vb.t[0:R, :], bq.t[0:R, :], AF.Copy, [bq], [Vb])
            yield "A"
            b1 = bank[0]
            tm_group(b1, TG, 512)
            sigm(Tv(3, R), b1.t[0:R, :], [b1], Tb[3])
            tt(Tv(3, R), b1.t[0:R, :], Tv(3, R), ALU.mult, [b1, Tb[3]], [Tb[3]])
            tt(Tv(3, R), Tv(3, R), ghnB.t[0:R, :], ALU.mult, [Tb[3], ghnB], [Tb[3]])

            b0, b1 = bank[0], bank[1]
            P.dma("sp", hres.t[0:R, :], src, reads=srcB, writes=[hres])
            yield "A"
            ck("l%dt%d_inproj" % (l, i))
            act(Tv(4, R)[:, 0:128], craw.t[0:R, 0:128], AF.Square, [craw], [Tb[4], smb("cs")], accum=smv("cs", R))
            rsqrt_small(("crs", 0, 1), ("cs", 0, 1), R, 1.0 / 128.0, EPS)
            stt(caug.t[0:R, i, 0:128], craw.t[0:R, 0:128], smv("crs", R), gkvB.t[0:R, :], ALU.mult, ALU.mult,
                [craw, smb("crs"), gkvB], [caugB[i]])
            tr(tbv(0, R), caug.t[0:R, i, 0:128], R, [caugB[i]], [tbB])
            ck("l%dt%d_c0" % (l, i))
            act(cT.t[:, i * 128:i * 128 + R], tbv(0, R), AF.Copy, [tbB], [cTB[i]])
            ck("l%dt%d_c" % (l, i))
            if i >= 1:
                wv = craw.t[0:R, 128:132]
                ts(smv("wsgn", R), wv, 0.0, ALU.is_ge, [craw], [smb("wsgn")], s2=2.0, op1=ALU.mult)
                ts(smv("wsgn", R), smv("wsgn", R), -1.0, ALU.add, [smb("wsgn")], [smb("wsgn")])
                tt(smv("wabs", R), wv, smv("wsgn", R), ALU.mult, [craw, smb("wsgn")], [smb("wabs")])
                ts(smv("wabs", R), smv("wabs", R), SCALE_I, ALU.mult, [smb("wabs")], [smb("wabs")])
                wb = smv("wabs", R)
                wbc = bass.AP(wb.tensor, wb.offset, [list(wb.ap[0]), [1, 4], [0, 64]])
                tt(_split(qis.t[0:R, :], 4, 64), _split(craw.t[0:R, 132:388], 4, 64), wbc, ALU.mult,
                   [craw, smb("wabs")], [qis])
                for j in range(2):
                    tr(tbv(j * 128, R), qis.t[0:R, j * 128:(j + 1) * 128], R, [qis], [tbB])
                for h in range(4):
                    pb = (h % 2) * 64
                    cp(qiT.t[pb:pb + 64, h, 0:R], tb[pb:pb + 64, (h // 2) * 128:(h // 2) * 128 + R], [tbB], [qiT])
            for h in range(8):
                bk = bank[3 + h // 4]
                mm(bk.t[:, (h % 4) * 128:(h % 4) * 128 + R], wuk.t[:, h, :], qT.t[:, h // 2, 0:R],
                   True, True, [wuk, qT], [bk])
            ck("l%dt%d_ql" % (l, i))
            act(qlT.t[:, 0:4, 0:R], _split(bank[3].t[:, :], 4, 128)[:, :, 0:R], AF.Copy, [bank[3]], [qlT])
            act(qlT.t[:, 4:8, 0:R], _split(bank[4].t[:, :], 4, 128)[:, :, 0:R], AF.Copy, [bank[4]], [qlT])

            yield "A"
            ck("l%dt%d_prep" % (l, i))
            mm(b0.t[0:R, :], m1t.t[0:R, 0:R], Tv(2, R), True, True, [m1t, Tb[2]], [b0])
            mm(b1.t[0:R, :], m3t.t[0:R, 0:R], Tv(2, R), True, True, [m3t, Tb[2]], [b1])
            for h in range(4):
                mm(bank[7].t[:, 2 * h:2 * h + 2], TT.t[0:R, 2, h * 128:(h + 1) * 128], sel.t[0:R, :], True, True,
                   [Tb[2], sel], [bank[7]])
            act(smv("dS"), bank[7].t[:, 0:8], AF.Exp, [bank[7]], [smb("dS")], scale=gs)
            act(Tv(4, R), b0.t[0:R, :], AF.Exp, [b0], [Tb[4]], scale=gs)
            tt(QD.t[0:R, :], Tv(0, R), Tv(4, R), ALU.mult, [Tb[0], Tb[4]], [QD])
            act(Tv(4, R), b0.t[0:R, :], AF.Exp, [b0], [Tb[4]], scale=-gs)
            tt(KD.t[0:R, :], Tv(1, R), Tv(4, R), ALU.mult, [Tb[1], Tb[4]], [KD])
            act(Tv(4, R), b1.t[0:R, :], AF.Exp, [b1], [Tb[4]], scale=gs)
            tt(KL.t[0:RA, :], Tv(1, RA), Tv(4, RA), ALU.mult, [Tb[1], Tb[4]], [KL])
            if R == 128:
                tt(KLB.t[64:128, :], TT.t[64:128, 1, :], TT.t[64:128, 4, :], ALU.mult, [Tb[1], Tb[4]], [KLB])
            yield "A"
            for h in range(4):
                tr(tbv(h * 128, R), QD.t[0:R, h * 128:(h + 1) * 128], R, [QD], [tbB])
            for h in range(4):
                tr(tbv(512 + h * 128, R), KD.t[0:R, h * 128:(h + 1) * 128], R, [KD], [tbB])
            t8 = _split(tb[:, :], 8, 128)
            act(QDTA.t[:, :, 0:RA], t8[:, 0:4, 0:RA], AF.Copy, [tbB], [QDTA])
            if R == 128:
                cp(QDTB.t[:, :, 64:128], t8[:, 0:4, 64:128], [tbB], [QDTB])
            act(KDT.t[:, :, 0:R], t8[:, 4:8, 0:R], AF.Copy, [tbB], [KDT])
            yield "A"
            b3, b4 = bank[3], bank[4]
            for h in range(4):
                mm(b3.t[0:R, h * 128:h * 128 + RA], KDT.t[:, h, 0:R], QDTA.t[:, h, 0:RA], True, True, [KDT, QDTA], [b3])
                if R == 128:
                    mm(b3.t[0:R, h * 128 + 64:h * 128 + 128], KDT.t[:, h, 0:R], QDTB.t[:, h, 64:128], True, True,
                       [KDT, QDTB], [b3])
            tt(SC.t[0:R, :, 0:R], _split(b3.t[0:R, :], 4, 128)[:, :, 0:R], _mid_bc(cmask.t[0:R, 0:R], 4), ALU.mult,
               [b3, cmask], [SC])
            for h in range(4):
                hc = slice(h * 128, (h + 1) * 128)
                mm(bank[5].t[:, hc], KL.t[0:RA, hc], Vb.t[0:RA, hc], True, True, [KL, Vb], [bank[5]])
                if R == 128:
                    mm(bank[6].t[:, hc], KLB.t[:, hc], Vb.t[:, hc], True, True, [KLB, Vb], [bank[6]])
            yield "A"
            for h in range(4):
                hc = slice(h * 128, (h + 1) * 128)
                mm(b4.t[0:R, hc], SC.t[0:R, h, 0:R], Vb.t[0:R, hc], h == 0, False, [SC, Vb], [b4], sg=True)
                mm(b4.t[0:R, hc], QDTA.t[:, h, 0:R], SbE.t[:, h, :], False, R < 128, [QDTA, SbE], [b4], sg=True)
            for h in range(4):
                hc = slice(h * 128, (h + 1) * 128)
                stt(Sf.t[:, h, :], Sf.t[:, h, :], smv("dS", 128, 2 * h, 2 * h + 1), bank[5].t[:, hc], ALU.mult, ALU.add,
                    [Sf, smb("dS"), bank[5]], [Sf])
            if R == 128:
                cp(SbM.t[:], Sf.t[:], [Sf], [SbM])
                for h in range(4):
                    hc = slice(h * 128, (h + 1) * 128)
                    mm(b4.t[0:R, hc], QDTB.t[:, h, :], SbM.t[:, h, :], False, True, [QDTB, SbM], [b4], sg=True)
                for h in range(4):
                    hc = slice(h * 128, (h + 1) * 128)
                    stt(Sf.t[:, h, :], Sf.t[:, h, :], smv("dS", 128, 2 * h + 1, 2 * h + 2), bank[6].t[:, hc], ALU.mult,
                        ALU.add, [Sf, smb("dS"), bank[6]], [Sf])
            cp(SbE.t[:], Sf.t[:], [Sf], [SbE])
            yield "A"
            if need_out:
                for h in range(4):
                    hc = slice(h * 128, (h + 1) * 128)
                    act(Tv(4, R)[:, hc], b4.t[0:R, hc], AF.Square, [b4], [Tb[4], smb("ssq4")],
                        accum=smv("ssq4", R, h, h + 1))
                rsqrt_small(("rs4", 0, 4), ("ssq4", 0, 4), R, 1.0 / 128.0, EPS)
                for h in range(4):
                    hc = slice(h * 128, (h + 1) * 128)
                    stt(Rb.t[0:R, hc], b4.t[0:R, hc], smv("rs4", R, h, h + 1), Tv(3, R)[:, hc], ALU.mult, ALU.mult,
                        [b4, smb("rs4"), Tb[3]], [Rb])
                for h in range(4):
                    tr(tbv(h * 128, R), Rb.t[0:R, h * 128:(h + 1) * 128], R, [Rb], [tbB])
                act(arT.t[:, 4:8, 0:R], t8[:, 0:4, 0:R], AF.Copy, [tbB], [arT])
            if taps and i == taps.get("_tile", 1) and l == taps.get("_layer", 0):
                tap("hgrn_o", b4, b4.t[0:R, :], [R, 512])
                tap("caug", caugB[i], caug.t[0:R, i, :], [R, 130])
                tap("qlT", qlT, qlT.t[:, :, :], [128, 8, 128])
                tap("kiT", kiTB[i], kiT.t[:, i * 128:(i + 1) * 128], [128, 128])
                tap("qiT", qiT, qiT.t[:, :, :], [128, 4, 128])
                tap("craw", craw, craw.t[:, :], [128, NTC])
                tap("gaT", gaT, gaT.t[:, :, :], [128, 4, 128])

            ck("l%dt%d_hgrn" % (l, i))
            yield "A_done"
            if not need_out:
                return
            use_thr = (i >= 1) and ((qb + 1) * 128 > ktop)
            if use_thr:
                nk = (qb + 1) * 128
                for b_ in sidxB:
                    b_.last_w = None
                    b_.readers = {}
                    P.alias(b_, [sidx])
                for kb0 in range(0, qb + 1, 2):
                    kbs = [kb for kb in (kb0, kb0 + 1) if kb <= qb]
                    for kb in kbs:
                        bkI = bank[kb % 2]
                        for h in range(4):
                            mm(bkI.t[:, h * 128:(h + 1) * 128], qiT.t[:, h, :],
                               kiT.t[:, (kb + 1) * 128:(kb + 2) * 128], True, True, [qiT, kiTB[kb + 1]], [bkI])
                    for kb in kbs:
                        bkI = bank[kb % 2]
                        act(itmpP[kb % 2].t[:, :, :], _split(bkI.t[:, :], 4, 128), AF.Relu, [bkI], [itmpP[kb % 2]])
                    for h in range(4):
                        for kb in kbs:
                            it_ = itmpP[kb % 2]
                            sv = sidx.t[:, kb * 128:(kb + 1) * 128]
                            sxb = sidxB[kb % 2]
                            if h == 0:
                                ts(sv, it_.t[:, 0, :], smv("wsgn", 128, 0, 1), ALU.mult, [it_, smb("wsgn")], [sxb])
                            else:
                                stt(sv, it_.t[:, h, :], smv("wsgn", 128, h, h + 1), sv, ALU.mult, ALU.add,
                                    [it_, smb("wsgn"), sxb], [sxb])
                    yield "idx"
                lw = [b_.last_w for b_ in sidxB if b_.last_w is not None]
                assert all(c[0] is lw[0][0] for c in lw)
                sidx.last_w = max(lw, key=lambda c: c[1])
                sidx.readers = {}
                yield "idx_done"
                sa = sidx.t[:, 0:nk]
                op("dve", lambda e, a=sa: e.tensor_reduce(out=smv("mx"), in_=a, axis=AX.X, op=ALU.max), [sidx], [smb("mx")])
                op("dve", lambda e, a=sa: e.tensor_reduce(out=smv("lo"), in_=a, axis=AX.X, op=ALU.min), [sidx], [smb("lo")])
                tt(smv("step"), smv("mx"), smv("lo"), ALU.subtract, [smb("mx"), smb("lo")], [smb("step")])
                dv = sidx.t[:, qb * 128:(qb + 1) * 128]
                tt(dv, dv, causneg.t[:], ALU.add, [sidx, causneg], [sidx])
                cD = int(nk * BIS_DVE_FRAC)
                thr_c = float(ktop) - 0.5 - (nk - cD) / 2.0
                stt(smv("mid"), smv("step"), 0.5, smv("lo"), ALU.mult, ALU.add, [smb("step"), smb("lo")], [smb("mid")])
                op("dve", lambda e, v_=thr_c: e.memset(smv("thrc"), v_), [], [smb("thrc")])
                for k in range(1, nit + 1):
                    f = 2.0 ** (-k)
                    act(_zs(jkA.t[:, 0:1], nk - cD), sidx.t[:, cD:nk], AF.Sign, [sidx, smb("mid")], [jkA, smb("sg")],
                        scale=-1.0, bias=smv("mid"), accum=smv("sg"))
                    ts(_zs(jkD.t[:, 0:1], cD), sidx.t[:, 0:cD], smv("mid"), ALU.is_ge, [sidx, smb("mid")], [jkD, smb("cnt")],
                       s2=0.0, op1=ALU.add, accum=smv("cnt"))
                    ts(smv("thrA"), smv("sg"), 0.5, ALU.mult, [smb("sg")], [smb("thrA")], s2=thr_c, op1=ALU.add)
                    stt(smv("base"), smv("step"), -0.5 * f, smv("mid"), ALU.mult, ALU.add, [smb("step"), smb("mid")],
                        [smb("base")])
                    ts(smv("fl"), smv("cnt"), smv("thrA"), ALU.is_ge, [smb("cnt"), smb("thrA")], [smb("fl")], s2=f, op1=ALU.mult)
                    stt(smv("mid"), smv("fl"), smv("step"), smv("base"), ALU.mult, ALU.add,
                        [smb("fl"), smb("step"), smb("base")], [smb("mid")])
                    yield "bis"
                stt(smv("lo"), smv("step"), -(2.0 ** (-nit - 1)), smv("mid"), ALU.mult, ALU.add, [smb("step"), smb("mid")],
                    [smb("lo")])
                ts(mball.t[:, 0:nk], sa, smv("lo"), ALU.is_lt, [sidx, smb("lo")], [mball])
            else:
                yield "idx_done"
            yield "bis_done"
            ck("l%dt%d_thr" % (l, i))
            if i == 0:
                kblocks = [(0, NMETA, (identb.t[0:NMETA, 0:NMETA], B0, NMETA), False, None)]
            else:
                if qb == 0:
                    kblocks = [(0, NMETA, (shiftm.t[:, :], B1, 128), False, None)]
                else:
                    kblocks = [(0, NMETA, None, False, None)]
                for kb in range(qb + 1):
                    if kb == qb:
                        bias = (identb.t[:, :], B0, 128)
                    elif kb == qb - 1:
                        bias = (identb.t[:, :], B1, 128)
                    else:
                        bias = None
                    kblocks.append((kb + 1, 128, bias, use_thr, kb))
            nblk = len(kblocks)
            def emit_pc(bi_, kt_, KR_, E_):
                for h in range(8):
                    mm(pc_ap(h, R), E_.t[0:KR_, h, 0:R], caug.t[0:KR_, kt_, 0:129], bi_ == 0 and h % 3 == 0, bi_ == nblk - 1,
                       [E_, caugB[kt_]], [bank[5 + h // 3]], sg=True)

            pend = None
            for bi, (kt, KR, bias, masked, kb) in enumerate(kblocks):
                E = Eb[bi % 2]
                if masked:
                    mb = maskb[bi % 2]
                    ts(mb.t[:, :], mball.t[:, kb * 128:(kb + 1) * 128], NEG, ALU.mult, [mball], [mb], eng="pool")
                for half in range(2):
                    bk = bank[3 + half]
                    lgv = _split(bk.t[0:KR, 0:4 * R], 4, R)
                    nacc = 1 + (2 if bias is not None else 0) + (1 if masked else 0)
                    na = [0]

                    def acc(lhsT, rhs, rd):
                        na[0] += 1
                        mm(bk.t[0:KR, 0:4 * R], lhsT, rhs, na[0] == 1, na[0] == nacc, rd, [bk])
                    acc(cT.t[:, kt * 128:kt * 128 + KR], qlT.t[:, 4 * half:4 * half + 4, 0:R], [cTB[kt], qlT])
                    if bias is not None:
                        bl, bt, bk_rows = bias
                        for hl in range(2):
                            acc(bl, bt.t[0:bk_rows, hl, 4 * half:4 * half + 4, 0:R], [bt, identb, shiftm])
                    if masked:
                        acc(mb.t[:, :], _mid_bc(identb.t[:, :], 4), [mb, identb])
                    act(E.t[0:KR, 4 * half:4 * half + 4, 0:R], lgv, AF.Exp, [bk], [E])
                if pend is not None:
                    emit_pc(*pend)
                pend = (bi, kt, KR, E)
                yield "post"
            emit_pc(*pend)
            for g in range(3):
                nh = 3 if g < 2 else 2
                dn = bank[5 + g].t[0:R, 0:nh * 129]
                dnv = bass.AP(dn.tensor, dn.offset + 128, [list(dn.ap[0]), [129, nh]])
                op("dve", lambda e, a=dnv, o_=smv("rden", R, 3 * g, 3 * g + nh): e.reciprocal(out=o_, in_=a),
                   [bank[5 + g]], [smb("rden")])
            for h in range(8):
                if h % 2 == 0:
                    act(olat.t[0:R, h, :], pc_ap(h, R, 128), AF.Copy, [bank[5 + h // 3], smb("rden")], [olat],
                        scale=smv("rden", R, h, h + 1))
                else:
                    ts(olat.t[0:R, h, :], pc_ap(h, R, 128), smv("rden", R, h, h + 1), ALU.mult,
                       [bank[5 + h // 3], smb("rden")], [olat])
            for h in range(8):
                tr(tbv(h * 128, R), olat.t[0:R, h, :], R, [olat], [tbB])
            act(olatT.t[:, :, 0:R], t8[:, :, 0:R], AF.Copy, [tbB], [olatT])
            yield "post"
            for j in range(4):
                mm(b0.t[:, j * 128:j * 128 + R], wuv.t[:, 2 * j, :], olatT.t[:, 2 * j, 0:R], True, False, [wuv, olatT], [b0])
                mm(b0.t[:, j * 128:j * 128 + R], wuv.t[:, 2 * j + 1, :], olatT.t[:, 2 * j + 1, 0:R], False, True,
                   [wuv, olatT], [b0])
            tt(arT.t[:, 0:4, 0:R], _split(b0.t[:, :], 4, 128)[:, :, 0:R], gaT.t[:, :, 0:R], ALU.mult, [b0, gaT], [arT])

            yield "post"
            ck("l%dt%d_attn" % (l, i))
            for half in range(2):
                bk = bank[half]
                for j in range(8):
                    mm(bk.t[0:R, :], arT.t[:, j, 0:R], wout.t[:, j, half * 512:(half + 1) * 512], j == 0, j == 7,
                       [arT, woutB[j]], [bk])
            for half in range(2):
                hv = hres.t[0:R, half * 512:(half + 1) * 512]
                stt(hv, hv, DN_ALPHA, bank[half].t[0:R, :], ALU.mult, ALU.add, [hres, bank[half]], [hres])
            stB = Buf("stats_", stats.t)
            for half in range(2):
                op("dve", lambda e, hf=half: e.bn_stats(out=stats.t[0:R, hf * 6:(hf + 1) * 6],
                                                        in_=hres.t[0:R, hf * 512:(hf + 1) * 512]), [hres], [stB])
            op("dve", lambda e: e.bn_aggr(out=smv("mv", R), in_=stats.t[0:R, :]), [stB], [smb("mv")])
            ts(smv("lnr", R), smv("mv", R, 1, 2), EPS, ALU.add, [smb("mv")], [smb("lnr")])
            act(smv("lnr", R), smv("lnr", R), AF.Ln, [smb("lnr")], [smb("lnr")])
            act(smv("lnr", R), smv("lnr", R), AF.Exp, [smb("lnr")], [smb("lnr")], scale=-0.5)
            stt(hres.t[0:R, :], hres.t[0:R, :], smv("mv", R, 0, 1), lnG.t[0:R, :], ALU.subtract, ALU.mult,
                [hres, smb("mv"), lnG], [hres])
            stt(hres.t[0:R, :], hres.t[0:R, :], smv("lnr", R), lnBt.t[0:R, :], ALU.mult, ALU.add,
                [hres, smb("lnr"), lnBt], [hres])
            if last_layer:
                P.dma("pool", out_d[qb * 128:(qb + 1) * 128, :], hres.t[0:R, :], reads=[hres], is_output=True)
            else:
                P.dma("pool", h1_d[i * 128:i * 128 + R, :], hres.t[0:R, :], reads=[hres], writes=[h1B[i]])
        def step(g):
            try:
                return next(g)
            except StopIteration:
                return None

        def run_to(g, marker):
            while True:
                m = step(g)
                if m is None or m == marker:
                    return m

        gens = [tile_gen(i) for i in range(NTT)]
        run_to(gens[0], "A_done")
        run_to(gens[0], "bis_done")
        if NTT > 1:
            run_to(gens[1], "A_done")
        for i in range(NTT):
            s1 = gens[i]
            s2 = gens[i + 1] if i + 1 < NTT else None
            s3 = gens[i + 2] if i + 2 < NTT else None
            l1, l2, l3 = True, s2 is not None, s3 is not None
            if l3:
                step(s3)
            while l1 or l2 or l3:
                if l2:
                    m = step(s2)
                    if m is None or m == "bis_done":
                        l2 = False
                if l1:
                    for _ in range(SCHED_POST_STEPS):
                        if step(s1) is None:
                            l1 = False
                            break
                elif l3:
                    m = step(s3)
                    if m is None or m == "A_done":
                        l3 = False
    P.finish()
    return P, tap_out


def prep_shared(inp):
    w_in = np.asarray(inp["w_in"], np.float32)
    o = {"q": (0, 512), "c": (512, 640), "qi": (640, 896), "ki": (896, 960), "wi": (960, 964), "ga": (964, 1476),
         "qh": (1476, 1988), "fh": (1988, 2500), "ih": (2500, 3012), "gh": (3012, 3524)}
    order = ["q", "ki", "ki", "ga", "c", "wi", "qi", "qh", "fh", "ih", "gh"]
    w_perm = np.ascontiguousarray(np.concatenate([w_in[:, :, o[k][0]:o[k][1]] for k in order], axis=2))
    assert w_perm.shape[2] == NCOL
    w_uk = np.asarray(inp["w_uk"], np.float32)
    wuk_l = np.zeros((2, 128, 8, 128), np.float32)
    for h in range(8):
        wuk_l[:, (h % 2) * 64:(h % 2) * 64 + 64, h, :] = w_uk[:, h]
    w_uv = np.asarray(inp["w_uv"], np.float32)
    wuv_l = np.zeros((2, 128, 8, 128), np.float32)
    for h in range(8):
        wuv_l[:, :, h, (h % 2) * 64:(h % 2) * 64 + 64] = w_uv[:, h]
    bc = lambda a, n: np.ascontiguousarray(np.broadcast_to(a[:, None, :], (a.shape[0], n, a.shape[1])))
    ghn = np.tile(np.asarray(inp["hgrn_norm_g"], np.float32), (1, 4))
    rb = np.asarray(inp["rel_bias"], np.float32)
    lbraw = np.asarray(inp["hgrn_lb_raw"], np.float32).reshape(1, 1024)
    d = {
        "meta": np.ascontiguousarray(np.asarray(inp["meta_tokens"], np.float32)),
        "w_in": w_perm,
        "w_out": np.ascontiguousarray(np.asarray(inp["w_out"], np.float32)),
        "w_uk": wuk_l.reshape(2, 128, 1024),
        "w_uv": wuv_l.reshape(2, 128, 1024),
        "gkv": bc(np.asarray(inp["kv_norm_g"], np.float32), 128),
        "ghn": bc(ghn, 128),
        "lng": bc(np.asarray(inp["ln_g"], np.float32), 128),
        "lnb": bc(np.asarray(inp["ln_b"], np.float32), 128),
        "lbraw": np.ascontiguousarray(np.broadcast_to(lbraw, (128, 1024))),
        "rb": np.ascontiguousarray(rb),
        "rb31": np.ascontiguousarray(rb[31].reshape(8, 1)),
    }
    d.update(host_constants())
    return d


NT_FULL = 32
NIT = 16
BIS_DVE_FRAC = 0.47
SCHED_POST_STEPS = 1


def kernel(**inputs):
    shared = prep_shared(inputs)
    x = np.asarray(inputs["x"], np.float32)
    B = x.shape[0]
    nc = bass.Bass("TRN2", target_bir_lowering=False)
    build_program(nc, NT_FULL, layers=(0, 1), ktop=256, nit=NIT, final_layer=1)
    in_maps = []
    for b in range(B):
        m = dict(shared)
        m["x"] = np.ascontiguousarray(x[b])
        in_maps.append(m)
    res = run_bass_kernel_spmd(nc, in_maps, core_ids=list(range(B)))
    out = np.stack([np.asarray(r["out"], np.float32) for r in res.results], axis=0)
    return out
```

```python
from contextlib import ExitStack
import numpy as np
import concourse.bass as bass
import concourse.mybir as mybir
from concourse.bass_utils import run_bass_kernel_spmd

F32 = mybir.dt.float32
BF16 = mybir.dt.bfloat16
U8 = mybir.dt.uint8
AF = mybir.ActivationFunctionType
ALU = mybir.AluOpType
AX = mybir.AxisListType

NDS = 16


class _Sem:
    def __init__(self, h, name):
        self.h = h
        self.name = name


class Buf:
    def __init__(self, name, t=None):
        self.name = name
        self.t = t
        self.last_w = None
        self.readers = {}


class _Eng:
    def __init__(self, name, sem):
        self.name = name
        self.sem = sem
        self.n = 0
        self.seen = {}
        self.q = []
        self.ndma = 0
        self.dsems = []


class Prog:
    ENG = ("pe", "act", "dve", "pool", "sp")

    def __init__(self, nc):
        self.nc = nc
        self.es = ExitStack()
        self.eng = {}
        for n in self.ENG:
            s = _Sem(self.es.enter_context(nc.semaphore("s_" + n)), n)
            self.eng[n] = _Eng(n, s)
        for n in ("sp", "pool", "act"):
            self.eng[n].dsems = [_Sem(self.es.enter_context(nc.semaphore("d_%s_%d" % (n, i))), "d%s%d" % (n, i))
                                 for i in range(NDS)]
        self.out_clocks = []
        self.nwait = 0

    def sb(self, name, shape, dt):
        return Buf(name, self.es.enter_context(self.nc.sbuf_tensor("s_" + name, list(shape), dt)))

    def ps(self, name, shape, dt):
        return Buf(name, self.es.enter_context(self.nc.psum_tensor("p_" + name, list(shape), dt)))

    def view(self, name, t):
        return Buf(name, t)

    def _need(self, E, clock, strict_same):
        sem, val = clock
        if sem is E.sem and not strict_same:
            return
        if E.seen.get(sem, 0) >= val:
            return
        E.seen[sem] = val
        E.q.append(("w", sem, val))
        self.nwait += 1

    def _deps(self, E, reads, writes, is_dma=False):
        strict = is_dma or E.name != "pe"
        for b in reads:
            if b.last_w is not None:
                self._need(E, b.last_w, True)
        for b in writes:
            if b.last_w is not None:
                self._need(E, b.last_w, strict)
            for s, v in b.readers.items():
                self._need(E, (s, v), strict)

    def _mark(self, clock, reads, writes):
        sem, val = clock
        for b in writes:
            b.last_w = clock
            b.readers = {}
        for b in reads:
            if b.readers.get(sem, 0) < val:
                b.readers[sem] = val

    def op(self, eng, fn, reads=(), writes=()):
        E = self.eng[eng]
        self._deps(E, reads, writes)
        E.n += 1
        E.q.append(("o", fn))
        self._mark((E.sem, E.n), reads, writes)

    def dma(self, queue, out, in_, reads=(), writes=(), is_output=False):
        E = self.eng[queue]
        j = E.ndma
        E.ndma += 1
        ds = E.dsems[j % NDS]
        prev = 16 * (j // NDS)
        if prev > 0:
            self._need(E, (ds, prev), True)
        self._deps(E, reads, writes, is_dma=True)
        E.q.append(("d", out, in_, ds))
        clock = (ds, prev + 16)
        self._mark(clock, reads, writes)
        if is_output:
            self.out_clocks.append(clock)
        return clock

    def finish(self, final_eng="sp"):
        E = self.eng[final_eng]
        last = {}
        for s, v in self.out_clocks:
            if last.get(s, (None, 0))[1] < v:
                last[s] = (s, v)
        for s, v in last.values():
            E.q.append(("w", s, v))
        nc = self.nc
        engs = self.eng

        def replay(E, e):
            for it in E.q:
                k = it[0]
                if k == "w":
                    e.wait_ge(it[1].h, it[2])
                elif k == "o":
                    it[1](e).then_inc(E.sem.h, 1)
                else:
                    e.dma_start(out=it[1], in_=it[2]).then_inc(it[3].h, 16)

        with nc.Block() as block:
            @block.tensor
            def _(e):
                replay(engs["pe"], e)

            @block.scalar
            def _(e):
                replay(engs["act"], e)

            @block.vector
            def _(e):
                replay(engs["dve"], e)

            @block.gpsimd
            def _(e):
                replay(engs["pool"], e)

            @block.sync
            def _(e):
                replay(engs["sp"], e)
        self.es.close()

    def alias(self, dst, srcs):
        for s in srcs:
            if s.last_w is not None:
                c = s.last_w
                if dst.readers.get(c[0], 0) < c[1]:
                    dst.readers[c[0]] = c[1]
            for k, v in s.readers.items():
                if dst.readers.get(k, 0) < v:
                    dst.readers[k] = v


D = 1024
NMETA = 16
FQ, FK, FG, TC, TQ, TF, TI, TG = 0, 512, 640, 1152, 1540, 2052, 2564, 3076
NTC = 388
NCOL = 3588
DN_ALPHA = 4.0 ** 0.25
EPS = 1e-6
NEG = -30000.0


def _mid_bc(ap, n):
    a = [list(x) for x in ap.ap]
    return bass.AP(ap.tensor, ap.offset, [a[0], [0, n]] + a[1:])


def host_constants():
    c = {}
    idx = np.arange(128)
    c["identf"] = np.eye(128, dtype=np.float32)
    same = (idx[:, None] // 64) == (idx[None, :] // 64)
    c["m1t"] = (same & (idx[:, None] <= idx[None, :])).astype(np.float32)
    c["m3t"] = (same & (idx[:, None] > idx[None, :])).astype(np.float32)
    sel = np.zeros((128, 2), np.float32)
    sel[:64, 0] = 1.0
    sel[64:, 1] = 1.0
    c["sel"] = sel
    c["causneg"] = np.where(idx[None, :] <= idx[:, None], 0.0, -1e30).astype(np.float32)
    c["cmask"] = (same & (idx[:, None] <= idx[None, :])).astype(np.float32)
    c["j128"] = np.fliplr(np.eye(128, dtype=np.float32)).copy()
    sh = np.zeros((128, 16), np.float32)
    sh[112 + np.arange(16), np.arange(16)] = 1.0
    c["shiftm"] = sh
    d = np.arange(384) - 127
    n = np.maximum(d, 0)
    nf = np.maximum(n, 1).astype(np.float32)
    large = 16 + (np.log(nf / np.float32(16)) / np.float32(np.log(128 / 16)) * np.float32(16)).astype(np.int32)
    large = np.minimum(large, 31)
    bucket = np.where(n < 16, n, large)
    oh = np.zeros((32, 384), np.float32)
    for j in range(384):
        if d[j] >= 0:
            oh[bucket[j], j] = 1.0
    c["ohd"] = oh
    c["negrow"] = np.broadcast_to(np.where(d >= 0, 0.0, NEG).astype(np.float32)[None, :], (8, 384)).copy()
    return c


CONST_SHAPES = {"identf": [128, 128], "m1t": [128, 128], "m3t": [128, 128], "sel": [128, 2],
                "causneg": [128, 128], "cmask": [128, 128], "j128": [128, 128], "shiftm": [128, 16],
                "ohd": [32, 384], "negrow": [8, 384]}


def _resplit(ap, a, b):
    return bass.AP(ap.tensor, ap.offset, [list(ap.ap[0]), [b, a], [1, b]])


def _zs(ap, n):
    return bass.AP(ap.tensor, ap.offset, [list(ap.ap[0]), [0, n]])


def _split(ap, a, b):
    p = list(ap.ap[0])
    st = ap.ap[-1][0]
    return bass.AP(ap.tensor, ap.offset, [p, [b * st, a], [st, b]])


class _StopBuild(Exception):
    pass


def build_program(nc, NT, layers=(0, 1), ktop=256, nit=24, taps=None, final_layer=1, stop_at=None):
    try:
        return _build_program(nc, NT, layers, ktop, nit, taps, final_layer, stop_at)
    except _StopBuild as e:
        P = e.args[0]
        P.finish()
        return P, {}


def _build_program(nc, NT, layers, ktop, nit, taps, final_layer, stop_at):
    NTT = NT + 1
    S = NT * 128
    P = Prog(nc)
    tap_out = {}

    def ck(name):
        if stop_at == name:
            raise _StopBuild(P)

    def dr(name, shape, kind="ExternalInput"):
        return nc.dram_tensor(name, list(shape), F32, kind=kind).ap()

    x_d = dr("x", [S, D])
    meta_d = dr("meta", [NMETA, D])
    out_d = dr("out", [S, D], "ExternalOutput")
    win_d = dr("w_in", [2, D, NCOL])
    wout_d = dr("w_out", [2, D, D])
    wuk_d = dr("w_uk", [2, 128, 1024])
    wuv_d = dr("w_uv", [2, 128, 1024])
    gkv_d = dr("gkv", [2, 128, 128])
    ghn_d = dr("ghn", [2, 128, 512])
    lng_d = dr("lng", [2, 128, D])
    lnb_d = dr("lnb", [2, 128, D])
    lbraw_d = dr("lbraw", [128, 1024])
    rb_d = dr("rb", [32, 8])
    rb31_d = dr("rb31", [8, 1])
    const_d = {k: dr(k, v) for k, v in CONST_SHAPES.items()}
    h1_d = dr("h1s", [NTT * 128, D], "Internal")
    vd_d = dr("vds", [8, 384], "Internal")
    h1B = [Buf("h1_%d" % i) for i in range(NTT)]
    vdB = Buf("vd")

    def tap(name, buf, ap, shape):
        if taps is None or name not in taps:
            return
        d = dr("tap_" + name, shape, "ExternalOutput")
        tap_out[name] = shape
        P.dma("pool", d, ap, reads=[buf], is_output=True)

    sb = P.sb
    win = sb("win", [128, 8, NCOL], BF16)
    winB = [Buf("win%d" % k, win.t) for k in range(8)]
    wout = sb("wout", [128, 8, D], BF16)
    woutB = [Buf("wout%d" % k, wout.t) for k in range(8)]
    wuk = sb("wuk", [128, 8, 128], BF16)
    wuv = sb("wuv", [128, 8, 128], BF16)
    caug = sb("caug", [128, NTT, 130], BF16)
    caugB = [Buf("caug%d" % i, caug.t) for i in range(NTT)]
    cT = sb("cT", [128, NTT * 128], BF16)
    cTB = [Buf("cT%d" % i, cT.t) for i in range(NTT)]
    kiT = sb("kiT", [128, NTT * 128], BF16)
    kiTB = [Buf("kiT%d" % i, kiT.t) for i in range(NTT)]
    B0 = sb("B0", [128, 2, 8, 128], BF16)
    B1 = sb("B1", [128, 2, 8, 128], BF16)
    shiftm = sb("shiftm", [128, 16], BF16)
    gkvB = sb("gkvB", [128, 128], F32)
    ghnB = sb("ghnB", [128, 512], F32)
    lnG = sb("lnG", [128, D], F32)
    lnBt = sb("lnB", [128, D], F32)
    lbB = sb("lbB", [128, 512], F32)
    omlB = sb("omlB", [128, 512], F32)
    identb = sb("identb", [128, 128], BF16)
    hresP = [sb("hres%d" % k, [128, D], F32) for k in range(2)]
    hres = hresP[0]
    hb = sb("hb", [128, D], BF16)
    hT = sb("hT", [128, 8, 128], BF16)
    qT = sb("qT", [128, 4, 128], BF16)
    qlTP = [sb("qlT%d" % k, [128, 8, 128], BF16) for k in range(2)]
    qlT = qlTP[0]
    qis = sb("qis", [128, 256], BF16)
    qiT = sb("qiT", [128, 4, 128], BF16)
    gaTP = [sb("gaT%d" % k, [128, 4, 128], BF16) for k in range(2)]
    gaT = gaTP[0]
    craw = sb("craw", [128, NTC], F32)
    sm = sb("sm", [128, 64], F32)
    smB = {}

    def small(name, c0, n):
        smB[name] = (Buf("sm_" + name, sm.t), c0, n)

    small("wabs", 0, 4); small("wsgn", 4, 4); small("cs", 8, 1); small("crs", 9, 1)
    small("lo", 10, 1); small("step", 11, 1); small("mid", 12, 1); small("cnt", 13, 1); small("fl", 14, 1)
    small("mx", 15, 1); small("rden", 16, 8); small("ssq4", 24, 4); small("rs4", 28, 4)
    small("sg", 36, 1); small("thrA", 37, 1); small("thrc", 38, 1); small("base", 39, 1); small("mv", 32, 2); small("lnr", 34, 1); small("lnm", 35, 1); small("dS", 40, 8)

    def smv(name, R=128, a=None, b=None):
        bf, c0, n = smB[name]
        a = 0 if a is None else a
        b = n if b is None else b
        return sm.t[0:R, c0 + a:c0 + b]

    def smb(name):
        return smB[name][0]

    stats = sb("stats", [128, 12], F32)
    sidx = sb("sidx", [128, 4096], F32)
    stgB = [Buf("stg0", sidx.t), Buf("stg1", sidx.t)]
    itmp = sb("itmp", [128, 4, 128], F32)
    itmpP = [itmp, sb("itmp2", [128, 4, 128], F32)]
    sidxB = [Buf("sidxA"), Buf("sidxB")]
    mball = sb("mball", [128, 4096], U8)
    jkD = sb("jkD", [128, 8], U8)
    jkA = sb("jkA", [128, 8], mybir.dt.int8)
    _iv = itmp.t[0:8, 0:3, :]
    vs = Buf("vs", None)
    vs_ap = bass.AP(_iv.tensor, _iv.offset, [list(_iv.ap[0]), [1, 384]])
    maskb = [sb("maskb%d" % i, [128, 128], BF16) for i in range(2)]
    E2 = sb("E2", [128, 2, 8, 128], BF16)
    Eb = [Buf("E%d" % k, E2.t[:, k]) for k in range(2)]
    _e0 = E2.t[:, 0, 0, :]
    olat = sb("olat", [128, 8, 128], BF16)
    olatT = sb("olatT", [128, 8, 128], BF16)
    arTP = [sb("arT%d" % k, [128, 8, 128], BF16) for k in range(2)]
    arT = arTP[0]
    TT = sb("TT", [128, 5, 512], F32)
    Tb = [Buf("T%d" % k, TT.t) for k in range(5)]

    def Tv(k, R=128):
        return TT.t[0:R, k, :]
    Vb = sb("Vb", [128, 512], BF16)
    QD = sb("QD", [128, 512], BF16)
    KD = sb("KD", [128, 512], BF16)
    KL = sb("KL", [128, 512], BF16)
    KLB = sb("KLB", [128, 512], BF16)
    QDTA = sb("QDTA", [128, 4, 128], BF16)
    QDTB = sb("QDTB", [128, 4, 128], BF16)
    KDT = sb("KDT", [128, 4, 128], BF16)
    SC = sb("SC", [128, 4, 128], BF16)
    Rb = QD
    Sf = sb("Sf", [128, 4, 128], F32)
    SbE = sb("SbE", [128, 4, 128], BF16)
    SbM = sb("SbM", [128, 4, 128], BF16)
    rbs = sb("rbs", [32, 8], F32)
    rb31 = sb("rb31", [8, 1], F32)

    _e2f = bass.AP(_e0.tensor, _e0.offset, [list(_e0.ap[0]), [1, 2048]]).bitcast(F32)
    cviews = {"ohd": _e2f[0:32, 0:384], "negrow": _e2f[0:8, 384:768], "j128": _e2f[:, 768:896],
              "shiftm": _e2f[:, 896:912], "identf": itmp.t[:, 3, :]}
    cst = {}
    for k, shp in CONST_SHAPES.items():
        if k in cviews:
            cst[k] = Buf("c_" + k, cviews[k])
        else:
            cst[k] = sb("c_" + k, shp, F32)
        P.dma("sp", cst[k].t[:], const_d[k], writes=[cst[k]])
    identf, m1t, m3t, sel, causneg, cmask, j128 = (cst[k] for k in
                                                   ("identf", "m1t", "m3t", "sel", "causneg", "cmask", "j128"))
    bank = [P.ps("bk%d" % i, [128, 512], F32) if i != 2 else P.ps("bk2", [128, 1024], BF16) for i in range(8)]
    tbB = bank[2]
    tb = bank[2].t[:, :]

    def tbv(c0, n, R=128):
        return tb[0:R, c0:c0 + n]

    op = P.op

    def act(out, in_, func, reads, writes, scale=None, bias=None, accum=None, eng="act"):
        kw = {}
        if scale is not None:
            kw["scale"] = scale
        if bias is not None:
            kw["bias"] = bias
        if accum is not None:
            kw["accum_out"] = accum
        op(eng, lambda e: e.activation(out=out, in_=in_, func=func, **kw), reads, writes)

    def ts(out, in0, s1, op0, reads, writes, s2=None, op1=None, accum=None, eng="dve"):
        kw = {}
        if op1 is not None:
            kw["op1"] = op1
        if accum is not None:
            kw["accum_out"] = accum
        op(eng, lambda e: e.tensor_scalar(out=out, in0=in0, scalar1=s1, scalar2=s2, op0=op0, **kw), reads, writes)

    def tt(out, in0, in1, o, reads, writes, eng="dve"):
        op(eng, lambda e: e.tensor_tensor(out=out, in0=in0, in1=in1, op=o), reads, writes)

    def stt(out, in0, scalar, in1, op0, op1, reads, writes):
        op("dve", lambda e: e.scalar_tensor_tensor(out=out, in0=in0, scalar=scalar, in1=in1, op0=op0, op1=op1),
           reads, writes)

    def cp(out, in_, reads, writes, eng="dve"):
        op(eng, lambda e: e.tensor_copy(out=out, in_=in_), reads, writes)

    def mm(out, lhsT, rhs, start, stop, reads, writes, sg=False):
        op("pe", lambda e: e.matmul(out, lhsT=lhsT, rhs=rhs, start=start, stop=stop, skip_group_check=sg), reads, writes)

    def tr(out, in_, R, reads, writes):
        op("pe", lambda e: e.transpose(out=out, in_=in_, identity=identb.t[0:R, 0:R]), list(reads) + [identb], writes)

    def sigm(dst, src, src_bufs, dbuf):
        act(dst, src, AF.Exp, src_bufs, [dbuf], scale=-1.0)
        act(dst, dst, AF.Ln, [dbuf], [dbuf], bias=1.0)
        act(dst, dst, AF.Exp, [dbuf], [dbuf], scale=-1.0)

    def rsqrt_small(dst, src, R, mul, add):
        dv = smv(dst[0], R, dst[1], dst[2])
        sv = smv(src[0], R, src[1], src[2])
        ts(dv, sv, mul, ALU.mult, [smb(src[0])], [smb(dst[0])], s2=add, op1=ALU.add)
        act(dv, dv, AF.Ln, [smb(dst[0])], [smb(dst[0])])
        act(dv, dv, AF.Exp, [smb(dst[0])], [smb(dst[0])], scale=-0.5)

    cp(identb.t[:], identf.t[:], [identf], [identb])
    op("dve", lambda e: e.memset(caug.t[:, :, 128:130], 1.0), [], caugB)
    op("dve", lambda e: e.memset(QDTA.t[:], 0.0), [], [QDTA])
    op("dve", lambda e: e.memset(QDTB.t[:], 0.0), [], [QDTB])
    op("dve", lambda e: e.memset(qiT.t[:], 0.0), [], [qiT])
    op("dve", lambda e: e.memset(KLB.t[:], 0.0), [], [KLB])
    P.dma("sp", rbs.t[:], rb_d, writes=[rbs])
    P.dma("sp", rb31.t[:], rb31_d, writes=[rb31])
    ohd = cst["ohd"]
    mm(bank[0].t[0:8, 0:384], rbs.t[:], ohd.t[:], True, True, [rbs, ohd], [bank[0]])
    ts(vs_ap, bank[0].t[0:8, 0:384], rb31.t[:, 0:1], ALU.subtract, [bank[0], rb31], [vs])
    tt(vs_ap, vs_ap, cst["negrow"].t[:], ALU.add, [vs, cst["negrow"]], [vs])
    P.dma("sp", vd_d, vs_ap, reads=[vs], writes=[vdB])
    P.alias(itmp, [vs])
    P.dma("sp", _split(hres.t[:, :], 8, 128), bass.AP(vd_d.tensor, 0, [[1, 128], [384, 8], [1, 128]]),
          reads=[vdB], writes=[hres])
    P.dma("sp", _resplit(TT.t[:, 0:2, :], 8, 128),
          bass.AP(vd_d.tensor, 128, [[1, 128], [384, 8], [1, 128]]), reads=[vdB], writes=[Tb[0], Tb[1]])
    for half in range(2):
        mm(bank[half].t[:, :], j128.t[:], hres.t[:, half * 512:(half + 1) * 512], True, True, [j128, hres], [bank[half]])
        act(B0.t[:, 0, half * 4:(half + 1) * 4, :], _split(bank[half].t[:, :], 4, 128), AF.Copy, [bank[half]], [B0])
        tt(B0.t[:, 1, half * 4:(half + 1) * 4, :], _split(bank[half].t[:, :], 4, 128), B0.t[:, 0, half * 4:(half + 1) * 4, :],
           ALU.subtract, [bank[half], B0], [B0])
    for half in range(2):
        mm(bank[half].t[:, :], j128.t[:], TT.t[:, half, :], True, True, [j128, Tb[half]], [bank[half]])
        act(B1.t[:, 0, half * 4:(half + 1) * 4, :], _split(bank[half].t[:, :], 4, 128), AF.Copy, [bank[half]], [B1])
        tt(B1.t[:, 1, half * 4:(half + 1) * 4, :], _split(bank[half].t[:, :], 4, 128), B1.t[:, 0, half * 4:(half + 1) * 4, :],
           ALU.subtract, [bank[half], B1], [B1])
    cp(shiftm.t[:], cst["shiftm"].t[:], [cst["shiftm"]], [shiftm])
    P.dma("sp", TT.t[:, 2:4, :], _split(lbraw_d, 2, 512), writes=[Tb[2], Tb[3]])
    tt(Tv(4), Tv(2), Tv(3), ALU.subtract, [Tb[2], Tb[3]], [Tb[4]])
    act(Tv(4), Tv(4), AF.Exp, [Tb[4]], [Tb[4]])
    ts(Tv(4), Tv(4), 1.0, ALU.add, [Tb[4]], [Tb[4]])
    op("dve", lambda e: e.reciprocal(out=lbB.t[:], in_=Tv(4)), [Tb[4]], [lbB])
    ts(omlB.t[:], lbB.t[:], -1.0, ALU.mult, [lbB], [omlB], s2=1.0, op1=ALU.add)

    for b_ in Eb:
        P.alias(b_, [cst["ohd"], cst["negrow"], cst["j128"], cst["shiftm"]])
    P.alias(itmp, [identf, vs])
    ck("prologue")
    SCALE_Q = 0.125
    SCALE_I = 1.0 / 16.0
    SCALE_H = 128.0 ** -0.5

    castn = [0]

    def cast(out, in_, reads, writes):
        e = ("pool", "dve", "act")[castn[0] % 3]
        castn[0] += 1
        if e == "act":
            act(out, in_, AF.Copy, reads, writes)
        else:
            cp(out, in_, reads, writes, eng=e)

    def pc_ap(h, R, n=129):
        return bank[5 + h // 3].t[0:R, (h % 3) * 129:(h % 3) * 129 + n]

    for l in layers:
        for b_ in stgB:
            P.alias(b_, [sidx])
        HW = NCOL // 2
        n = 0
        for kc in range(8):
            for half in range(2):
                st = stgB[n % 2]
                so = (n % 2) * 2048
                n += 1
                P.dma("sp", sidx.t[:, so:so + HW], win_d[l, kc * 128:(kc + 1) * 128, half * HW:(half + 1) * HW], writes=[st])
                cast(win.t[:, kc, half * HW:(half + 1) * HW], sidx.t[:, so:so + HW], [st], [winB[kc]])
        for j in range(8):
            st = stgB[n % 2]
            so = (n % 2) * 2048
            n += 1
            P.dma("sp", sidx.t[:, so:so + D], wout_d[l, j * 128:(j + 1) * 128, :], writes=[st])
            cast(wout.t[:, j, :], sidx.t[:, so:so + D], [st], [woutB[j]])
        st = stgB[n % 2]; so = (n % 2) * 2048; n += 1
        P.dma("sp", sidx.t[:, so:so + 1024], wuk_d[l], writes=[st])
        cast(wuk.t[:, :, :], _split(sidx.t[:, so:so + 1024], 8, 128), [st], [wuk])
        st = stgB[n % 2]; so = (n % 2) * 2048; n += 1
        P.dma("sp", sidx.t[:, so:so + 1024], wuv_d[l], writes=[st])
        cast(wuv.t[:, :, :], _split(sidx.t[:, so:so + 1024], 8, 128), [st], [wuv])
        P.alias(sidx, stgB)
        P.dma("sp", gkvB.t[:], gkv_d[l], writes=[gkvB])
        P.dma("sp", ghnB.t[:], ghn_d[l], writes=[ghnB])
        P.dma("sp", lnG.t[:], lng_d[l], writes=[lnG])
        P.dma("sp", lnBt.t[:], lnb_d[l], writes=[lnBt])
        op("dve", lambda e: e.memset(Sf.t[:], 0.0), [], [Sf])
        op("dve", lambda e: e.memset(SbE.t[:], 0.0), [], [SbE])
        last_layer = (l == final_layer)
        ck("weights%d" % l)

        def tile_gen(i, l=l, last_layer=last_layer):
            R = NMETA if i == 0 else 128
            RA = min(R, 64)
            qb = i - 1
            need_out = not (last_layer and i == 0)
            hres, qlT, gaT, arT = hresP[i % 2], qlTP[i % 2], gaTP[i % 2], arTP[i % 2]
            if l == 0:
                src, srcB = (meta_d if i == 0 else x_d[qb * 128:(qb + 1) * 128, :]), []
            else:
                src, srcB = h1_d[i * 128:i * 128 + R, :], [h1B[i]]
            P.dma("pool", hb.t[0:R, :], src, reads=srcB, writes=[hb])
            yield "A0"
            for kc in range(8):
                tr(tbv(kc * 128, R), hb.t[0:R, kc * 128:(kc + 1) * 128], R, [hb], [tbB])
            act(hT.t[:, :, 0:R], _split(tb[:, :], 8, 128)[:, :, 0:R], AF.Copy, [tbB], [hT])

            ck("l%dt%d_load" % (l, i))
            def fm_group(bk, col0, nchunk):
                for j in range(nchunk):
                    for kc in range(8):
                        mm(bk.t[:, j * 128:j * 128 + R], win.t[:, kc, col0 + j * 128:col0 + (j + 1) * 128],
                           hT.t[:, kc, 0:R], kc == 0, kc == 7, [winB[kc], hT], [bk])

            def tm_group(bk, col0, ncol):
                for kc in range(8):
                    mm(bk.t[0:R, 0:ncol], hT.t[:, kc, 0:R], win.t[:, kc, col0:col0 + ncol], kc == 0, kc == 7,
                       [winB[kc], hT], [bk])

            yield "A"
            b0, b1 = bank[0], bank[1]
            bq = bank[0]
            fm_group(bq, FQ, 4)
            act(qT.t[:, :, 0:R], _split(bq.t[:, :], 4, 128)[:, :, 0:R], AF.Copy, [bq], [qT], scale=SCALE_Q)
            yield "A"
            bq = bank[1]
            fm_group(bq, FK, 1)
            act(kiT.t[:, i * 128:i * 128 + R], bq.t[:, 0:R], AF.Copy, [bq], [kiTB[i]])
            yield "A"
            bq = bank[3]
            fm_group(bq, FG, 4)
            g4v = _split(bq.t[:, :], 4, 128)[:, :, 0:R]
            t4v = _split(TT.t[:, 4, :], 4, 128)[:, :, 0:R]
            sigm(t4v, g4v, [bq], Tb[4])
            tt(gaT.t[:, :, 0:R], g4v, t4v, ALU.mult, [bq, Tb[4]], [gaT])
            yield "A"
            bq = bank[4]
            tm_group(bq, TC, NTC)
            act(craw.t[0:R, :], bq.t[0:R, 0:NTC], AF.Copy, [bq], [craw])
            yield "A"
            bq = bank[5]
            tm_group(bq, TQ, 512)
            sigm(Tv(0, R), bq.t[0:R, :], [bq], Tb[0])
            stt(Tv(0, R), bq.t[0:R, :], SCALE_H, Tv(0, R), ALU.mult, ALU.mult, [bq, Tb[0]], [Tb[0]])
            yield "A"
            b1 = bank[6]
            tm_group(b1, TF, 512)
            act(Tv(1, R), b1.t[0:R, :], AF.Exp, [b1], [Tb[1]], scale=-1.0)
            act(Tv(2, R), Tv(1, R), AF.Ln, [Tb[1]], [Tb[2]], bias=1.0)
            act(Tv(1, R), Tv(2, R), AF.Exp, [Tb[2]], [Tb[1]], scale=-1.0)
            gs = -1.0
            if l > 0:
                gs = 1.0
                tt(Tv(1, R), Tv(1, R), omlB.t[0:R, :], ALU.mult, [Tb[1], omlB], [Tb[1]])
                tt(Tv(1, R), Tv(1, R), lbB.t[0:R, :], ALU.add, [Tb[1], lbB], [Tb[1]])
                act(Tv(2, R), Tv(1, R), AF.Ln, [Tb[1]], [Tb[2]])
            ts(Tv(1, R), Tv(1, R), -1.0, ALU.mult, [Tb[1]], [Tb[1]], s2=1.0, op1=ALU.add)
            yield "A"
            bq = bank[7]
            tm_group(bq, TI, 512)
            act(Vb.t[0:R, :], bq.t[0:R, :], AF.Copy, [bq], [Vb])
            yield "A"
            b1 = bank[0]
            tm_group(b1, TG, 512)
            sigm(Tv(3, R), b1.t[0:R, :], [b1], Tb[3])
            tt(Tv(3, R), b1.t[0:R, :], Tv(3, R), ALU.mult, [b1, Tb[3]], [Tb[3]])
            tt(Tv(3, R), Tv(3, R), ghnB.t[0:R, :], ALU.mult, [Tb[3], ghnB], [Tb[3]])

            b0, b1 = bank[0], bank[1]
            P.dma("sp", hres.t[0:R, :], src, reads=srcB, writes=[hres])
            yield "A"
            yield "A"
            ck("l%dt%d_prep" % (l, i))
            mm(b0.t[0:R, :], m1t.t[0:R, 0:R], Tv(2, R), True, True, [m1t, Tb[2]], [b0])
            mm(b1.t[0:R, :], m3t.t[0:R, 0:R], Tv(2, R), True, True, [m3t, Tb[2]], [b1])
            for h in range(4):
                mm(bank[7].t[:, 2 * h:2 * h + 2], TT.t[0:R, 2, h * 128:(h + 1) * 128], sel.t[0:R, :], True, True,
                   [Tb[2], sel], [bank[7]])
            act(smv("dS"), bank[7].t[:, 0:8], AF.Exp, [bank[7]], [smb("dS")], scale=gs)
            act(Tv(4, R), b0.t[0:R, :], AF.Exp, [b0], [Tb[4]], scale=gs)
            tt(QD.t[0:R, :], Tv(0, R), Tv(4, R), ALU.mult, [Tb[0], Tb[4]], [QD])
            act(Tv(4, R), b0.t[0:R, :], AF.Exp, [b0], [Tb[4]], scale=-gs)
            tt(KD.t[0:R, :], Tv(1, R), Tv(4, R), ALU.mult, [Tb[1], Tb[4]], [KD])
            act(Tv(4, R), b1.t[0:R, :], AF.Exp, [b1], [Tb[4]], scale=gs)
            tt(KL.t[0:RA, :], Tv(1, RA), Tv(4, RA), ALU.mult, [Tb[1], Tb[4]], [KL])
            if R == 128:
                tt(KLB.t[64:128, :], TT.t[64:128, 1, :], TT.t[64:128, 4, :], ALU.mult, [Tb[1], Tb[4]], [KLB])
            yield "A"
            for h in range(4):
                tr(tbv(h * 128, R), QD.t[0:R, h * 128:(h + 1) * 128], R, [QD], [tbB])
            for h in range(4):
                tr(tbv(512 + h * 128, R), KD.t[0:R, h * 128:(h + 1) * 128], R, [KD], [tbB])
            t8 = _split(tb[:, :], 8, 128)
            act(QDTA.t[:, :, 0:RA], t8[:, 0:4, 0:RA], AF.Copy, [tbB], [QDTA])
            if R == 128:
                cp(QDTB.t[:, :, 64:128], t8[:, 0:4, 64:128], [tbB], [QDTB])
            act(KDT.t[:, :, 0:R], t8[:, 4:8, 0:R], AF.Copy, [tbB], [KDT])
            yield "A"
            b3, b4 = bank[3], bank[4]
            for h in range(4):
                mm(b3.t[0:R, h * 128:h * 128 + RA], KDT.t[:, h, 0:R], QDTA.t[:, h, 0:RA], True, True, [KDT, QDTA], [b3])
                if R == 128:
                    mm(b3.t[0:R, h * 128 + 64:h * 128 + 128], KDT.t[:, h, 0:R], QDTB.t[:, h, 64:128], True, True,
                       [KDT, QDTB], [b3])
            tt(SC.t[0:R, :, 0:R], _split(b3.t[0:R, :], 4, 128)[:, :, 0:R], _mid_bc(cmask.t[0:R, 0:R], 4), ALU.mult,
               [b3, cmask], [SC])
            for h in range(4):
                hc = slice(h * 128, (h + 1) * 128)
                mm(bank[5].t[:, hc], KL.t[0:RA, hc], Vb.t[0:RA, hc], True, True, [KL, Vb], [bank[5]])
                if R == 128:
                    mm(bank[6].t[:, hc], KLB.t[:, hc], Vb.t[:, hc], True, True, [KLB, Vb], [bank[6]])
            yield "A"
            ck("l%dt%d_inproj" % (l, i))
            act(Tv(4, R)[:, 0:128], craw.t[0:R, 0:128], AF.Square, [craw], [Tb[4], smb("cs")], accum=smv("cs", R))
            rsqrt_small(("crs", 0, 1), ("cs", 0, 1), R, 1.0 / 128.0, EPS)
            stt(caug.t[0:R, i, 0:128], craw.t[0:R, 0:128], smv("crs", R), gkvB.t[0:R, :], ALU.mult, ALU.mult,
                [craw, smb("crs"), gkvB], [caugB[i]])
            tr(tbv(0, R), caug.t[0:R, i, 0:128], R, [caugB[i]], [tbB])
            ck("l%dt%d_c0" % (l, i))
            act(cT.t[:, i * 128:i * 128 + R], tbv(0, R), AF.Copy, [tbB], [cTB[i]])
            ck("l%dt%d_c" % (l, i))
            if i >= 1:
                wv = craw.t[0:R, 128:132]
                ts(smv("wsgn", R), wv, 0.0, ALU.is_ge, [craw], [smb("wsgn")], s2=2.0, op1=ALU.mult)
                ts(smv("wsgn", R), smv("wsgn", R), -1.0, ALU.add, [smb("wsgn")], [smb("wsgn")])
                tt(smv("wabs", R), wv, smv("wsgn", R), ALU.mult, [craw, smb("wsgn")], [smb("wabs")])
                ts(smv("wabs", R), smv("wabs", R), SCALE_I, ALU.mult, [smb("wabs")], [smb("wabs")])
                wb = smv("wabs", R)
                wbc = bass.AP(wb.tensor, wb.offset, [list(wb.ap[0]), [1, 4], [0, 64]])
                tt(_split(qis.t[0:R, :], 4, 64), _split(craw.t[0:R, 132:388], 4, 64), wbc, ALU.mult,
                   [craw, smb("wabs")], [qis])
                for j in range(2):
                    tr(tbv(j * 128, R), qis.t[0:R, j * 128:(j + 1) * 128], R, [qis], [tbB])
                for h in range(4):
                    pb = (h % 2) * 64
                    cp(qiT.t[pb:pb + 64, h, 0:R], tb[pb:pb + 64, (h // 2) * 128:(h // 2) * 128 + R], [tbB], [qiT])
            for h in range(8):
                bk = bank[3 + h // 4]
                mm(bk.t[:, (h % 4) * 128:(h % 4) * 128 + R], wuk.t[:, h, :], qT.t[:, h // 2, 0:R],
                   True, True, [wuk, qT], [bk])
            ck("l%dt%d_ql" % (l, i))
            act(qlT.t[:, 0:4, 0:R], _split(bank[3].t[:, :], 4, 128)[:, :, 0:R], AF.Copy, [bank[3]], [qlT])
            act(qlT.t[:, 4:8, 0:R], _split(bank[4].t[:, :], 4, 128)[:, :, 0:R], AF.Copy, [bank[4]], [qlT])

            yield "A"
            for h in range(4):
                hc = slice(h * 128, (h + 1) * 128)
                mm(b4.t[0:R, hc], SC.t[0:R, h, 0:R], Vb.t[0:R, hc], h == 0, False, [SC, Vb], [b4], sg=True)
                mm(b4.t[0:R, hc], QDTA.t[:, h, 0:R], SbE.t[:, h, :], False, R < 128, [QDTA, SbE], [b4], sg=True)
            for h in range(4):
                hc = slice(h * 128, (h + 1) * 128)
                stt(Sf.t[:, h, :], Sf.t[:, h, :], smv("dS", 128, 2 * h, 2 * h + 1), bank[5].t[:, hc], ALU.mult, ALU.add,
                    [Sf, smb("dS"), bank[5]], [Sf])
            if R == 128:
                cp(SbM.t[:], Sf.t[:], [Sf], [SbM])
                for h in range(4):
                    hc = slice(h * 128, (h + 1) * 128)
                    mm(b4.t[0:R, hc], QDTB.t[:, h, :], SbM.t[:, h, :], False, True, [QDTB, SbM], [b4], sg=True)
                for h in range(4):
                    hc = slice(h * 128, (h + 1) * 128)
                    stt(Sf.t[:, h, :], Sf.t[:, h, :], smv("dS", 128, 2 * h + 1, 2 * h + 2), bank[6].t[:, hc], ALU.mult,
                        ALU.add, [Sf, smb("dS"), bank[6]], [Sf])
            cp(SbE.t[:], Sf.t[:], [Sf], [SbE])
            yield "A"
            if need_out:
                for h in range(4):
                    hc = slice(h * 128, (h + 1) * 128)
                    act(Tv(4, R)[:, hc], b4.t[0:R, hc], AF.Square, [b4], [Tb[4], smb("ssq4")],
                        accum=smv("ssq4", R, h, h + 1))
                rsqrt_small(("rs4", 0, 4), ("ssq4", 0, 4), R, 1.0 / 128.0, EPS)
                for h in range(4):
                    hc = slice(h * 128, (h + 1) * 128)
                    stt(Rb.t[0:R, hc], b4.t[0:R, hc], smv("rs4", R, h, h + 1), Tv(3, R)[:, hc], ALU.mult, ALU.mult,
                        [b4, smb("rs4"), Tb[3]], [Rb])
                for h in range(4):
                    tr(tbv(h * 128, R), Rb.t[0:R, h * 128:(h + 1) * 128], R, [Rb], [tbB])
                act(arT.t[:, 4:8, 0:R], t8[:, 0:4, 0:R], AF.Copy, [tbB], [arT])
            if taps and i == taps.get("_tile", 1) and l == taps.get("_layer", 0):
                tap("hgrn_o", b4, b4.t[0:R, :], [R, 512])
                tap("caug", caugB[i], caug.t[0:R, i, :], [R, 130])
                tap("qlT", qlT, qlT.t[:, :, :], [128, 8, 128])
                tap("kiT", kiTB[i], kiT.t[:, i * 128:(i + 1) * 128], [128, 128])
                tap("qiT", qiT, qiT.t[:, :, :], [128, 4, 128])
                tap("craw", craw, craw.t[:, :], [128, NTC])
                tap("gaT", gaT, gaT.t[:, :, :], [128, 4, 128])

            ck("l%dt%d_hgrn" % (l, i))
            yield "A_done"
            if not need_out:
                return
            use_thr = (i >= 1) and ((qb + 1) * 128 > ktop)
            if use_thr:
                nk = (qb + 1) * 128
                for b_ in sidxB:
                    b_.last_w = None
                    b_.readers = {}
                    P.alias(b_, [sidx])
                for kb0 in range(0, qb + 1, 2):
                    kbs = [kb for kb in (kb0, kb0 + 1) if kb <= qb]
                    for kb in kbs:
                        bkI = bank[kb % 2]
                        for h in range(4):
                            mm(bkI.t[:, h * 128:(h + 1) * 128], qiT.t[:, h, :],
                               kiT.t[:, (kb + 1) * 128:(kb + 2) * 128], True, True, [qiT, kiTB[kb + 1]], [bkI])
                    for kb in kbs:
                        bkI = bank[kb % 2]
                        act(itmpP[kb % 2].t[:, :, :], _split(bkI.t[:, :], 4, 128), AF.Relu, [bkI], [itmpP[kb % 2]])
                    for h in range(4):
                        for kb in kbs:
                            it_ = itmpP[kb % 2]
                            sv = sidx.t[:, kb * 128:(kb + 1) * 128]
                            sxb = sidxB[kb % 2]
                            if h == 0:
                                ts(sv, it_.t[:, 0, :], smv("wsgn", 128, 0, 1), ALU.mult, [it_, smb("wsgn")], [sxb])
                            else:
                                stt(sv, it_.t[:, h, :], smv("wsgn", 128, h, h + 1), sv, ALU.mult, ALU.add,
                                    [it_, smb("wsgn"), sxb], [sxb])
                    yield "idx"
                lw = [b_.last_w for b_ in sidxB if b_.last_w is not None]
                assert all(c[0] is lw[0][0] for c in lw)
                sidx.last_w = max(lw, key=lambda c: c[1])
                sidx.readers = {}
                yield "idx_done"
                sa = sidx.t[:, 0:nk]
                op("dve", lambda e, a=sa: e.tensor_reduce(out=smv("mx"), in_=a, axis=AX.X, op=ALU.max), [sidx], [smb("mx")])
                op("dve", lambda e, a=sa: e.tensor_reduce(out=smv("lo"), in_=a, axis=AX.X, op=ALU.min), [sidx], [smb("lo")])
                tt(smv("step"), smv("mx"), smv("lo"), ALU.subtract, [smb("mx"), smb("lo")], [smb("step")])
                dv = sidx.t[:, qb * 128:(qb + 1) * 128]
                tt(dv, dv, causneg.t[:], ALU.add, [sidx, causneg], [sidx])
                cD = int(nk * BIS_DVE_FRAC)
                thr_c = float(ktop) - 0.5 - (nk - cD) / 2.0
                stt(smv("mid"), smv("step"), 0.5, smv("lo"), ALU.mult, ALU.add, [smb("step"), smb("lo")], [smb("mid")])
                op("dve", lambda e, v_=thr_c: e.memset(smv("thrc"), v_), [], [smb("thrc")])
                for k in range(1, nit + 1):
                    f = 2.0 ** (-k)
                    act(_zs(jkA.t[:, 0:1], nk - cD), sidx.t[:, cD:nk], AF.Sign, [sidx, smb("mid")], [jkA, smb("sg")],
                        scale=-1.0, bias=smv("mid"), accum=smv("sg"))
                    ts(_zs(jkD.t[:, 0:1], cD), sidx.t[:, 0:cD], smv("mid"), ALU.is_ge, [sidx, smb("mid")], [jkD, smb("cnt")],
                       s2=0.0, op1=ALU.add, accum=smv("cnt"))
                    ts(smv("thrA"), smv("sg"), 0.5, ALU.mult, [smb("sg")], [smb("thrA")], s2=thr_c, op1=ALU.add)
                    stt(smv("base"), smv("step"), -0.5 * f, smv("mid"), ALU.mult, ALU.add, [smb("step"), smb("mid")],
                        [smb("base")])
                    ts(smv("fl"), smv("cnt"), smv("thrA"), ALU.is_ge, [smb("cnt"), smb("thrA")], [smb("fl")], s2=f, op1=ALU.mult)
                    stt(smv("mid"), smv("fl"), smv("step"), smv("base"), ALU.mult, ALU.add,
                        [smb("fl"), smb("step"), smb("base")], [smb("mid")])
                    yield "bis"
                stt(smv("lo"), smv("step"), -(2.0 ** (-nit - 1)), smv("mid"), ALU.mult, ALU.add, [smb("step"), smb("mid")],
                    [smb("lo")])
                ts(mball.t[:, 0:nk], sa, smv("lo"), ALU.is_lt, [sidx, smb("lo")], [mball])
            else:
                yield "idx_done"
            yield "bis_done"
            ck("l%dt%d_thr" % (l, i))
            if i == 0:
                kblocks = [(0, NMETA, (identb.t[0:NMETA, 0:NMETA], B0, NMETA), False, None)]
            else:
                if qb == 0:
                    kblocks = [(0, NMETA, (shiftm.t[:, :], B1, 128), False, None)]
                else:
                    kblocks = [(0, NMETA, None, False, None)]
                for kb in range(qb + 1):
                    if kb == qb:
                        bias = (identb.t[:, :], B0, 128)
                    elif kb == qb - 1:
                        bias = (identb.t[:, :], B1, 128)
                    else:
                        bias = None
                    kblocks.append((kb + 1, 128, bias, use_thr, kb))
            nblk = len(kblocks)
            def emit_pc(bi_, kt_, KR_, E_):
                for h in range(8):
                    mm(pc_ap(h, R), E_.t[0:KR_, h, 0:R], caug.t[0:KR_, kt_, 0:129], bi_ == 0 and h % 3 == 0, bi_ == nblk - 1,
                       [E_, caugB[kt_]], [bank[5 + h // 3]], sg=True)

            pend = None
            for bi, (kt, KR, bias, masked, kb) in enumerate(kblocks):
                E = Eb[bi % 2]
                if masked:
                    mb = maskb[bi % 2]
                    ts(mb.t[:, :], mball.t[:, kb * 128:(kb + 1) * 128], NEG, ALU.mult, [mball], [mb], eng="pool")
                for half in range(2):
                    bk = bank[3 + half]
                    lgv = _split(bk.t[0:KR, 0:4 * R], 4, R)
                    nacc = 1 + (2 if bias is not None else 0) + (1 if masked else 0)
                    na = [0]

                    def acc(lhsT, rhs, rd):
                        na[0] += 1
                        mm(bk.t[0:KR, 0:4 * R], lhsT, rhs, na[0] == 1, na[0] == nacc, rd, [bk])
                    acc(cT.t[:, kt * 128:kt * 128 + KR], qlT.t[:, 4 * half:4 * half + 4, 0:R], [cTB[kt], qlT])
                    if bias is not None:
                        bl, bt, bk_rows = bias
                        for hl in range(2):
                            acc(bl, bt.t[0:bk_rows, hl, 4 * half:4 * half + 4, 0:R], [bt, identb, shiftm])
                    if masked:
                        acc(mb.t[:, :], _mid_bc(identb.t[:, :], 4), [mb, identb])
                    act(E.t[0:KR, 4 * half:4 * half + 4, 0:R], lgv, AF.Exp, [bk], [E])
                if pend is not None:
                    emit_pc(*pend)
                pend = (bi, kt, KR, E)
                yield "post"
            emit_pc(*pend)
            for g in range(3):
                nh = 3 if g < 2 else 2
                dn = bank[5 + g].t[0:R, 0:nh * 129]
                dnv = bass.AP(dn.tensor, dn.offset + 128, [list(dn.ap[0]), [129, nh]])
                op("dve", lambda e, a=dnv, o_=smv("rden", R, 3 * g, 3 * g + nh): e.reciprocal(out=o_, in_=a),
                   [bank[5 + g]], [smb("rden")])
            for h in range(8):
                if h % 2 == 0:
                    act(olat.t[0:R, h, :], pc_ap(h, R, 128), AF.Copy, [bank[5 + h // 3], smb("rden")], [olat],
                        scale=smv("rden", R, h, h + 1))
                else:
                    ts(olat.t[0:R, h, :], pc_ap(h, R, 128), smv("rden", R, h, h + 1), ALU.mult,
                       [bank[5 + h // 3], smb("rden")], [olat])
            for h in range(8):
                tr(tbv(h * 128, R), olat.t[0:R, h, :], R, [olat], [tbB])
            act(olatT.t[:, :, 0:R], t8[:, :, 0:R], AF.Copy, [tbB], [olatT])
            yield "post"
            for j in range(4):
                mm(b0.t[:, j * 128:j * 128 + R], wuv.t[:, 2 * j, :], olatT.t[:, 2 * j, 0:R], True, False, [wuv, olatT], [b0])
                mm(b0.t[:, j * 128:j * 128 + R], wuv.t[:, 2 * j + 1, :], olatT.t[:, 2 * j + 1, 0:R], False, True,
                   [wuv, olatT], [b0])
            tt(arT.t[:, 0:4, 0:R], _split(b0.t[:, :], 4, 128)[:, :, 0:R], gaT.t[:, :, 0:R], ALU.mult, [b0, gaT], [arT])

            yield "post"
            ck("l%dt%d_attn" % (l, i))
            for half in range(2):
                bk = bank[half]
                for j in range(8):
                    mm(bk.t[0:R, :], arT.t[:, j, 0:R], wout.t[:, j, half * 512:(half + 1) * 512], j == 0, j == 7,
                       [arT, woutB[j]], [bk])
            for half in range(2):
                hv = hres.t[0:R, half * 512:(half + 1) * 512]
                stt(hv, hv, DN_ALPHA, bank[half].t[0:R, :], ALU.mult, ALU.add, [hres, bank[half]], [hres])
            stB = Buf("stats_", stats.t)
            for half in range(2):
                op("dve", lambda e, hf=half: e.bn_stats(out=stats.t[0:R, hf * 6:(hf + 1) * 6],
                                                        in_=hres.t[0:R, hf * 512:(hf + 1) * 512]), [hres], [stB])
            op("dve", lambda e: e.bn_aggr(out=smv("mv", R), in_=stats.t[0:R, :]), [stB], [smb("mv")])
            ts(smv("lnr", R), smv("mv", R, 1, 2), EPS, ALU.add, [smb("mv")], [smb("lnr")])
            act(smv("lnr", R), smv("lnr", R), AF.Ln, [smb("lnr")], [smb("lnr")])
            act(smv("lnr", R), smv("lnr", R), AF.Exp, [smb("lnr")], [smb("lnr")], scale=-0.5)
            stt(hres.t[0:R, :], hres.t[0:R, :], smv("mv", R, 0, 1), lnG.t[0:R, :], ALU.subtract, ALU.mult,
                [hres, smb("mv"), lnG], [hres])
            stt(hres.t[0:R, :], hres.t[0:R, :], smv("lnr", R), lnBt.t[0:R, :], ALU.mult, ALU.add,
                [hres, smb("lnr"), lnBt], [hres])
            if last_layer:
                P.dma("pool", out_d[qb * 128:(qb + 1) * 128, :], hres.t[0:R, :], reads=[hres], is_output=True)
            else:
                P.dma("pool", h1_d[i * 128:i * 128 + R, :], hres.t[0:R, :], reads=[hres], writes=[h1B[i]])
        def step(g):
            try:
                return next(g)
            except StopIteration:
                return None

        def run_to(g, marker):
            while True:
                m = step(g)
                if m is None or m == marker:
                    return m

        gens = [tile_gen(i) for i in range(NTT)]
        run_to(gens[0], "A_done")
        run_to(gens[0], "bis_done")
        if NTT > 1:
            run_to(gens[1], "A_done")
        for i in range(NTT):
            s1 = gens[i]
            s2 = gens[i + 1] if i + 1 < NTT else None
            s3 = gens[i + 2] if i + 2 < NTT else None
            l1, l2, l3 = True, s2 is not None, s3 is not None
            if l3:
                step(s3)
            while l1 or l2 or l3:
                if l2:
                    m = step(s2)
                    if m is None or m == "bis_done":
                        l2 = False
                if l1:
                    for _ in range(SCHED_POST_STEPS):
                        if step(s1) is None:
                            l1 = False
                            break
                elif l3:
                    m = step(s3)
                    if m is None or m == "A_done":
                        l3 = False
    P.finish()
    return P, tap_out


def prep_shared(inp):
    w_in = np.asarray(inp["w_in"], np.float32)
    o = {"q": (0, 512), "c": (512, 640), "qi": (640, 896), "ki": (896, 960), "wi": (960, 964), "ga": (964, 1476),
         "qh": (1476, 1988), "fh": (1988, 2500), "ih": (2500, 3012), "gh": (3012, 3524)}
    order = ["q", "ki", "ki", "ga", "c", "wi", "qi", "qh", "fh", "ih", "gh"]
    w_perm = np.ascontiguousarray(np.concatenate([w_in[:, :, o[k][0]:o[k][1]] for k in order], axis=2))
    assert w_perm.shape[2] == NCOL
    w_uk = np.asarray(inp["w_uk"], np.float32)
    wuk_l = np.zeros((2, 128, 8, 128), np.float32)
    for h in range(8):
        wuk_l[:, (h % 2) * 64:(h % 2) * 64 + 64, h, :] = w_uk[:, h]
    w_uv = np.asarray(inp["w_uv"], np.float32)
    wuv_l = np.zeros((2, 128, 8, 128), np.float32)
    for h in range(8):
        wuv_l[:, :, h, (h % 2) * 64:(h % 2) * 64 + 64] = w_uv[:, h]
    bc = lambda a, n: np.ascontiguousarray(np.broadcast_to(a[:, None, :], (a.shape[0], n, a.shape[1])))
    ghn = np.tile(np.asarray(inp["hgrn_norm_g"], np.float32), (1, 4))
    rb = np.asarray(inp["rel_bias"], np.float32)
    lbraw = np.asarray(inp["hgrn_lb_raw"], np.float32).reshape(1, 1024)
    d = {
        "meta": np.ascontiguousarray(np.asarray(inp["meta_tokens"], np.float32)),
        "w_in": w_perm,
        "w_out": np.ascontiguousarray(np.asarray(inp["w_out"], np.float32)),
        "w_uk": wuk_l.reshape(2, 128, 1024),
        "w_uv": wuv_l.reshape(2, 128, 1024),
        "gkv": bc(np.asarray(inp["kv_norm_g"], np.float32), 128),
        "ghn": bc(ghn, 128),
        "lng": bc(np.asarray(inp["ln_g"], np.float32), 128),
        "lnb": bc(np.asarray(inp["ln_b"], np.float32), 128),
        "lbraw": np.ascontiguousarray(np.broadcast_to(lbraw, (128, 1024))),
        "rb": np.ascontiguousarray(rb),
        "rb31": np.ascontiguousarray(rb[31].reshape(8, 1)),
    }
    d.update(host_constants())
    return d


NT_FULL = 32
NIT = 16
BIS_DVE_FRAC = 0.47
SCHED_POST_STEPS = 1


def kernel(**inputs):
    shared = prep_shared(inputs)
    x = np.asarray(inputs["x"], np.float32)
    B = x.shape[0]
    nc = bass.Bass("TRN2", target_bir_lowering=False)
    build_program(nc, NT_FULL, layers=(0, 1), ktop=256, nit=NIT, final_layer=1)
    in_maps = []
    for b in range(B):
        m = dict(shared)
        m["x"] = np.ascontiguousarray(x[b])
        in_maps.append(m)
    res = run_bass_kernel_spmd(nc, in_maps, core_ids=list(range(B)))
    out = np.stack([np.asarray(r["out"], np.float32) for r in res.results], axis=0)
    return out
```
